# Optimizing a Trainium2 kernel written in Bass

```python
import math
import jax, jax.numpy as jnp
from jax import lax
import numpy as np

D_MODEL = 2048
BATCH = 1
SEQ = 8192
DEPTH = 4

CHUNK = 64
QBLOCK = 128
ATT_WIDTH = D_MODEL // 2
DIFF_HEAD_DIM = 64
DIFF_V_DIM = 2 * DIFF_HEAD_DIM
DIFF_HEADS = ATT_WIDTH // DIFF_V_DIM
SSM_WIDTH = D_MODEL // 2
SSM_HEAD_DIM = 64
SSM_HEADS = SSM_WIDTH // SSM_HEAD_DIM
SSM_GROUPS = 2
SSM_STATE = 128
CONV_WIDTH = 4
XBC_WIDTH = SSM_WIDTH + 2 * SSM_GROUPS * SSM_STATE
EVEN_IN = 4 * ATT_WIDTH + SSM_WIDTH + XBC_WIDTH + SSM_HEADS
POOL_WINDOWS = (2, 4, 8, 16)
POOL_WIDTH = D_MODEL
POOL_NGROUPS = len(POOL_WINDOWS)
POOL_GROUP = POOL_WIDTH // POOL_NGROUPS
N_EVEN = (DEPTH + 1) // 2
N_ODD = DEPTH // 2
DEEPNORM_ALPHA = (2.0 * DEPTH) ** 0.25
DEEPNORM_BETA = (8.0 * DEPTH) ** -0.25
LN_EPS = 1e-5
RMS_EPS = 1e-5

kernel_name = "hybrid_diffattn_ssd_pool_deepnorm"

F32 = jnp.float32


def layer_norm(x, g, b):
    xf = x.astype(F32)
    mu = jnp.mean(xf, axis=-1, keepdims=True)
    var = jnp.mean(jnp.square(xf - mu), axis=-1, keepdims=True)
    return ((xf - mu) * lax.rsqrt(var + LN_EPS) * g.astype(F32) + b.astype(F32)).astype(x.dtype)


def rms_norm(x, g=None):
    xf = x.astype(F32)
    y = xf * lax.rsqrt(jnp.mean(jnp.square(xf), axis=-1, keepdims=True) + RMS_EPS)
    if g is not None:
        y = y * g.astype(F32)
    return y


def segsum(a):
    t = a.shape[-1]
    ar = jnp.broadcast_to(a[..., :, None], a.shape + (t,))
    ar = jnp.where(jnp.tril(jnp.ones((t, t), bool), -1), ar, 0.0)
    cs = jnp.cumsum(ar, axis=-2)
    return jnp.where(jnp.tril(jnp.ones((t, t), bool), 0), cs, -jnp.inf)


def diff_attention(q, k, v, lam):
    s = q.shape[1]
    scale = DIFF_HEAD_DIM ** -0.5
    qf, kf, vf = q.astype(F32), k.astype(F32), v.astype(F32)
    outs = []
    for i in range(s // QBLOCK):
        qs, qe = i * QBLOCK, (i + 1) * QBLOCK
        sc = jnp.einsum('bqhmd,bkhmd->bhmqk', qf[:, qs:qe], kf[:, :qe]) * scale
        q_chunk = (qs + jnp.arange(QBLOCK)) // CHUNK
        k_chunk = jnp.arange(qe) // CHUNK
        sc = jnp.where(k_chunk[None, :] <= q_chunk[:, None], sc, -jnp.inf)
        p = jax.nn.softmax(sc, axis=-1)
        attn = p[:, :, 0] - lam * p[:, :, 1]
        outs.append(jnp.einsum('bhqk,bkhe->bqhe', attn, vf[:, :qe]))
    return jnp.concatenate(outs, axis=1)


def causal_depthwise_conv(u, w, bias):
    c = u.shape[-1]
    taps = w.astype(u.dtype)[:, None, :]
    y = lax.conv_general_dilated(u, taps, window_strides=(1,), padding=[(CONV_WIDTH - 1, 0)],
                                 dimension_numbers=('NWC', 'WIO', 'NWC'), feature_group_count=c)
    return y + bias.astype(u.dtype)


def ssd_chunked(xdt, a_dt, b_h, c_h):
    bsz, s, h, p = xdt.shape
    n = b_h.shape[-1]
    nc = s // CHUNK
    x = xdt.reshape(bsz, nc, CHUNK, h, p)
    a = a_dt.reshape(bsz, nc, CHUNK, h).transpose(0, 3, 1, 2)
    bc = b_h.reshape(bsz, nc, CHUNK, h, n)
    cc = c_h.reshape(bsz, nc, CHUNK, h, n)
    a_cs = jnp.cumsum(a, axis=-1)
    decay = jnp.exp(segsum(a))
    scores = jnp.einsum('bclhn,bcshn->bhcls', cc, bc) * decay
    y_diag = jnp.einsum('bhcls,bcshp->bclhp', scores, x)
    decay_states = jnp.exp(a_cs[..., -1:] - a_cs)
    states = jnp.einsum('bclhn,bhcl,bclhp->bchpn', bc, decay_states, x)
    chunk_decay = jnp.exp(a_cs[..., -1])

    def step(carry, inp):
        st, dec = inp
        return carry * dec[..., None, None] + st, carry

    _, prev = lax.scan(step, jnp.zeros((bsz, h, p, n), F32),
                       (states.transpose(1, 0, 2, 3, 4), chunk_decay.transpose(2, 0, 1)))
    prev = prev.transpose(1, 0, 2, 3, 4)
    y_off = jnp.einsum('bclhn,bchpn,bhcl->bclhp', cc, prev, jnp.exp(a_cs))
    return (y_diag + y_off).reshape(bsz, s, h, p)


def even_mixer(x, w_in, conv_w, conv_b, dt_bias, a_log, d_skip, ssm_norm_g,
               lq1, lk1, lq2, lk2, subln_g, w_out, lam_init):
    bsz, s, _ = x.shape
    hcat = x @ w_in
    o1 = 4 * ATT_WIDTH + SSM_WIDTH
    q, k, v, g_att, z, xbc, dt_raw = jnp.split(
        hcat, [ATT_WIDTH, 2 * ATT_WIDTH, 3 * ATT_WIDTH, 4 * ATT_WIDTH, o1, o1 + XBC_WIDTH], axis=-1)
    lam = (jnp.exp(jnp.sum(lq1.astype(F32) * lk1.astype(F32)))
           - jnp.exp(jnp.sum(lq2.astype(F32) * lk2.astype(F32))) + lam_init)
    q = q.reshape(bsz, s, DIFF_HEADS, 2, DIFF_HEAD_DIM)
    k = k.reshape(bsz, s, DIFF_HEADS, 2, DIFF_HEAD_DIM)
    v = v.reshape(bsz, s, DIFF_HEADS, DIFF_V_DIM)
    o = rms_norm(diff_attention(q, k, v, lam), subln_g) * (1.0 - lam_init)
    y_att = o.reshape(bsz, s, ATT_WIDTH).astype(x.dtype) * jax.nn.silu(g_att)
    xbc = jax.nn.silu(causal_depthwise_conv(xbc, conv_w, conv_b))
    xs, bm, cm = jnp.split(xbc, [SSM_WIDTH, SSM_WIDTH + SSM_GROUPS * SSM_STATE], axis=-1)
    rep = SSM_HEADS // SSM_GROUPS
    xs = xs.reshape(bsz, s, SSM_HEADS, SSM_HEAD_DIM).astype(F32)
    bm = jnp.repeat(bm.reshape(bsz, s, SSM_GROUPS, SSM_STATE), rep, axis=2).astype(F32)
    cm = jnp.repeat(cm.reshape(bsz, s, SSM_GROUPS, SSM_STATE), rep, axis=2).astype(F32)
    dt = jax.nn.softplus(dt_raw.astype(F32) + dt_bias.astype(F32))
    a = -jnp.exp(a_log.astype(F32))
    y = ssd_chunked(xs * dt[..., None], a * dt, bm, cm) + d_skip.astype(F32)[:, None] * xs
    y = y.reshape(bsz, s, SSM_WIDTH) * jax.nn.silu(z.astype(F32))
    y = rms_norm(y.reshape(bsz, s, SSM_GROUPS, SSM_WIDTH // SSM_GROUPS))
    y_ssm = (y.reshape(bsz, s, SSM_WIDTH) * ssm_norm_g.astype(F32)).astype(x.dtype)
    return jnp.concatenate([y_att, y_ssm], axis=-1) @ w_out


def pool_mixer(x, w_in, w_grp, b_grp, scale, w_out):
    bsz, s, _ = x.shape
    v, g = jnp.split(x @ w_in, [POOL_WIDTH], axis=-1)
    vf = v.astype(F32).reshape(bsz, s, POOL_NGROUPS, POOL_GROUP)
    cs = jnp.pad(jnp.cumsum(vf, axis=1), ((0, 0), (1, 0), (0, 0), (0, 0)))
    pos = jnp.arange(1, s + 1, dtype=F32)
    pooled = []
    for gi, w in enumerate(POOL_WINDOWS):
        c = cs[:, :, gi]
        lag = jnp.pad(c, ((0, 0), (w, 0), (0, 0)))[:, 1:s + 1]
        mean = (c[:, 1:] - lag) / jnp.minimum(pos, float(w))[None, :, None]
        pooled.append(mean - vf[:, :, gi])
    pooled = jnp.stack(pooled, axis=2).astype(x.dtype)
    m = jnp.einsum('bsgc,gcd->bsgd', pooled, w_grp) + b_grp
    y = m.reshape(bsz, s, POOL_WIDTH) * scale * jax.nn.silu(g)
    return y @ w_out


def setup_inputs(seed: int = 0) -> dict:
    key = jax.random.key(seed)
    ks = jax.random.split(key, 24)
    nrm = jax.random.normal
    dt0 = jnp.exp(jax.random.uniform(ks[4], (N_EVEN, SSM_HEADS), F32) * (math.log(0.1) - math.log(0.001))
                  + math.log(0.001))
    mix_w = ATT_WIDTH + SSM_WIDTH
    return {
        "x": nrm(ks[0], (BATCH, SEQ, D_MODEL), F32),
        "ev_w_in": nrm(ks[1], (N_EVEN, D_MODEL, EVEN_IN), F32) * D_MODEL ** -0.5,
        "ev_conv_w": nrm(ks[2], (N_EVEN, CONV_WIDTH, XBC_WIDTH), F32) * CONV_WIDTH ** -0.5,
        "ev_conv_b": 0.01 * nrm(ks[3], (N_EVEN, XBC_WIDTH), F32),
        "ev_dt_bias": dt0 + jnp.log(-jnp.expm1(-dt0)),
        "ev_a_log": jnp.log(jax.random.uniform(ks[5], (N_EVEN, SSM_HEADS), F32, 1.0, 16.0)),
        "ev_d_skip": 1.0 + 0.1 * nrm(ks[6], (N_EVEN, SSM_HEADS), F32),
        "ev_ssm_norm_g": 1.0 + 0.02 * nrm(ks[7], (N_EVEN, SSM_WIDTH), F32),
        "ev_lambda_q1": 0.1 * nrm(ks[8], (N_EVEN, DIFF_HEAD_DIM), F32),
        "ev_lambda_k1": 0.1 * nrm(ks[9], (N_EVEN, DIFF_HEAD_DIM), F32),
        "ev_lambda_q2": 0.1 * nrm(ks[10], (N_EVEN, DIFF_HEAD_DIM), F32),
        "ev_lambda_k2": 0.1 * nrm(ks[11], (N_EVEN, DIFF_HEAD_DIM), F32),
        "ev_subln_g": 1.0 + 0.02 * nrm(ks[12], (N_EVEN, DIFF_V_DIM), F32),
        "ev_w_out": nrm(ks[13], (N_EVEN, mix_w, D_MODEL), F32) * mix_w ** -0.5 * DEEPNORM_BETA,
        "od_w_in": nrm(ks[14], (N_ODD, D_MODEL, 2 * POOL_WIDTH), F32) * D_MODEL ** -0.5,
        "od_w_grp": nrm(ks[15], (N_ODD, POOL_NGROUPS, POOL_GROUP, POOL_GROUP), F32) * POOL_GROUP ** -0.5,
        "od_b_grp": 0.01 * nrm(ks[16], (N_ODD, POOL_NGROUPS, POOL_GROUP), F32),
        "od_scale": 1.0 + 0.02 * nrm(ks[17], (N_ODD, POOL_WIDTH), F32),
        "od_w_out": nrm(ks[18], (N_ODD, POOL_WIDTH, D_MODEL), F32) * POOL_WIDTH ** -0.5 * DEEPNORM_BETA,
        "ln_g": 1.0 + 0.02 * nrm(ks[19], (DEPTH, D_MODEL), F32),
        "ln_b": 0.02 * nrm(ks[20], (DEPTH, D_MODEL), F32),
    }


def reference(x, ev_w_in, ev_conv_w, ev_conv_b, ev_dt_bias, ev_a_log, ev_d_skip, ev_ssm_norm_g,
              ev_lambda_q1, ev_lambda_k1, ev_lambda_q2, ev_lambda_k2, ev_subln_g, ev_w_out,
              od_w_in, od_w_grp, od_b_grp, od_scale, od_w_out, ln_g, ln_b):
    for l in range(DEPTH):
        i = l // 2
        if l % 2 == 0:
            lam_init = 0.8 - 0.6 * math.exp(-0.3 * l)
            y = even_mixer(x, ev_w_in[i], ev_conv_w[i], ev_conv_b[i], ev_dt_bias[i], ev_a_log[i],
                           ev_d_skip[i], ev_ssm_norm_g[i], ev_lambda_q1[i], ev_lambda_k1[i],
                           ev_lambda_q2[i], ev_lambda_k2[i], ev_subln_g[i], ev_w_out[i], lam_init)
        else:
            y = pool_mixer(x, od_w_in[i], od_w_grp[i], od_b_grp[i], od_scale[i], od_w_out[i])
        x = layer_norm(DEEPNORM_ALPHA * x + y, ln_g[l], ln_b[l])
    return x
```

```python
import math
import numpy as np
import ml_dtypes
import concourse.bass as bass
import concourse.mybir as mybir
from concourse.bass_utils import run_bass_kernel_spmd

F32 = mybir.dt.float32
BF16 = mybir.dt.bfloat16
AF = mybir.ActivationFunctionType
ALU = mybir.AluOpType

NCORES = 8
D = 2048
S = 8192
TPC = S // NCORES
HALO = 16
DEPTH = 4
ALPHA = (2.0 * DEPTH) ** 0.25
EPS = 1e-5
NPJ = 1026


class Prog:
    ENG = ("pe", "act", "dve", "pool", "sp")

    def __init__(self, nc):
        self.nc = nc
        self.q = {e: [] for e in self.ENG}
        self.sem = {e: nc.alloc_semaphore("sem_" + e) for e in ("pe", "act", "dve", "pool")}
        self.cnt = {e: 0 for e in self.sem}
        self.last_w = {}
        self.reads = {}
        self.waited = {e: {} for e in self.ENG}
        self.dsem = {}
        self.out_tokens = []

    def _deps(self, eng, reads, writes):
        toks = []
        for k in list(reads) + list(writes):
            t = self.last_w.get(k)
            if t is not None:
                toks.append(t)
        for k in writes:
            toks.extend(self.reads.get(k, ()))
        waits = {}
        for (sem, val, src) in toks:
            if src == "pe" and eng == "pe":
                continue
            sid = id(sem)
            if self.waited[eng].get(sid, (None, 0))[1] >= val:
                continue
            if sid not in waits or waits[sid][1] < val:
                waits[sid] = (sem, val)
        for sid, (sem, val) in waits.items():
            self.waited[eng][sid] = (sem, val)
        return list(waits.values())

    def _commit(self, tok, reads, writes):
        for k in reads:
            self.reads.setdefault(k, []).append(tok)
        for k in writes:
            self.last_w[k] = tok
            self.reads[k] = []

    def op(self, eng, fn, reads=(), writes=()):
        waits = self._deps(eng, reads, writes)
        self.cnt[eng] += 1
        tok = (self.sem[eng], self.cnt[eng], eng)
        self.q[eng].append((waits, fn, (self.sem[eng], 1)))
        self._commit(tok, reads, writes)

    def dma(self, queue, fn, semname, reads=(), writes=(), is_output=False):
        if semname not in self.dsem:
            self.dsem[semname] = [self.nc.alloc_semaphore("d_" + semname), 0]
        ent = self.dsem[semname]
        waits = self._deps(queue, reads, writes)
        ent[1] += 16
        tok = (ent[0], ent[1], "dma")
        self.q[queue].append((waits, fn, (ent[0], 16)))
        self._commit(tok, reads, writes)
        if is_output:
            self.out_tokens.append(tok)

    def emit(self):
        nc = self.nc
        fin = {}
        for (sem, val, _) in self.out_tokens:
            if id(sem) not in fin or fin[id(sem)][1] < val:
                fin[id(sem)] = (sem, val)
        q = self.q

        def run(e, name, final=False):
            for waits, fn, (sem, amt) in q[name]:
                for (ws, wv) in waits:
                    e.wait_ge(ws, wv)
                fn(e).then_inc(sem, amt)
            if final:
                for (ws, wv) in fin.values():
                    e.wait_ge(ws, wv)

        with nc.Block() as block:
            @block.sync
            def _(e):
                run(e, "sp", final=True)

            @block.tensor
            def _(e):
                run(e, "pe")

            @block.scalar
            def _(e):
                run(e, "act")

            @block.vector
            def _(e):
                run(e, "dve")

            @block.gpsimd
            def _(e):
                run(e, "pool")


def PK(pt):
    return pt.name


def _din(nc, name, shape, dt=F32):
    return nc.dram_tensor(name, list(shape), dt, kind="ExternalInput").ap()


def _dout(nc, name, shape, dt=F32):
    return nc.dram_tensor(name, list(shape), dt, kind="ExternalOutput").ap()


def build_tail(front):
    nc = bass.Bass("TRN2", target_bir_lowering=False)
    P = Prog(nc)
    NT = TPC // 128
    xres = _din(nc, "xres", [TPC, D])
    w_out = _din(nc, "w_out", [D, D])
    lng = _din(nc, "lng", [D])
    lnb = _din(nc, "lnb", [D])
    xo = _dout(nc, "xo", [TPC, D])
    xoT = _dout(nc, "xoT", [D, TPC], BF16)
    identf_d = _din(nc, "identf", [128, 128])
    if front:
        xT = _din(nc, "xT", [D, HALO + TPC], BF16)
        w_in = _din(nc, "w_in", [D, 2 * D])
        w_grp = _din(nc, "w_grp", [4, 512, 512])
        bgrp = _din(nc, "bgrp", [128, 16])
        oscale = _din(nc, "oscale", [128, 16])
        invtab = _din(nc, "invtab", [4 * 16])
    else:
        yT_d = _din(nc, "yT", [D, TPC], BF16)
        ssq8 = _din(nc, "ssq8", [8, TPC])
        sel_d = _din(nc, "sel", [8, 2 * 128])

    NTOK = HALO + TPC
    XT_E = 16 * NTOK
    WB_E = 16 * 512
    big = nc.alloc_sbuf_tensor("big", [128, max(XT_E + 2 * WB_E, 16 * D)], BF16)
    wout_sb = big[:, 0:16 * D].rearrange("p (k n) -> p k n", k=16)
    yT = nc.alloc_sbuf_tensor("yT_sb", [128, 16, TPC], BF16)
    lng_sb = nc.alloc_sbuf_tensor("lng_sb", [128, D], F32)
    lnb_sb = nc.alloc_sbuf_tensor("lnb_sb", [128, D], F32)
    xr_sb = nc.alloc_sbuf_tensor("xr_sb", [128, D], F32)
    z_sb = nc.alloc_sbuf_tensor("z_sb", [128, D], F32)
    zT_sb = nc.alloc_sbuf_tensor("zT_sb", [128, 4, 512], BF16)
    xoT_v = xoT.rearrange("(k p) n -> p k n", p=128)
    stats = nc.alloc_sbuf_tensor("stats", [128, 4, 6], F32)
    mv = nc.alloc_sbuf_tensor("mv", [128, 2], F32)
    rstd = nc.alloc_sbuf_tensor("rstd", [128, 1], F32)
    nmr = nc.alloc_sbuf_tensor("nmr", [128, 1], F32)
    identf = nc.alloc_sbuf_tensor("identf_sb", [128, 128], F32)
    ps = [nc.alloc_psum_tensor("ps%d" % i, [128, 512], F32) for i in range(8)]

    P.dma("sp", lambda e: e.dma_start(out=lng_sb[:], in_=lng.partition_broadcast(128)), "c1", writes=["lng"])
    P.dma("sp", lambda e: e.dma_start(out=lnb_sb[:], in_=lnb.partition_broadcast(128)), "c2", writes=["lnb"])
    P.dma("sp", lambda e: e.dma_start(out=identf[:], in_=identf_d), "c3", writes=["identf"])

    if front:
        xT_sb = big[:, 0:XT_E].rearrange("p (k n) -> p k n", k=16)
        wbuf = [big[:, XT_E + i * WB_E: XT_E + (i + 1) * WB_E].rearrange("p (k n) -> p k n", k=16) for i in range(2)]
        wg_sb = nc.alloc_sbuf_tensor("wg_sb", [128, 4, 4, 512], BF16)
        bg_sb = nc.alloc_sbuf_tensor("bg_sb", [128, 16], F32)
        os_sb = nc.alloc_sbuf_tensor("os_sb", [128, 16], F32)
        inv_sb = nc.alloc_sbuf_tensor("inv_sb", [128, 4, 16], F32)
        vT = nc.alloc_sbuf_tensor("vT", [128, NTOK], F32)
        sA = nc.alloc_sbuf_tensor("sA", [128, NTOK], F32)
        sB = nc.alloc_sbuf_tensor("sB", [128, NTOK], F32)
        pooled = nc.alloc_sbuf_tensor("pooled", [128, 4, TPC], BF16)
        ge = nc.alloc_sbuf_tensor("ge", [128, TPC], F32)
        gs = nc.alloc_sbuf_tensor("gs", [128, TPC], F32)
        mt = nc.alloc_sbuf_tensor("mt", [128, TPC], F32)

        xT_v = xT.rearrange("(k p) n -> p k n", p=128)
        for h in range(4):
            P.dma("sp" if h % 2 == 0 else "act", lambda e, h=h: e.dma_start(out=xT_sb[:, 4 * h:4 * h + 4, :], in_=xT_v[:, 4 * h:4 * h + 4, :]),
                  "xT", writes=["xT"])
        wg_v = w_grp.rearrange("g (c p) d -> p g c d", p=128)
        P.dma("sp", lambda e: e.dma_start(out=bg_sb[:], in_=bgrp), "c4", writes=["bg"])
        P.dma("sp", lambda e: e.dma_start(out=os_sb[:], in_=oscale), "c5", writes=["os"])
        P.dma("sp", lambda e: e.dma_start(out=inv_sb[:].rearrange("p g i -> p (g i)"), in_=invtab.partition_broadcast(128)),
              "c6", writes=["inv"])
        w_in_v = w_in.rearrange("(k p) n -> p k n", p=128)

        def load_w(i, col0):
            b = i % 2
            P.dma("pool", lambda e: e.dma_start(out=wbuf[b][:, :, :], in_=w_in_v[:, :, col0:col0 + 512]),
                  "wbuf%d" % b, writes=["wbuf%d" % b])

        order = []
        for gi in range(4):
            order.append(gi * 512)
            order.append(D + gi * 512)
        load_w(0, order[0])
        for g in range(4):
            P.dma("pool", lambda e, g=g: e.dma_start(out=wg_sb[:, g, :, :], in_=wg_v[:, g, :, :]), "wg", writes=["wg"])
        load_w(1, order[1])

        for gi in range(4):
            wv = 2 * gi
            wgt = 2 * gi + 1
            w = 2 ** (gi + 1)
            for ft in range(4):
                b = wv % 2
                regs = [(ps[0], 0, HALO, 512), (ps[1], 0, HALO + 512, 512), (ps[2], 0, 0, HALO)]
                for (pt, pc, tc0, n) in regs:
                    for kc in range(16):
                        P.op("pe", lambda e, pt=pt, pc=pc, tc0=tc0, n=n, kc=kc, b=b, ft=ft: e.matmul(
                            pt[:, pc:pc + n], lhsT=wbuf[b][:, kc, ft * 128:(ft + 1) * 128],
                            rhs=xT_sb[:, kc, tc0:tc0 + n], start=(kc == 0), stop=(kc == 15)),
                            reads=["wbuf%d" % b, "xT"], writes=[PK(pt)])
                P.op("act", lambda e: e.activation(out=vT[:, HALO:HALO + 512], in_=ps[0][:, :], func=AF.Copy),
                     reads=["ps0"], writes=["vT"])
                P.op("act", lambda e: e.activation(out=vT[:, HALO + 512:NTOK], in_=ps[1][:, :], func=AF.Copy),
                     reads=["ps1"], writes=["vT"])
                P.op("act", lambda e: e.activation(out=vT[:, 0:HALO], in_=ps[2][:, 0:HALO], func=AF.Copy),
                     reads=["ps2"], writes=["vT"])
                src, skey = vT, "vT"
                dsts = [(sA, "sA"), (sB, "sB")]
                sh = 1
                lo = 0
                for st in range(gi + 1):
                    dst, dkey = dsts[st % 2]
                    lo = lo + sh
                    P.op("dve" if st % 2 == 0 else "pool",
                         lambda e, dst=dst, src=src, lo=lo, sh=sh: e.tensor_tensor(
                             out=dst[:, lo:NTOK], in0=src[:, lo:NTOK], in1=src[:, lo - sh:NTOK - sh], op=ALU.add),
                         reads=[skey], writes=[dkey])
                    src, skey = dst, dkey
                    sh *= 2
                P.op("dve", lambda e, src=src, w=w, ft=ft: e.scalar_tensor_tensor(
                    out=pooled[:, ft, HALO:TPC], in0=src[:, 2 * HALO:NTOK], scalar=1.0 / w,
                    in1=vT[:, 2 * HALO:NTOK], op0=ALU.mult, op1=ALU.subtract),
                    reads=[skey, "vT"], writes=["pooled%d" % ft])
                P.op("pool", lambda e, src=src, gi=gi: e.tensor_tensor(
                    out=src[:, HALO:2 * HALO], in0=src[:, HALO:2 * HALO], in1=inv_sb[:, gi, :], op=ALU.mult),
                    reads=[skey, "inv"], writes=[skey])
                P.op("pool", lambda e, src=src, ft=ft: e.tensor_tensor(
                    out=pooled[:, ft, 0:HALO], in0=src[:, HALO:2 * HALO], in1=vT[:, HALO:2 * HALO], op=ALU.subtract),
                    reads=[skey, "vT"], writes=["pooled%d" % ft])
            if gi < 3:
                load_w(wv + 2, order[wv + 2])
            for dti in range(4):
                dt_ = gi * 4 + dti
                b = wgt % 2
                for hf in range(2):
                    pt = ps[3 + hf]
                    for kc in range(16):
                        P.op("pe", lambda e, pt=pt, hf=hf, kc=kc, b=b, dti=dti: e.matmul(
                            pt[:, :], lhsT=wbuf[b][:, kc, dti * 128:(dti + 1) * 128],
                            rhs=xT_sb[:, kc, HALO + hf * 512:HALO + (hf + 1) * 512],
                            start=(kc == 0), stop=(kc == 15)),
                            reads=["wbuf%d" % b, "xT"], writes=[PK(pt)])
                for hf in range(2):
                    pt = ps[5 + hf]
                    for cc in range(4):
                        P.op("pe", lambda e, pt=pt, hf=hf, cc=cc, gi=gi, dti=dti: e.matmul(
                            pt[:, :], lhsT=wg_sb[:, gi, cc, dti * 128:(dti + 1) * 128],
                            rhs=pooled[:, cc, hf * 512:(hf + 1) * 512], start=(cc == 0), stop=(cc == 3)),
                            reads=["wg", "pooled%d" % cc], writes=[PK(pt)])
                for hf in range(2):
                    sl = slice(hf * 512, (hf + 1) * 512)
                    P.op("act", lambda e, hf=hf, sl=sl: e.activation(out=ge[:, sl], in_=ps[3 + hf][:, :], func=AF.Exp, scale=-1.0),
                         reads=["ps%d" % (3 + hf)], writes=["ge%d" % hf])
                    P.op("pool", lambda e, sl=sl: e.tensor_scalar(out=ge[:, sl], in0=ge[:, sl], scalar1=1.0, scalar2=None, op0=ALU.add),
                         reads=["ge%d" % hf], writes=["ge%d" % hf])
                    P.op("dve", lambda e, sl=sl: e.reciprocal(out=ge[:, sl], in_=ge[:, sl]),
                         reads=["ge%d" % hf], writes=["ge%d" % hf])
                    P.op("dve", lambda e, hf=hf, sl=sl: e.tensor_tensor(out=gs[:, sl], in0=ge[:, sl], in1=ps[3 + hf][:, :], op=ALU.mult),
                         reads=["ge%d" % hf, "ps%d" % (3 + hf)], writes=["gs%d" % hf])
                    P.op("act", lambda e, hf=hf, sl=sl, dt_=dt_: e.activation(
                        out=mt[:, sl], in_=ps[5 + hf][:, :], func=AF.Identity,
                        bias=bg_sb[:, dt_:dt_ + 1], scale=1.0),
                        reads=["ps%d" % (5 + hf), "bg"], writes=["mt%d" % hf])
                    P.op("dve", lambda e, sl=sl, dt_=dt_: e.scalar_tensor_tensor(
                        out=yT[:, dt_, sl], in0=mt[:, sl], scalar=os_sb[:, dt_:dt_ + 1], in1=gs[:, sl],
                        op0=ALU.mult, op1=ALU.mult),
                        reads=["mt%d" % hf, "gs%d" % hf, "os"], writes=["yT"])
            if gi < 3:
                load_w(wgt + 2, order[wgt + 2])
    else:
        ssq_sb = nc.alloc_sbuf_tensor("ssq_sb", [8, TPC], F32)
        sel_sb = nc.alloc_sbuf_tensor("sel_sb", [8, 256], F32)
        rs_sb = nc.alloc_sbuf_tensor("rs_sb", [128, 2, TPC], F32)
        yT_v = yT_d.rearrange("(k p) n -> p k n", p=128)
        for h in range(4):
            P.dma("sp", lambda e, h=h: e.dma_start(out=yT[:, 4 * h:4 * h + 4, :], in_=yT_v[:, 4 * h:4 * h + 4, :]),
                  "yT", writes=["yT"])
        P.dma("sp", lambda e: e.dma_start(out=ssq_sb[:], in_=ssq8), "c7", writes=["ssq"])
        P.dma("sp", lambda e: e.dma_start(out=sel_sb[:], in_=sel_d), "c8", writes=["sel"])
        for g in range(2):
            for hf in range(2):
                pt = ps[2 * g + hf]
                P.op("pe", lambda e, pt=pt, g=g, hf=hf: e.matmul(
                    pt[:, :], lhsT=sel_sb[:, g * 128:(g + 1) * 128], rhs=ssq_sb[:, hf * 512:(hf + 1) * 512],
                    start=True, stop=True), reads=["sel", "ssq"], writes=[PK(pt)])
                P.op("act", lambda e, pt=pt, g=g, hf=hf: e.activation(
                    out=rs_sb[:, g, hf * 512:(hf + 1) * 512], in_=pt[:, :], func=AF.Ln, bias=EPS, scale=1.0 / 512),
                    reads=[PK(pt)], writes=["rs%d%d" % (g, hf)])
                P.op("act", lambda e, g=g, hf=hf: e.activation(
                    out=rs_sb[:, g, hf * 512:(hf + 1) * 512], in_=rs_sb[:, g, hf * 512:(hf + 1) * 512],
                    func=AF.Exp, scale=-0.5),
                    reads=["rs%d%d" % (g, hf)], writes=["rs%d%d" % (g, hf)])
            for kc in range(4):
                k = 8 + 4 * g + kc
                P.op("dve" if kc % 2 == 0 else "pool", lambda e, k=k, g=g: e.tensor_tensor(
                    out=yT[:, k, :], in0=yT[:, k, :], in1=rs_sb[:, g, :], op=ALU.mult),
                    reads=["yT", "rs%d0" % g, "rs%d1" % g], writes=["yT"])

    w_out_v = w_out.rearrange("(k p) n -> p k n", p=128)
    fkeys = ["xT", "wbuf0", "wbuf1"] if front else []
    for h in range(4):
        P.dma("pool", lambda e, h=h: e.dma_start(out=wout_sb[:, 4 * h:4 * h + 4, :], in_=w_out_v[:, 4 * h:4 * h + 4, :]),
              "wout", writes=["wout"] + fkeys)
    for tt in range(NT):
        P.dma("sp", lambda e, tt=tt: e.dma_start(out=xr_sb[:], in_=xres[tt * 128:(tt + 1) * 128, :]), "xr", writes=["xr"])
        for nb in range(4):
            pt = ps[4 * (tt % 2) + nb]
            for kc in range(16):
                P.op("pe", lambda e, pt=pt, kc=kc, tt=tt, nb=nb: e.matmul(
                    pt[:, :], lhsT=yT[:, kc, tt * 128:(tt + 1) * 128], rhs=wout_sb[:, kc, nb * 512:(nb + 1) * 512],
                    start=(kc == 0), stop=(kc == 15)), reads=["yT", "wout"], writes=[PK(pt)])
            P.op("dve", lambda e, pt=pt, nb=nb: e.scalar_tensor_tensor(
                out=z_sb[:, nb * 512:(nb + 1) * 512], in0=xr_sb[:, nb * 512:(nb + 1) * 512], scalar=ALPHA,
                in1=pt[:, :], op0=ALU.mult, op1=ALU.add),
                reads=["xr", PK(pt)], writes=["z%d" % nb])
            P.op("dve", lambda e, nb=nb: e.bn_stats(out=stats[:, nb, :], in_=z_sb[:, nb * 512:(nb + 1) * 512]),
                 reads=["z%d" % nb], writes=["stats%d" % nb])
        zk = ["z%d" % i for i in range(4)]
        P.op("dve", lambda e: e.bn_aggr(out=mv[:], in_=stats[:].rearrange("p a b -> p (a b)")),
             reads=["stats%d" % i for i in range(4)], writes=["mv"])
        P.op("act", lambda e: e.activation(out=rstd[:], in_=mv[:, 1:2], func=AF.Ln, bias=EPS, scale=1.0),
             reads=["mv"], writes=["rstd"])
        P.op("act", lambda e: e.activation(out=rstd[:], in_=rstd[:], func=AF.Exp, scale=-0.5),
             reads=["rstd"], writes=["rstd"])
        P.op("dve", lambda e: e.scalar_tensor_tensor(out=nmr[:], in0=mv[:, 0:1], scalar=-1.0, in1=rstd[:],
                                                     op0=ALU.mult, op1=ALU.mult),
             reads=["mv", "rstd"], writes=["nmr"])
        P.op("act", lambda e: e.activation(out=z_sb[:], in_=z_sb[:], func=AF.Identity, bias=nmr[:], scale=rstd[:]),
             reads=zk + ["nmr", "rstd"], writes=zk)
        P.op("pool", lambda e: e.tensor_tensor(out=z_sb[:], in0=z_sb[:], in1=lng_sb[:], op=ALU.mult),
             reads=zk + ["lng"], writes=zk)
        P.op("dve", lambda e: e.tensor_tensor(out=z_sb[:], in0=z_sb[:], in1=lnb_sb[:], op=ALU.add),
             reads=zk + ["lnb"], writes=zk)
        P.dma("sp", lambda e, tt=tt: e.dma_start(out=xo[tt * 128:(tt + 1) * 128, :], in_=z_sb[:]), "xo",
              reads=zk, is_output=True)
        for grp in range(4):
            pk = "ps%d" % (4 * ((tt + 1) % 2) + grp)
            pt = ps[4 * ((tt + 1) % 2) + grp]
            for j in range(4):
                kc = grp * 4 + j
                P.op("pe", lambda e, pt=pt, j=j, kc=kc: e.transpose(
                    pt[:, j * 128:(j + 1) * 128], z_sb[:, kc * 128:(kc + 1) * 128], identf[:]),
                    reads=zk + ["identf"], writes=[pk])
            P.op("act", lambda e, pt=pt, grp=grp: e.activation(out=zT_sb[:, grp, :], in_=pt[:, :], func=AF.Copy),
                 reads=[pk], writes=["zT%d" % grp])
            P.dma("act", lambda e, grp=grp, tt=tt: e.dma_start(
                out=xoT_v[:, grp * 4:grp * 4 + 4, tt * 128:(tt + 1) * 128],
                in_=zT_sb[:, grp, :].rearrange("p (k n) -> p k n", k=4)), "xoT%d" % grp,
                reads=["zT%d" % grp], is_output=True)

    P.emit()
    return nc


def build_even(x_bf16, nblk=S // 512):
    nc = bass.Bass("TRN2", target_bir_lowering=False)
    P = Prog(nc)
    NBLK = nblk
    S = nblk * 512
    xT = _din(nc, "xT", [D, S], BF16 if x_bf16 else F32)
    wA = _din(nc, "wA", [D, NPJ])
    cw_d = _din(nc, "cw", [128, 12])
    cb_d = _din(nc, "cb", [128, 3])
    colp_d = _din(nc, "colp", [128, 8])
    dtp_d = _din(nc, "dtp", [2, 2])
    lamv_d = _din(nc, "lamv", [256])
    identf_d = _din(nc, "identf", [128, 128])
    mask2_d = _din(nc, "mask2", [128, 64])
    bdm_d = _din(nc, "bdmask", [128, 128])
    hsel_d = _din(nc, "hsel", [2, 128])
    eye2_d = _din(nc, "eye2", [2, 2])
    yatt = _dout(nc, "yatt", [128, S], BF16)
    yssm = _dout(nc, "yssm", [128, S], BF16)
    ssq = _dout(nc, "ssq", [1, S])

    A = nc.alloc_sbuf_tensor
    W_sb = A("W_sb", [128, 16, NPJ], BF16)
    xb = [A("xb%d" % i, [128, 16, 512], BF16) for i in range(2)]
    KT = A("KT", [128, S], BF16)
    V = A("V", [128, S // 128, 128], BF16)
    QT = A("QT", [128, 512], BF16)
    sg = A("sg", [128, 512], F32)
    sz = A("sz", [128, 512], F32)
    raw = A("raw", [128, 3, 515], F32)
    xsT = A("xsT", [128, 512], F32)
    xsdup = A("xsdup", [128, 1024], BF16)
    BTdup = A("BTdup", [128, 1024], BF16)
    CT = A("CT", [128, 512], BF16)
    tmp = {k: A("t" + k, [128, 512], F32) for k in "ABCDEF"}
    E = [[A("E%d%d" % (i, m), [128, 512], BF16) for m in range(2)] for i in range(2)]
    ysT = A("ysT", [128, 512], F32)
    yatt_sb = A("yatt_sb", [128, 512], BF16)
    yssm_sb = A("yssm_sb", [128, 512], BF16)
    ssq_sb = A("ssq_sb", [1, 512], F32)
    cw = A("cw_sb", [128, 12], F32)
    cb = A("cb_sb", [128, 3], F32)
    colp = A("colp_sb", [128, 8], F32)
    lamv = A("lamv_sb", [128, 256], F32)
    lamt = A("lamt", [128, 128], F32)
    lams = A("lams", [128, 4], F32)
    identb = A("identb", [128, 128], BF16)
    mask2 = A("mask2_sb", [128, 64], F32)
    bdm = A("bdm_sb", [128, 128], F32)
    ones_b = A("ones_b", [128, 128], BF16)
    ones_f = A("ones_f", [128, 128], F32)
    hsel = A("hsel_sb", [2, 128], F32)
    eye2 = A("eye2_sb", [2, 2], F32)
    dtp = A("dtp_sb", [2, 2], F32)
    Acol = A("Acol", [2, 1], F32)
    dt_sb = A("dt_sb", [2, 512], F32)
    a_sb = A("a_sb", [2, 512], F32)
    acs = A("acs", [2, 512], F32)
    dtBD = A("dtBD", [2, 1024], F32)
    acsBD = A("acsBD", [2, 1024], F32)
    lastbc = A("lastbc", [2, 8, 128], F32)
    cols = A("cols", [128, 16], F32)
    negc = A("negc", [128, 8], F32)
    Sst = A("Sst", [128, 128], F32)
    Sbf = A("Sbf", [128, 128], BF16)
    Gm = A("Gm", [128, 64], F32)
    Dm = A("Dm", [128, 64], F32)
    E2 = A("E2", [128, 64], F32)
    M2 = A("M2", [128, 64], BF16)
    R2e = A("R2e", [128, 64], F32)
    XdtBD = A("XdtBD", [128, 128], BF16)
    XdecBD = A("XdecBD", [128, 128], BF16)
    Btok = A("Btok", [128, 128], BF16)
    CD = A("CD", [128, 128], F32)
    y1 = A("y1", [128, 64], F32)
    ps = [nc.alloc_psum_tensor("ps%d" % i, [128, 512], F32) for i in range(8)]
    psS = [ps[0], ps[1]]
    psO = [ps[2], ps[3]]
    psD = [ps[4], ps[5]]
    rG, rR, rX, rcol = ps[6][:, 0:64], ps[6][:, 64:128], ps[6][:, 128:256], ps[6][:, 256:272]
    rB, rC, rS = ps[7][:, 0:128], ps[7][:, 128:256], ps[7][:, 256:384]
    rYo, rYd = ps[7][:, 384:448], ps[7][:, 448:512]

    def ld(q, dst, src, key):
        P.dma(q, lambda e: e.dma_start(out=dst, in_=src), key, writes=[key])

    ld("sp", cw[:], cw_d, "cw")
    ld("sp", cb[:], cb_d, "cb")
    ld("sp", colp[:], colp_d, "colp")
    ld("sp", dtp[:], dtp_d, "dtp")
    ld("sp", lamv[:], lamv_d.partition_broadcast(128), "lamv")
    ld("sp", mask2[:], mask2_d, "mask2")
    ld("sp", bdm[:], bdm_d, "bdm")
    ld("sp", hsel[:], hsel_d, "hsel")
    ld("sp", eye2[:], eye2_d, "eye2")
    ld("pool", identb[:], identf_d, "identb")
    P.op("pool", lambda e: e.memset(ones_b[:], 1.0), writes=["ones_b"])
    P.op("pool", lambda e: e.memset(ones_f[:], 1.0), writes=["ones_f"])
    P.op("pool", lambda e: e.memset(raw[:], 0.0), writes=["raw0", "raw1", "raw2"])
    P.op("pool", lambda e: e.memset(Sst[:], 0.0), writes=["Sst"])
    wA_v = wA.rearrange("(k p) n -> p k n", p=128)
    for h in range(4):
        P.dma("pool", lambda e, h=h: e.dma_start(out=W_sb[:, 4 * h:4 * h + 4, :], in_=wA_v[:, 4 * h:4 * h + 4, :]),
              "W", writes=["W"])
    xT_v = xT.rearrange("(k p) n -> p k n", p=128)

    def load_x(b):
        i = b % 2
        for h in range(2):
            if x_bf16:
                P.dma("sp" if h == 0 else "act", lambda e, h=h: e.dma_start(
                    out=xb[i][:, 8 * h:8 * h + 8, :], in_=xT_v[:, 8 * h:8 * h + 8, b * 512:(b + 1) * 512]),
                    "xb%d" % i, writes=["xb%d" % i])
            else:
                P.dma("pool", lambda e, h=h: e.dma_start(
                    out=xb[i][:, 8 * h:8 * h + 8, :], in_=xT_v[:, 8 * h:8 * h + 8, b * 512:(b + 1) * 512]),
                    "xb%d" % i, writes=["xb%d" % i])

    P.op("dve", lambda e: e.tensor_tensor(out=lamt[:, 0:64], in0=lamv[:, 0:64], in1=lamv[:, 64:128], op=ALU.mult),
         reads=["lamv"], writes=["lamt0"])
    P.op("dve", lambda e: e.tensor_tensor(out=lamt[:, 64:128], in0=lamv[:, 128:192], in1=lamv[:, 192:256], op=ALU.mult),
         reads=["lamv"], writes=["lamt1"])
    P.op("dve", lambda e: e.reduce_sum(out=lams[:, 0:1], in_=lamt[:, 0:64], axis=mybir.AxisListType.X),
         reads=["lamt0"], writes=["lams0"])
    P.op("dve", lambda e: e.reduce_sum(out=lams[:, 1:2], in_=lamt[:, 64:128], axis=mybir.AxisListType.X),
         reads=["lamt1"], writes=["lams1"])
    P.op("act", lambda e: e.activation(out=lams[:, 0:2], in_=lams[:, 0:2], func=AF.Exp),
         reads=["lams0", "lams1"], writes=["lams0", "lams1"])
    P.op("dve", lambda e: e.tensor_tensor(out=lams[:, 2:3], in0=lams[:, 1:2], in1=lams[:, 0:1], op=ALU.subtract),
         reads=["lams0", "lams1"], writes=["neglam"])
    P.op("dve", lambda e: e.tensor_tensor(out=lams[:, 2:3], in0=lams[:, 2:3], in1=colp[:, 3:4], op=ALU.subtract),
         reads=["neglam", "colp"], writes=["neglam"])
    P.op("dve", lambda e: e.tensor_tensor(out=lams[:, 3:4], in0=colp[:, 2:3], in1=colp[:, 4:5], op=ALU.mult),
         reads=["colp"], writes=["coef"])
    P.op("act", lambda e: e.activation(out=Acol[:], in_=dtp[:, 1:2], func=AF.Exp), reads=["dtp"], writes=["Acol"])
    P.op("dve", lambda e: e.tensor_scalar(out=Acol[:], in0=Acol[:], scalar1=-1.0, scalar2=None, op0=ALU.mult),
         reads=["Acol"], writes=["Acol"])

    def silu_from(src_ap, src_keys, dst_ap, dst_key, t, tkey):
        P.op("act", lambda e: e.activation(out=t, in_=src_ap, func=AF.Exp, scale=-1.0), reads=src_keys, writes=[tkey])
        P.op("pool", lambda e: e.tensor_scalar(out=t, in0=t, scalar1=1.0, scalar2=None, op0=ALU.add),
             reads=[tkey], writes=[tkey])
        P.op("dve", lambda e: e.reciprocal(out=t, in_=t), reads=[tkey], writes=[tkey])
        P.op("dve", lambda e: e.tensor_tensor(out=dst_ap, in0=t, in1=src_ap, op=ALU.mult),
             reads=[tkey] + list(src_keys), writes=[dst_key])

    def inproj(b):
        xbi = xb[b % 2]
        xk = "xb%d" % (b % 2)
        bank = [0]

        def nextbank():
            bank[0] ^= 1
            return ps[bank[0]], "ps%d" % bank[0]

        def proj(c0):
            pt, pk = nextbank()
            for kc in range(16):
                P.op("pe", lambda e, kc=kc: e.matmul(pt[:, :], lhsT=W_sb[:, kc, c0:c0 + 128], rhs=xbi[:, kc, :],
                                                     start=(kc == 0), stop=(kc == 15)),
                     reads=["W", xk], writes=[pk])
            return pt, pk

        pt, pk = proj(0)
        P.op("act", lambda e, pt=pt: e.activation(out=QT[:], in_=pt[:, :], func=AF.Copy), reads=[pk], writes=["QT"])
        pt, pk = proj(128)
        P.op("act", lambda e, pt=pt: e.activation(out=KT[:, b * 512:(b + 1) * 512], in_=pt[:, :], func=AF.Copy),
             reads=[pk], writes=["KT%d" % b])
        pt, pk = nextbank()
        for i in range(4):
            for kc in range(16):
                P.op("pe", lambda e, pt=pt, i=i, kc=kc: e.matmul(
                    pt[:, i * 128:(i + 1) * 128], lhsT=xbi[:, kc, i * 128:(i + 1) * 128], rhs=W_sb[:, kc, 256:384],
                    start=(kc == 0), stop=(kc == 15)), reads=["W", xk], writes=[pk])
        P.op("act", lambda e, pt=pt: e.activation(
            out=V[:, b * 4:(b + 1) * 4, :].rearrange("p a d -> p (a d)"), in_=pt[:, :], func=AF.Copy),
            reads=[pk], writes=["V%d" % b])
        pt, pk = proj(384)
        silu_from(pt[:, :], [pk], sg[:], "sg", tmp["A"][:], "tA")
        pt, pk = proj(512)
        silu_from(pt[:, :], [pk], sz[:], "sz", tmp["B"][:], "tB")
        for t in range(3):
            pt, pk = proj(640 + 128 * t)
            rk = "raw%d" % t
            P.op("act", lambda e, pt=pt, t=t: e.activation(out=raw[:, t, 3:515], in_=pt[:, :], func=AF.Copy),
                 reads=[pk], writes=[rk])
            acc = tmp["C"][:]
            P.op("dve", lambda e, t=t: e.tensor_scalar(
                out=acc, in0=raw[:, t, 3:515], scalar1=cw[:, 4 * t + 3:4 * t + 4], scalar2=cb[:, t:t + 1],
                op0=ALU.mult, op1=ALU.add), reads=[rk, "cw", "cb"], writes=["tC"])
            for j in range(3):
                P.op("dve", lambda e, t=t, j=j: e.scalar_tensor_tensor(
                    out=acc, in0=raw[:, t, j:j + 512], scalar=cw[:, 4 * t + j:4 * t + j + 1], in1=acc,
                    op0=ALU.mult, op1=ALU.add), reads=[rk, "cw", "tC"], writes=["tC"])
            P.op("pool", lambda e, t=t: e.tensor_copy(out=raw[:, t, 0:3], in_=raw[:, t, 512:515]),
                 reads=[rk, "tC"], writes=[rk])
            if t == 0:
                silu_from(acc, ["tC"], xsT[:], "xsT", tmp["D"][:], "tD")
                for u in range(2):
                    P.op("pool", lambda e, u=u: e.tensor_copy(
                        out=xsdup[:].rearrange("p (c u l) -> p c u l", c=8, u=2)[:, :, u, :],
                        in_=xsT[:].rearrange("p (c l) -> p c l", c=8)), reads=["xsT"], writes=["xsdup"])
            elif t == 1:
                silu_from(acc, ["tC"], tmp["E"][:], "tE", tmp["D"][:], "tD")
                for u in range(2):
                    P.op("pool", lambda e, u=u: e.tensor_copy(
                        out=BTdup[:].rearrange("p (c u l) -> p c u l", c=8, u=2)[:, :, u, :],
                        in_=tmp["E"][:].rearrange("p (c l) -> p c l", c=8)), reads=["tE"], writes=["BTdup"])
            else:
                silu_from(acc, ["tC"], CT[:], "CT", tmp["D"][:], "tD")
        pt, pk = nextbank()
        for kc in range(16):
            P.op("pe", lambda e, pt=pt, kc=kc: e.matmul(pt[0:2, :], lhsT=W_sb[:, kc, 1024:1026], rhs=xbi[:, kc, :],
                                                        start=(kc == 0), stop=(kc == 15)),
                 reads=["W", xk], writes=[pk])
        P.op("act", lambda e, pt=pt: e.activation(out=dt_sb[:], in_=pt[0:2, :], func=AF.Exp, bias=dtp[:, 0:1], scale=1.0),
             reads=[pk, "dtp"], writes=["dt"])
        P.op("act", lambda e: e.activation(out=dt_sb[:], in_=dt_sb[:], func=AF.Ln, bias=1.0, scale=1.0),
             reads=["dt"], writes=["dt"])
        P.op("dve", lambda e: e.tensor_scalar(out=a_sb[:], in0=dt_sb[:], scalar1=Acol[:, 0:1], scalar2=None, op0=ALU.mult),
             reads=["dt", "Acol"], writes=["a"])
        for c in range(8):
            P.op("dve", lambda e, c=c: e.tensor_tensor_scan(
                out=acs[:, c * 64:(c + 1) * 64], data0=ones_f[0:2, 0:64], data1=a_sb[:, c * 64:(c + 1) * 64],
                initial=0.0, op0=ALU.mult, op1=ALU.add), reads=["a", "ones_f"], writes=["acs"])
        for h in range(2):
            P.op("pool", lambda e, h=h: e.tensor_scalar(
                out=dtBD[:].rearrange("p (c u l) -> p c u l", c=8, u=2)[:, :, h, :],
                in0=dt_sb[:].rearrange("p (c l) -> p c l", c=8), scalar1=eye2[:, h:h + 1], scalar2=None, op0=ALU.mult),
                reads=["dt", "eye2"], writes=["dtBD"])
            P.op("pool", lambda e, h=h: e.tensor_scalar(
                out=acsBD[:].rearrange("p (c u l) -> p c u l", c=8, u=2)[:, :, h, :],
                in0=acs[:].rearrange("p (c l) -> p c l", c=8), scalar1=eye2[:, h:h + 1], scalar2=None, op0=ALU.mult),
                reads=["acs", "eye2"], writes=["acsBD"])
        for c in range(8):
            P.op("pool", lambda e, c=c: e.tensor_scalar(
                out=lastbc[:, c, :], in0=ones_f[0:2, :], scalar1=acs[:, c * 64 + 63:c * 64 + 64], scalar2=None, op0=ALU.mult),
                reads=["acs", "ones_f"], writes=["lastbc"])

    def attention(b):
        nkb = 4 * b + 4
        for kb in range(nkb):
            j = kb - 4 * b
            c0 = max(0, j) * 128
            st = kb % 2
            for m in range(2):
                P.op("pe", lambda e, kb=kb, m=m, c0=c0: e.matmul(
                    psS[m][:, c0:512], lhsT=KT[m * 64:(m + 1) * 64, kb * 128:(kb + 1) * 128],
                    rhs=QT[m * 64:(m + 1) * 64, c0:512], start=True, stop=True),
                    reads=["KT%d" % (kb // 4), "QT"], writes=["ps%d" % m])
                P.op("act", lambda e, m=m, c0=c0, st=st: e.activation(
                    out=E[st][m][:, c0:512], in_=psS[m][:, c0:512], func=AF.Exp, scale=0.125),
                    reads=["ps%d" % m], writes=["E%d%d" % (st, m)])
                if j >= 0:
                    P.op("pool", lambda e, m=m, c0=c0, st=st: e.memset(E[st][m][64:128, c0:c0 + 64], 0.0),
                         reads=[], writes=["E%d%d" % (st, m)])
                P.op("pe", lambda e, kb=kb, m=m, c0=c0, st=st: e.matmul(
                    psO[m][:, c0:512], lhsT=V[:, kb, :], rhs=E[st][m][:, c0:512],
                    start=(kb == 0), stop=(kb == nkb - 1)),
                    reads=["V%d" % (kb // 4), "E%d%d" % (st, m)], writes=["ps%d" % (2 + m)])
                P.op("pe", lambda e, kb=kb, m=m, c0=c0, st=st: e.matmul(
                    psD[m][:, c0:512], lhsT=ones_b[:], rhs=E[st][m][:, c0:512],
                    start=(kb == 0), stop=(kb == nkb - 1)),
                    reads=["ones_b", "E%d%d" % (st, m)], writes=["ps%d" % (4 + m)])
        tA, tB, tC, tD = tmp["A"][:], tmp["B"][:], tmp["C"][:], tmp["D"][:]
        P.op("dve", lambda e: e.reciprocal(out=tA, in_=psD[0][:, :]), reads=["ps4"], writes=["tA"])
        P.op("dve", lambda e: e.reciprocal(out=tB, in_=psD[1][:, :]), reads=["ps5"], writes=["tB"])
        P.op("dve", lambda e: e.tensor_tensor(out=tA, in0=tA, in1=psO[0][:, :], op=ALU.mult), reads=["tA", "ps2"], writes=["tA"])
        P.op("dve", lambda e: e.tensor_tensor(out=tB, in0=tB, in1=psO[1][:, :], op=ALU.mult), reads=["tB", "ps3"], writes=["tB"])
        P.op("dve", lambda e: e.scalar_tensor_tensor(out=tA, in0=tB, scalar=lams[:, 2:3], in1=tA, op0=ALU.mult, op1=ALU.add),
             reads=["tA", "tB", "neglam"], writes=["tA"])
        P.op("act", lambda e: e.activation(out=tB, in_=tA, func=AF.Square), reads=["tA"], writes=["tB"])
        P.op("pe", lambda e: e.matmul(psS[0][:, :], lhsT=ones_f[:], rhs=tB, start=True, stop=True),
             reads=["ones_f", "tB"], writes=["ps0"])
        P.op("act", lambda e: e.activation(out=tB, in_=psS[0][:, :], func=AF.Ln, bias=EPS, scale=1.0 / 128),
             reads=["ps0"], writes=["tB"])
        P.op("act", lambda e: e.activation(out=tB, in_=tB, func=AF.Exp, scale=-0.5), reads=["tB"], writes=["tB"])
        P.op("dve", lambda e: e.tensor_tensor(out=tA, in0=tA, in1=tB, op=ALU.mult), reads=["tA", "tB"], writes=["tA"])
        P.op("dve", lambda e: e.scalar_tensor_tensor(out=yatt_sb[:], in0=tA, scalar=lams[:, 3:4], in1=sg[:],
                                                     op0=ALU.mult, op1=ALU.mult),
             reads=["tA", "coef", "sg"], writes=["yatt_sb"])
        P.dma("sp", lambda e: e.dma_start(out=yatt[:, b * 512:(b + 1) * 512], in_=yatt_sb[:]), "yatt",
              reads=["yatt_sb"], is_output=True)

    def ssd(b):
        for c in range(8):
            P.op("pe", lambda e, c=c: e.matmul(rcol[:, 2 * c:2 * c + 1], lhsT=dtBD[:, c * 128:(c + 1) * 128],
                                               rhs=ones_f[0:2, 0:1], start=True, stop=True),
                 reads=["dtBD", "ones_f"], writes=["ps6"])
            P.op("pe", lambda e, c=c: e.matmul(rcol[:, 2 * c + 1:2 * c + 2], lhsT=acsBD[:, c * 128:(c + 1) * 128],
                                               rhs=ones_f[0:2, 0:1], start=True, stop=True),
                 reads=["acsBD", "ones_f"], writes=["ps6"])
        P.op("act", lambda e: e.activation(out=cols[:], in_=rcol, func=AF.Copy), reads=["ps6"], writes=["cols"])
        P.op("dve", lambda e: e.tensor_scalar(
            out=negc[:], in0=cols[:].rearrange("p (c t) -> p c t", t=2)[:, :, 1], scalar1=-1.0, scalar2=None, op0=ALU.mult),
            reads=["cols"], writes=["negc"])
        for c in range(8):
            cs = slice(c * 64, (c + 1) * 64)
            cd = slice(c * 128, (c + 1) * 128)
            P.op("pe", lambda e, cs=cs, cd=cd: e.matmul(rG, lhsT=BTdup[:, cd], rhs=CT[:, cs], start=True, stop=True),
                 reads=["BTdup", "CT"], writes=["ps6"])
            P.op("pe", lambda e, cs=cs: e.matmul(rR, lhsT=hsel[:], rhs=acs[:, cs], start=True, stop=True),
                 reads=["hsel", "acs"], writes=["ps6"])
            P.op("pe", lambda e, c=c: e.matmul(rC, lhsT=lastbc[:, c, :], rhs=hsel[:], start=True, stop=True),
                 reads=["lastbc", "hsel"], writes=["ps7"])
            P.op("pe", lambda e, cd=cd: e.matmul(rX, lhsT=xsdup[:, cd], rhs=identb[:], start=True, stop=True),
                 reads=["xsdup", "identb"], writes=["ps6"])
            P.op("pe", lambda e, cd=cd: e.matmul(rB, lhsT=BTdup[:, cd], rhs=identb[:], start=True, stop=True),
                 reads=["BTdup", "identb"], writes=["ps7"])
            P.op("dve", lambda e: e.tensor_tensor(out=Gm[:], in0=rG, in1=mask2[:], op=ALU.mult),
                 reads=["ps6", "mask2"], writes=["Gm"])
            P.op("dve", lambda e, c=c: e.tensor_scalar(out=Dm[:], in0=rR, scalar1=negc[:, c:c + 1], scalar2=0.0,
                                                       op0=ALU.add, op1=ALU.min),
                 reads=["ps6", "negc"], writes=["Dm"])
            P.op("act", lambda e: e.activation(out=E2[:], in_=Dm[:], func=AF.Exp), reads=["Dm"], writes=["E2"])
            P.op("act", lambda e: e.activation(out=R2e[:], in_=rR, func=AF.Exp), reads=["ps6"], writes=["R2e"])
            P.op("pool", lambda e: e.tensor_tensor(out=M2[:], in0=Gm[:], in1=E2[:], op=ALU.mult),
                 reads=["Gm", "E2"], writes=["M2"])
            P.op("dve", lambda e, c=c: e.scalar_tensor_tensor(
                out=XdtBD[:], in0=rX, scalar=cols[:, 2 * c:2 * c + 1], in1=bdm[:], op0=ALU.mult, op1=ALU.mult),
                reads=["ps6", "cols", "bdm"], writes=["XdtBD"])
            P.op("pool", lambda e: e.tensor_scalar(out=XdecBD[:], in0=XdtBD[:], scalar1=E2[:, 63:64], scalar2=None,
                                                   op0=ALU.mult),
                 reads=["XdtBD", "E2"], writes=["XdecBD"])
            P.op("act", lambda e: e.activation(out=Btok[:], in_=rB, func=AF.Copy), reads=["ps7"], writes=["Btok"])
            P.op("act", lambda e: e.activation(out=CD[:], in_=rC, func=AF.Exp), reads=["ps7"], writes=["CD"])
            P.op("pool", lambda e: e.tensor_copy(out=Sbf[:], in_=Sst[:]), reads=["Sst"], writes=["Sbf"])
            P.op("pe", lambda e, cs=cs: e.matmul(rYo, lhsT=Sbf[:], rhs=CT[:, cs], start=True, stop=True),
                 reads=["Sbf", "CT"], writes=["ps7"])
            P.op("pe", lambda e: e.matmul(rYd, lhsT=XdtBD[:], rhs=M2[:], start=True, stop=True),
                 reads=["XdtBD", "M2"], writes=["ps7"])
            P.op("pe", lambda e: e.matmul(rS, lhsT=Btok[:], rhs=XdecBD[:], start=True, stop=True),
                 reads=["Btok", "XdecBD"], writes=["ps7"])
            P.op("dve", lambda e: e.tensor_tensor(out=y1[:], in0=rYo, in1=R2e[:], op=ALU.mult),
                 reads=["ps7", "R2e"], writes=["y1"])
            P.op("dve", lambda e, cs=cs: e.tensor_tensor(out=ysT[:, cs], in0=y1[:], in1=rYd, op=ALU.add),
                 reads=["y1", "ps7"], writes=["ysT"])
            P.op("pool", lambda e: e.tensor_tensor(out=Sst[:], in0=Sst[:], in1=CD[:], op=ALU.mult),
                 reads=["Sst", "CD"], writes=["Sst"])
            P.op("dve", lambda e: e.tensor_tensor(out=Sst[:], in0=Sst[:], in1=rS, op=ALU.add),
                 reads=["Sst", "ps7"], writes=["Sst"])
        tE, tF = tmp["E"][:], tmp["F"][:]
        P.op("dve", lambda e: e.scalar_tensor_tensor(out=tE, in0=xsT[:], scalar=colp[:, 0:1], in1=ysT[:],
                                                     op0=ALU.mult, op1=ALU.add),
             reads=["xsT", "colp", "ysT"], writes=["tE"])
        P.op("dve", lambda e: e.tensor_tensor(out=tE, in0=tE, in1=sz[:], op=ALU.mult), reads=["tE", "sz"], writes=["tE"])
        P.op("act", lambda e: e.activation(out=tF, in_=tE, func=AF.Square), reads=["tE"], writes=["tF"])
        P.op("pe", lambda e: e.matmul(ps[1][0:1, :], lhsT=ones_f[:, 0:1], rhs=tF, start=True, stop=True),
             reads=["ones_f", "tF"], writes=["ps1"])
        P.op("act", lambda e: e.activation(out=ssq_sb[:], in_=ps[1][0:1, :], func=AF.Copy), reads=["ps1"], writes=["ssq_sb"])
        P.op("pool", lambda e: e.tensor_scalar(out=yssm_sb[:], in0=tE, scalar1=colp[:, 1:2], scalar2=None, op0=ALU.mult),
             reads=["tE", "colp"], writes=["yssm_sb"])
        P.dma("sp", lambda e: e.dma_start(out=ssq[0:1, b * 512:(b + 1) * 512], in_=ssq_sb[:]), "ssqo",
              reads=["ssq_sb"], is_output=True)
        P.dma("sp", lambda e: e.dma_start(out=yssm[:, b * 512:(b + 1) * 512], in_=yssm_sb[:]), "yssm",
              reads=["yssm_sb"], is_output=True)

    load_x(0)
    for b in range(NBLK):
        if b + 1 < NBLK:
            load_x(b + 1)
        inproj(b)
        attention(b)
        ssd(b)
    P.emit()
    return nc


_NC = {}
BF = ml_dtypes.bfloat16


def _get(name, builder):
    if name not in _NC:
        _NC[name] = builder()
    return _NC[name]


def _launch(nc, in_maps):
    res = run_bass_kernel_spmd(nc, in_maps, core_ids=list(range(len(in_maps))))
    return res.results


_IDENTF = np.eye(128, dtype=np.float32)


def run_odd(xtok, xT_bf, w_in, w_grp, b_grp, scale, w_out, lng, lnb, cores=range(NCORES)):
    nc = _get("odd", lambda: build_tail(True))
    bg = np.ascontiguousarray(b_grp.reshape(16, 128).T)
    osc = np.ascontiguousarray(scale.reshape(16, 128).T)
    in_maps = []
    for c in cores:
        t0 = c * TPC
        xT_c = np.zeros((D, HALO + TPC), BF)
        xT_c[:, HALO:] = xT_bf[:, t0:t0 + TPC]
        if c > 0:
            xT_c[:, :HALO] = xT_bf[:, t0 - HALO:t0]
        inv = np.zeros((4, 16), np.float32)
        for gi in range(4):
            for i in range(16):
                inv[gi, i] = 1.0 / min(t0 + i + 1, 2 ** (gi + 1))
        in_maps.append(dict(xres=np.ascontiguousarray(xtok[t0:t0 + TPC]), w_out=w_out, lng=lng, lnb=lnb,
                            identf=_IDENTF, xT=xT_c, w_in=w_in, w_grp=w_grp, bgrp=bg, oscale=osc,
                            invtab=inv.reshape(-1)))
    res = _launch(nc, in_maps)
    xo = np.concatenate([r["xo"] for r in res], 0)
    xoT = np.concatenate([r["xoT"] for r in res], 1)
    return xo, xoT


def _even_consts():
    mask2 = np.zeros((128, 64), np.float32)
    for u in range(2):
        for s_ in range(64):
            mask2[u * 64 + s_, s_:] = 1.0
    bdm = np.zeros((128, 128), np.float32)
    bdm[:64, :64] = 1.0
    bdm[64:, 64:] = 1.0
    hsel = np.zeros((2, 128), np.float32)
    hsel[0, :64] = 1.0
    hsel[1, 64:] = 1.0
    return dict(identf=_IDENTF, mask2=mask2, bdmask=bdm, hsel=hsel, eye2=np.eye(2, dtype=np.float32))


def run_even(xT, w_in, conv_w, conv_b, dt_bias, a_log, d_skip, ssm_norm_g, lq1, lk1, lq2, lk2, subln_g,
             lam_init, cores=range(NCORES), nblk=S // 512):
    x_bf16 = xT.dtype != np.float32
    nc = _get(("even", x_bf16, nblk), lambda: build_even(x_bf16, nblk))
    consts = _even_consts()
    ar = np.arange(128)
    in_maps = []
    for c in cores:
        g = c // 4
        idx = np.concatenate([c * 128 + ar, 1024 + c * 128 + ar, 2048 + c * 128 + ar, 3072 + c * 128 + ar,
                              4096 + c * 128 + ar, 5120 + c * 128 + ar, 6144 + g * 128 + ar, 6400 + g * 128 + ar,
                              6656 + 2 * c + np.arange(2)])
        wA = np.ascontiguousarray(w_in[:, idx])
        chans = [c * 128 + ar, 1024 + g * 128 + ar, 1280 + g * 128 + ar]
        cw = np.zeros((128, 12), np.float32)
        cb = np.zeros((128, 3), np.float32)
        for t in range(3):
            cw[:, 4 * t:4 * t + 4] = conv_w[:, chans[t]].T
            cb[:, t] = conv_b[chans[t]]
        colp = np.zeros((128, 8), np.float32)
        colp[:, 0] = np.repeat(d_skip[2 * c:2 * c + 2], 64)
        colp[:, 1] = ssm_norm_g[c * 128:(c + 1) * 128]
        colp[:, 2] = subln_g
        colp[:, 3] = lam_init
        colp[:, 4] = 1.0 - lam_init
        dtp = np.stack([dt_bias[2 * c:2 * c + 2], a_log[2 * c:2 * c + 2]], 1).astype(np.float32)
        lamv = np.concatenate([lq1, lk1, lq2, lk2]).astype(np.float32)
        m = dict(xT=xT, wA=wA, cw=cw, cb=cb, colp=colp, dtp=dtp, lamv=lamv)
        m.update(consts)
        in_maps.append(m)
    res = _launch(nc, in_maps)
    return [r["yatt"] for r in res], [r["yssm"] for r in res], [r["ssq"] for r in res]


def _sel8():
    sel = np.zeros((8, 256), np.float32)
    sel[0:4, 0:128] = 1.0
    sel[4:8, 128:256] = 1.0
    return sel


def run_evtail(xtok, yT, ssq8, w_out, lng, lnb, cores=range(NCORES)):
    nc = _get("evtail", lambda: build_tail(False))
    sel = _sel8()
    in_maps = []
    for c in cores:
        t0 = c * TPC
        in_maps.append(dict(xres=np.ascontiguousarray(xtok[t0:t0 + TPC]), w_out=w_out, lng=lng, lnb=lnb,
                            identf=_IDENTF, yT=np.ascontiguousarray(yT[:, t0:t0 + TPC]),
                            ssq8=np.ascontiguousarray(ssq8[:, t0:t0 + TPC]), sel=sel))
    res = _launch(nc, in_maps)
    xo = np.concatenate([r["xo"] for r in res], 0)
    xoT = np.concatenate([r["xoT"] for r in res], 1)
    return xo, xoT


def kernel(x, ev_w_in, ev_conv_w, ev_conv_b, ev_dt_bias, ev_a_log, ev_d_skip, ev_ssm_norm_g,
           ev_lambda_q1, ev_lambda_k1, ev_lambda_q2, ev_lambda_k2, ev_subln_g, ev_w_out,
           od_w_in, od_w_grp, od_b_grp, od_scale, od_w_out, ln_g, ln_b):
    f = lambda a: np.asarray(a, dtype=np.float32)
    xtok = np.ascontiguousarray(f(x)[0])
    xT = np.ascontiguousarray(xtok.T)
    for l in range(DEPTH):
        i = l // 2
        if l % 2 == 0:
            lam_init = 0.8 - 0.6 * math.exp(-0.3 * l)
            ya, ys, sq = run_even(xT, f(ev_w_in[i]), f(ev_conv_w[i]), f(ev_conv_b[i]), f(ev_dt_bias[i]), f(ev_a_log[i]),
                                  f(ev_d_skip[i]), f(ev_ssm_norm_g[i]), f(ev_lambda_q1[i]), f(ev_lambda_k1[i]),
                                  f(ev_lambda_q2[i]), f(ev_lambda_k2[i]), f(ev_subln_g[i]), lam_init)
            yT = np.concatenate(list(ya) + list(ys), 0)
            ssq8 = np.concatenate(list(sq), 0)
            xtok, xT = run_evtail(xtok, yT, ssq8, f(ev_w_out[i]), f(ln_g[l]), f(ln_b[l]))
        else:
            xtok, xT = run_odd(xtok, xT, f(od_w_in[i]), f(od_w_grp[i]), f(od_b_grp[i]), f(od_scale[i]), f(od_w_out[i]),
                               f(ln_g[l]), f(ln_b[l]))
    return np.ascontiguousarray(xtok[None]).astype(np.float32)
```

```python
import math
import numpy as np
import ml_dtypes
import concourse.bass as bass
import concourse.mybir as mybir
from concourse.bass_utils import run_bass_kernel_spmd

F32 = mybir.dt.float32
BF16 = mybir.dt.bfloat16
AF = mybir.ActivationFunctionType
ALU = mybir.AluOpType

NCORES = 8
D = 2048
S = 8192
TPC = S // NCORES
HALO = 16
DEPTH = 4
ALPHA = (2.0 * DEPTH) ** 0.25
EPS = 1e-5
NPJ = 1026


class Prog:
    ENG = ("pe", "act", "dve", "pool", "sp")

    def __init__(self, nc):
        self.nc = nc
        self.q = {e: [] for e in self.ENG}
        self.sem = {e: nc.alloc_semaphore("sem_" + e) for e in ("pe", "act", "dve", "pool")}
        self.cnt = {e: 0 for e in self.sem}
        self.last_w = {}
        self.reads = {}
        self.waited = {e: {} for e in self.ENG}
        self.dsem = {}
        self.out_tokens = []

    def _deps(self, eng, reads, writes):
        toks = []
        for k in list(reads) + list(writes):
            t = self.last_w.get(k)
            if t is not None:
                toks.append(t)
        for k in writes:
            toks.extend(self.reads.get(k, ()))
        waits = {}
        for (sem, val, src) in toks:
            if src == "pe" and eng == "pe":
                continue
            sid = id(sem)
            if self.waited[eng].get(sid, (None, 0))[1] >= val:
                continue
            if sid not in waits or waits[sid][1] < val:
                waits[sid] = (sem, val)
        for sid, (sem, val) in waits.items():
            self.waited[eng][sid] = (sem, val)
        return list(waits.values())

    def _commit(self, tok, reads, writes):
        for k in reads:
            self.reads.setdefault(k, []).append(tok)
        for k in writes:
            self.last_w[k] = tok
            self.reads[k] = []

    def op(self, eng, fn, reads=(), writes=()):
        waits = self._deps(eng, reads, writes)
        self.cnt[eng] += 1
        tok = (self.sem[eng], self.cnt[eng], eng)
        self.q[eng].append((waits, fn, (self.sem[eng], 1)))
        self._commit(tok, reads, writes)

    def dma(self, queue, fn, semname, reads=(), writes=(), is_output=False):
        if semname not in self.dsem:
            self.dsem[semname] = [self.nc.alloc_semaphore("d_" + semname), 0]
        ent = self.dsem[semname]
        waits = self._deps(queue, reads, writes)
        ent[1] += 16
        tok = (ent[0], ent[1], "dma")
        self.q[queue].append((waits, fn, (ent[0], 16)))
        self._commit(tok, reads, writes)
        if is_output:
            self.out_tokens.append(tok)

    def emit(self):
        nc = self.nc
        fin = {}
        for (sem, val, _) in self.out_tokens:
            if id(sem) not in fin or fin[id(sem)][1] < val:
                fin[id(sem)] = (sem, val)
        q = self.q

        def run(e, name, final=False):
            for waits, fn, (sem, amt) in q[name]:
                for (ws, wv) in waits:
                    e.wait_ge(ws, wv)
                fn(e).then_inc(sem, amt)
            if final:
                for (ws, wv) in fin.values():
                    e.wait_ge(ws, wv)

        with nc.Block() as block:
            @block.sync
            def _(e):
                run(e, "sp", final=True)

            @block.tensor
            def _(e):
                run(e, "pe")

            @block.scalar
            def _(e):
                run(e, "act")

            @block.vector
            def _(e):
                run(e, "dve")

            @block.gpsimd
            def _(e):
                run(e, "pool")


def PK(pt):
    return pt.name


def _din(nc, name, shape, dt=F32):
    return nc.dram_tensor(name, list(shape), dt, kind="ExternalInput").ap()


def _dout(nc, name, shape, dt=F32):
    return nc.dram_tensor(name, list(shape), dt, kind="ExternalOutput").ap()


def build_tail(front):
    nc = bass.Bass("TRN2", target_bir_lowering=False)
    P = Prog(nc)
    NT = TPC // 128
    xres = _din(nc, "xres", [TPC, D])
    w_out = _din(nc, "w_out", [D, D])
    lng = _din(nc, "lng", [D])
    lnb = _din(nc, "lnb", [D])
    xo = _dout(nc, "xo", [TPC, D])
    xoT = _dout(nc, "xoT", [D, TPC], BF16)
    identf_d = _din(nc, "identf", [128, 128])
    if front:
        xT = _din(nc, "xT", [D, HALO + TPC], BF16)
        w_in = _din(nc, "w_in", [D, 2 * D])
        w_grp = _din(nc, "w_grp", [4, 512, 512])
        bgrp = _din(nc, "bgrp", [128, 16])
        oscale = _din(nc, "oscale", [128, 16])
        invtab = _din(nc, "invtab", [4 * 16])
    else:
        yT_d = _din(nc, "yT", [D, TPC], BF16)
        ssq8 = _din(nc, "ssq8", [8, TPC])
        sel_d = _din(nc, "sel", [8, 2 * 128])

    NTOK = HALO + TPC
    XT_E = 16 * NTOK
    WB_E = 16 * 512
    big = nc.alloc_sbuf_tensor("big", [128, max(XT_E + 2 * WB_E, 16 * D)], BF16)
    wout_sb = big[:, 0:16 * D].rearrange("p (k n) -> p k n", k=16)
    yT = nc.alloc_sbuf_tensor("yT_sb", [128, 16, TPC], BF16)
    lng_sb = nc.alloc_sbuf_tensor("lng_sb", [128, D], F32)
    lnb_sb = nc.alloc_sbuf_tensor("lnb_sb", [128, D], F32)
    xr_sb = nc.alloc_sbuf_tensor("xr_sb", [128, D], F32)
    z_sb = nc.alloc_sbuf_tensor("z_sb", [128, D], F32)
    zT_sb = nc.alloc_sbuf_tensor("zT_sb", [128, 4, 512], BF16)
    xoT_v = xoT.rearrange("(k p) n -> p k n", p=128)
    stats = nc.alloc_sbuf_tensor("stats", [128, 4, 6], F32)
    mv = nc.alloc_sbuf_tensor("mv", [128, 2], F32)
    rstd = nc.alloc_sbuf_tensor("rstd", [128, 1], F32)
    nmr = nc.alloc_sbuf_tensor("nmr", [128, 1], F32)
    identf = nc.alloc_sbuf_tensor("identf_sb", [128, 128], F32)
    ps = [nc.alloc_psum_tensor("ps%d" % i, [128, 512], F32) for i in range(8)]

    P.dma("sp", lambda e: e.dma_start(out=lng_sb[:], in_=lng.partition_broadcast(128)), "c1", writes=["lng"])
    P.dma("sp", lambda e: e.dma_start(out=lnb_sb[:], in_=lnb.partition_broadcast(128)), "c2", writes=["lnb"])
    P.dma("sp", lambda e: e.dma_start(out=identf[:], in_=identf_d), "c3", writes=["identf"])

    if front:
        xT_sb = big[:, 0:XT_E].rearrange("p (k n) -> p k n", k=16)
        wbuf = [big[:, XT_E + i * WB_E: XT_E + (i + 1) * WB_E].rearrange("p (k n) -> p k n", k=16) for i in range(2)]
        wg_sb = nc.alloc_sbuf_tensor("wg_sb", [128, 4, 4, 512], BF16)
        bg_sb = nc.alloc_sbuf_tensor("bg_sb", [128, 16], F32)
        os_sb = nc.alloc_sbuf_tensor("os_sb", [128, 16], F32)
        inv_sb = nc.alloc_sbuf_tensor("inv_sb", [128, 4, 16], F32)
        vT = nc.alloc_sbuf_tensor("vT", [128, NTOK], F32)
        sA = nc.alloc_sbuf_tensor("sA", [128, NTOK], F32)
        sB = nc.alloc_sbuf_tensor("sB", [128, NTOK], F32)
        pooled = nc.alloc_sbuf_tensor("pooled", [128, 4, TPC], BF16)
        ge = nc.alloc_sbuf_tensor("ge", [128, TPC], F32)
        gs = nc.alloc_sbuf_tensor("gs", [128, TPC], F32)
        mt = nc.alloc_sbuf_tensor("mt", [128, TPC], F32)

        xT_v = xT.rearrange("(k p) n -> p k n", p=128)
        for h in range(4):
            P.dma("sp" if h % 2 == 0 else "act", lambda e, h=h: e.dma_start(out=xT_sb[:, 4 * h:4 * h + 4, :], in_=xT_v[:, 4 * h:4 * h + 4, :]),
                  "xT", writes=["xT"])
        wg_v = w_grp.rearrange("g (c p) d -> p g c d", p=128)
        P.dma("sp", lambda e: e.dma_start(out=bg_sb[:], in_=bgrp), "c4", writes=["bg"])
        P.dma("sp", lambda e: e.dma_start(out=os_sb[:], in_=oscale), "c5", writes=["os"])
        P.dma("sp", lambda e: e.dma_start(out=inv_sb[:].rearrange("p g i -> p (g i)"), in_=invtab.partition_broadcast(128)),
              "c6", writes=["inv"])
        w_in_v = w_in.rearrange("(k p) n -> p k n", p=128)

        def load_w(i, col0):
            b = i % 2
            P.dma("pool", lambda e: e.dma_start(out=wbuf[b][:, :, :], in_=w_in_v[:, :, col0:col0 + 512]),
                  "wbuf%d" % b, writes=["wbuf%d" % b])

        order = []
        for gi in range(4):
            order.append(gi * 512)
            order.append(D + gi * 512)
        load_w(0, order[0])
        for g in range(4):
            P.dma("pool", lambda e, g=g: e.dma_start(out=wg_sb[:, g, :, :], in_=wg_v[:, g, :, :]), "wg", writes=["wg"])
        load_w(1, order[1])

        for gi in range(4):
            wv = 2 * gi
            wgt = 2 * gi + 1
            w = 2 ** (gi + 1)
            for ft in range(4):
                b = wv % 2
                regs = [(ps[0], 0, HALO, 512), (ps[1], 0, HALO + 512, 512), (ps[2], 0, 0, HALO)]
                for (pt, pc, tc0, n) in regs:
                    for kc in range(16):
                        P.op("pe", lambda e, pt=pt, pc=pc, tc0=tc0, n=n, kc=kc, b=b, ft=ft: e.matmul(
                            pt[:, pc:pc + n], lhsT=wbuf[b][:, kc, ft * 128:(ft + 1) * 128],
                            rhs=xT_sb[:, kc, tc0:tc0 + n], start=(kc == 0), stop=(kc == 15)),
                            reads=["wbuf%d" % b, "xT"], writes=[PK(pt)])
                P.op("act", lambda e: e.activation(out=vT[:, HALO:HALO + 512], in_=ps[0][:, :], func=AF.Copy),
                     reads=["ps0"], writes=["vT"])
                P.op("act", lambda e: e.activation(out=vT[:, HALO + 512:NTOK], in_=ps[1][:, :], func=AF.Copy),
                     reads=["ps1"], writes=["vT"])
                P.op("act", lambda e: e.activation(out=vT[:, 0:HALO], in_=ps[2][:, 0:HALO], func=AF.Copy),
                     reads=["ps2"], writes=["vT"])
                src, skey = vT, "vT"
                dsts = [(sA, "sA"), (sB, "sB")]
                sh = 1
                lo = 0
                for st in range(gi + 1):
                    dst, dkey = dsts[st % 2]
                    lo = lo + sh
                    P.op("dve" if st % 2 == 0 else "pool",
                         lambda e, dst=dst, src=src, lo=lo, sh=sh: e.tensor_tensor(
                             out=dst[:, lo:NTOK], in0=src[:, lo:NTOK], in1=src[:, lo - sh:NTOK - sh], op=ALU.add),
                         reads=[skey], writes=[dkey])
                    src, skey = dst, dkey
                    sh *= 2
                P.op("dve", lambda e, src=src, w=w, ft=ft: e.scalar_tensor_tensor(
                    out=pooled[:, ft, HALO:TPC], in0=src[:, 2 * HALO:NTOK], scalar=1.0 / w,
                    in1=vT[:, 2 * HALO:NTOK], op0=ALU.mult, op1=ALU.subtract),
                    reads=[skey, "vT"], writes=["pooled%d" % ft])
                P.op("pool", lambda e, src=src, gi=gi: e.tensor_tensor(
                    out=src[:, HALO:2 * HALO], in0=src[:, HALO:2 * HALO], in1=inv_sb[:, gi, :], op=ALU.mult),
                    reads=[skey, "inv"], writes=[skey])
                P.op("pool", lambda e, src=src, ft=ft: e.tensor_tensor(
                    out=pooled[:, ft, 0:HALO], in0=src[:, HALO:2 * HALO], in1=vT[:, HALO:2 * HALO], op=ALU.subtract),
                    reads=[skey, "vT"], writes=["pooled%d" % ft])
            if gi < 3:
                load_w(wv + 2, order[wv + 2])
            for dti in range(4):
                dt_ = gi * 4 + dti
                b = wgt % 2
                for hf in range(2):
                    pt = ps[3 + hf]
                    for kc in range(16):
                        P.op("pe", lambda e, pt=pt, hf=hf, kc=kc, b=b, dti=dti: e.matmul(
                            pt[:, :], lhsT=wbuf[b][:, kc, dti * 128:(dti + 1) * 128],
                            rhs=xT_sb[:, kc, HALO + hf * 512:HALO + (hf + 1) * 512],
                            start=(kc == 0), stop=(kc == 15)),
                            reads=["wbuf%d" % b, "xT"], writes=[PK(pt)])
                for hf in range(2):
                    pt = ps[5 + hf]
                    for cc in range(4):
                        P.op("pe", lambda e, pt=pt, hf=hf, cc=cc, gi=gi, dti=dti: e.matmul(
                            pt[:, :], lhsT=wg_sb[:, gi, cc, dti * 128:(dti + 1) * 128],
                            rhs=pooled[:, cc, hf * 512:(hf + 1) * 512], start=(cc == 0), stop=(cc == 3)),
                            reads=["wg", "pooled%d" % cc], writes=[PK(pt)])
                for hf in range(2):
                    sl = slice(hf * 512, (hf + 1) * 512)
                    P.op("act", lambda e, hf=hf, sl=sl: e.activation(out=ge[:, sl], in_=ps[3 + hf][:, :], func=AF.Exp, scale=-1.0),
                         reads=["ps%d" % (3 + hf)], writes=["ge%d" % hf])
                    P.op("pool", lambda e, sl=sl: e.tensor_scalar(out=ge[:, sl], in0=ge[:, sl], scalar1=1.0, scalar2=None, op0=ALU.add),
                         reads=["ge%d" % hf], writes=["ge%d" % hf])
                    P.op("dve", lambda e, sl=sl: e.reciprocal(out=ge[:, sl], in_=ge[:, sl]),
                         reads=["ge%d" % hf], writes=["ge%d" % hf])
                    P.op("dve", lambda e, hf=hf, sl=sl: e.tensor_tensor(out=gs[:, sl], in0=ge[:, sl], in1=ps[3 + hf][:, :], op=ALU.mult),
                         reads=["ge%d" % hf, "ps%d" % (3 + hf)], writes=["gs%d" % hf])
                    P.op("act", lambda e, hf=hf, sl=sl, dt_=dt_: e.activation(
                        out=mt[:, sl], in_=ps[5 + hf][:, :], func=AF.Identity,
                        bias=bg_sb[:, dt_:dt_ + 1], scale=1.0),
                        reads=["ps%d" % (5 + hf), "bg"], writes=["mt%d" % hf])
                    P.op("dve", lambda e, sl=sl, dt_=dt_: e.scalar_tensor_tensor(
                        out=yT[:, dt_, sl], in0=mt[:, sl], scalar=os_sb[:, dt_:dt_ + 1], in1=gs[:, sl],
                        op0=ALU.mult, op1=ALU.mult),
                        reads=["mt%d" % hf, "gs%d" % hf, "os"], writes=["yT"])
            if gi < 3:
                load_w(wgt + 2, order[wgt + 2])
    else:
        ssq_sb = nc.alloc_sbuf_tensor("ssq_sb", [8, TPC], F32)
        sel_sb = nc.alloc_sbuf_tensor("sel_sb", [8, 256], F32)
        rs_sb = nc.alloc_sbuf_tensor("rs_sb", [128, 2, TPC], F32)
        yT_v = yT_d.rearrange("(k p) n -> p k n", p=128)
        for h in range(4):
            P.dma("sp", lambda e, h=h: e.dma_start(out=yT[:, 4 * h:4 * h + 4, :], in_=yT_v[:, 4 * h:4 * h + 4, :]),
                  "yT", writes=["yT"])
        P.dma("sp", lambda e: e.dma_start(out=ssq_sb[:], in_=ssq8), "c7", writes=["ssq"])
        P.dma("sp", lambda e: e.dma_start(out=sel_sb[:], in_=sel_d), "c8", writes=["sel"])
        for g in range(2):
            for hf in range(2):
                pt = ps[2 * g + hf]
                P.op("pe", lambda e, pt=pt, g=g, hf=hf: e.matmul(
                    pt[:, :], lhsT=sel_sb[:, g * 128:(g + 1) * 128], rhs=ssq_sb[:, hf * 512:(hf + 1) * 512],
                    start=True, stop=True), reads=["sel", "ssq"], writes=[PK(pt)])
                P.op("act", lambda e, pt=pt, g=g, hf=hf: e.activation(
                    out=rs_sb[:, g, hf * 512:(hf + 1) * 512], in_=pt[:, :], func=AF.Ln, bias=EPS, scale=1.0 / 512),
                    reads=[PK(pt)], writes=["rs%d%d" % (g, hf)])
                P.op("act", lambda e, g=g, hf=hf: e.activation(
                    out=rs_sb[:, g, hf * 512:(hf + 1) * 512], in_=rs_sb[:, g, hf * 512:(hf + 1) * 512],
                    func=AF.Exp, scale=-0.5),
                    reads=["rs%d%d" % (g, hf)], writes=["rs%d%d" % (g, hf)])
            for kc in range(4):
                k = 8 + 4 * g + kc
                P.op("dve" if kc % 2 == 0 else "pool", lambda e, k=k, g=g: e.tensor_tensor(
                    out=yT[:, k, :], in0=yT[:, k, :], in1=rs_sb[:, g, :], op=ALU.mult),
                    reads=["yT", "rs%d0" % g, "rs%d1" % g], writes=["yT"])

    w_out_v = w_out.rearrange("(k p) n -> p k n", p=128)
    fkeys = ["xT", "wbuf0", "wbuf1"] if front else []
    for h in range(4):
        P.dma("pool", lambda e, h=h: e.dma_start(out=wout_sb[:, 4 * h:4 * h + 4, :], in_=w_out_v[:, 4 * h:4 * h + 4, :]),
              "wout", writes=["wout"] + fkeys)
    for tt in range(NT):
        P.dma("sp", lambda e, tt=tt: e.dma_start(out=xr_sb[:], in_=xres[tt * 128:(tt + 1) * 128, :]), "xr", writes=["xr"])
        for nb in range(4):
            pt = ps[4 * (tt % 2) + nb]
            for kc in range(16):
                P.op("pe", lambda e, pt=pt, kc=kc, tt=tt, nb=nb: e.matmul(
                    pt[:, :], lhsT=yT[:, kc, tt * 128:(tt + 1) * 128], rhs=wout_sb[:, kc, nb * 512:(nb + 1) * 512],
                    start=(kc == 0), stop=(kc == 15)), reads=["yT", "wout"], writes=[PK(pt)])
            P.op("dve", lambda e, pt=pt, nb=nb: e.scalar_tensor_tensor(
                out=z_sb[:, nb * 512:(nb + 1) * 512], in0=xr_sb[:, nb * 512:(nb + 1) * 512], scalar=ALPHA,
                in1=pt[:, :], op0=ALU.mult, op1=ALU.add),
                reads=["xr", PK(pt)], writes=["z%d" % nb])
            P.op("dve", lambda e, nb=nb: e.bn_stats(out=stats[:, nb, :], in_=z_sb[:, nb * 512:(nb + 1) * 512]),
                 reads=["z%d" % nb], writes=["stats%d" % nb])
        zk = ["z%d" % i for i in range(4)]
        P.op("dve", lambda e: e.bn_aggr(out=mv[:], in_=stats[:].rearrange("p a b -> p (a b)")),
             reads=["stats%d" % i for i in range(4)], writes=["mv"])
        P.op("act", lambda e: e.activation(out=rstd[:], in_=mv[:, 1:2], func=AF.Ln, bias=EPS, scale=1.0),
             reads=["mv"], writes=["rstd"])
        P.op("act", lambda e: e.activation(out=rstd[:], in_=rstd[:], func=AF.Exp, scale=-0.5),
             reads=["rstd"], writes=["rstd"])
        P.op("dve", lambda e: e.scalar_tensor_tensor(out=nmr[:], in0=mv[:, 0:1], scalar=-1.0, in1=rstd[:],
                                                     op0=ALU.mult, op1=ALU.mult),
             reads=["mv", "rstd"], writes=["nmr"])
        P.op("act", lambda e: e.activation(out=z_sb[:], in_=z_sb[:], func=AF.Identity, bias=nmr[:], scale=rstd[:]),
             reads=zk + ["nmr", "rstd"], writes=zk)
        P.op("pool", lambda e: e.tensor_tensor(out=z_sb[:], in0=z_sb[:], in1=lng_sb[:], op=ALU.mult),
             reads=zk + ["lng"], writes=zk)
        P.op("dve", lambda e: e.tensor_tensor(out=z_sb[:], in0=z_sb[:], in1=lnb_sb[:], op=ALU.add),
             reads=zk + ["lnb"], writes=zk)
        P.dma("sp", lambda e, tt=tt: e.dma_start(out=xo[tt * 128:(tt + 1) * 128, :], in_=z_sb[:]), "xo",
              reads=zk, is_output=True)
        for grp in range(4):
            pk = "ps%d" % (4 * ((tt + 1) % 2) + grp)
            pt = ps[4 * ((tt + 1) % 2) + grp]
            for j in range(4):
                kc = grp * 4 + j
                P.op("pe", lambda e, pt=pt, j=j, kc=kc: e.transpose(
                    pt[:, j * 128:(j + 1) * 128], z_sb[:, kc * 128:(kc + 1) * 128], identf[:]),
                    reads=zk + ["identf"], writes=[pk])
            P.op("act", lambda e, pt=pt, grp=grp: e.activation(out=zT_sb[:, grp, :], in_=pt[:, :], func=AF.Copy),
                 reads=[pk], writes=["zT%d" % grp])
            P.dma("act", lambda e, grp=grp, tt=tt: e.dma_start(
                out=xoT_v[:, grp * 4:grp * 4 + 4, tt * 128:(tt + 1) * 128],
                in_=zT_sb[:, grp, :].rearrange("p (k n) -> p k n", k=4)), "xoT%d" % grp,
                reads=["zT%d" % grp], is_output=True)

    P.emit()
    return nc


def build_even(x_bf16, nblk=S // 512):
    nc = bass.Bass("TRN2", target_bir_lowering=False)
    P = Prog(nc)
    NBLK = nblk
    S = nblk * 512
    xT = _din(nc, "xT", [D, S], BF16 if x_bf16 else F32)
    wA = _din(nc, "wA", [D, NPJ])
    cw_d = _din(nc, "cw", [128, 12])
    cb_d = _din(nc, "cb", [128, 3])
    colp_d = _din(nc, "colp", [128, 8])
    dtp_d = _din(nc, "dtp", [2, 2])
    lamv_d = _din(nc, "lamv", [256])
    identf_d = _din(nc, "identf", [128, 128])
    mask2_d = _din(nc, "mask2", [128, 64])
    bdm_d = _din(nc, "bdmask", [128, 128])
    hsel_d = _din(nc, "hsel", [2, 128])
    eye2_d = _din(nc, "eye2", [2, 2])
    yatt = _dout(nc, "yatt", [128, S], BF16)
    yssm = _dout(nc, "yssm", [128, S], BF16)
    ssq = _dout(nc, "ssq", [1, S])

    A = nc.alloc_sbuf_tensor
    W_sb = A("W_sb", [128, 16, NPJ], BF16)
    xb = [A("xb%d" % i, [128, 16, 512], BF16) for i in range(2)]
    KT = A("KT", [128, S], BF16)
    V = A("V", [128, S // 128, 128], BF16)
    QT = A("QT", [128, 512], BF16)
    sg = A("sg", [128, 512], F32)
    sz = A("sz", [128, 512], F32)
    raw = A("raw", [128, 3, 515], F32)
    xsT = A("xsT", [128, 512], F32)
    xsdup = A("xsdup", [128, 1024], BF16)
    BTdup = A("BTdup", [128, 1024], BF16)
    CT = A("CT", [128, 512], BF16)
    tmp = {k: A("t" + k, [128, 512], F32) for k in "ABCDEF"}
    E = [[A("E%d%d" % (i, m), [128, 512], BF16) for m in range(2)] for i in range(2)]
    ysT = A("ysT", [128, 512], F32)
    yatt_sb = A("yatt_sb", [128, 512], BF16)
    yssm_sb = A("yssm_sb", [128, 512], BF16)
    ssq_sb = A("ssq_sb", [1, 512], F32)
    cw = A("cw_sb", [128, 12], F32)
    cb = A("cb_sb", [128, 3], F32)
    colp = A("colp_sb", [128, 8], F32)
    lamv = A("lamv_sb", [128, 256], F32)
    lamt = A("lamt", [128, 128], F32)
    lams = A("lams", [128, 4], F32)
    identb = A("identb", [128, 128], BF16)
    mask2 = A("mask2_sb", [128, 64], F32)
    bdm = A("bdm_sb", [128, 128], F32)
    ones_b = A("ones_b", [128, 128], BF16)
    ones_f = A("ones_f", [128, 128], F32)
    hsel = A("hsel_sb", [2, 128], F32)
    eye2 = A("eye2_sb", [2, 2], F32)
    dtp = A("dtp_sb", [2, 2], F32)
    Acol = A("Acol", [2, 1], F32)
    dt_sb = A("dt_sb", [2, 512], F32)
    a_sb = A("a_sb", [2, 512], F32)
    acs = A("acs", [2, 512], F32)
    dtBD = A("dtBD", [2, 1024], F32)
    acsBD = A("acsBD", [2, 1024], F32)
    lastbc = A("lastbc", [2, 8, 128], F32)
    cols = A("cols", [128, 16], F32)
    negc = A("negc", [128, 8], F32)
    Sst = A("Sst", [128, 128], F32)
    Sbf = A("Sbf", [128, 128], BF16)
    Gm = A("Gm", [128, 64], F32)
    Dm = A("Dm", [128, 64], F32)
    E2 = A("E2", [128, 64], F32)
    M2 = A("M2", [128, 64], BF16)
    R2e = A("R2e", [128, 64], F32)
    XdtBD = A("XdtBD", [128, 128], BF16)
    XdecBD = A("XdecBD", [128, 128], BF16)
    Btok = A("Btok", [128, 128], BF16)
    CD = A("CD", [128, 128], F32)
    y1 = A("y1", [128, 64], F32)
    ps = [nc.alloc_psum_tensor("ps%d" % i, [128, 512], F32) for i in range(8)]
    psS = [ps[0], ps[1]]
    psO = [ps[2], ps[3]]
    psD = [ps[4], ps[5]]
    rG, rR, rX, rcol = ps[6][:, 0:64], ps[6][:, 64:128], ps[6][:, 128:256], ps[6][:, 256:272]
    rB, rC, rS = ps[7][:, 0:128], ps[7][:, 128:256], ps[7][:, 256:384]
    rYo, rYd = ps[7][:, 384:448], ps[7][:, 448:512]

    def ld(q, dst, src, key):
        P.dma(q, lambda e: e.dma_start(out=dst, in_=src), key, writes=[key])

    ld("sp", cw[:], cw_d, "cw")
    ld("sp", cb[:], cb_d, "cb")
    ld("sp", colp[:], colp_d, "colp")
    ld("sp", dtp[:], dtp_d, "dtp")
    ld("sp", lamv[:], lamv_d.partition_broadcast(128), "lamv")
    ld("sp", mask2[:], mask2_d, "mask2")
    ld("sp", bdm[:], bdm_d, "bdm")
    ld("sp", hsel[:], hsel_d, "hsel")
    ld("sp", eye2[:], eye2_d, "eye2")
    ld("pool", identb[:], identf_d, "identb")
    P.op("pool", lambda e: e.memset(ones_b[:], 1.0), writes=["ones_b"])
    P.op("pool", lambda e: e.memset(ones_f[:], 1.0), writes=["ones_f"])
    P.op("pool", lambda e: e.memset(raw[:], 0.0), writes=["raw0", "raw1", "raw2"])
    P.op("pool", lambda e: e.memset(Sst[:], 0.0), writes=["Sst"])
    wA_v = wA.rearrange("(k p) n -> p k n", p=128)
    for h in range(4):
        P.dma("pool", lambda e, h=h: e.dma_start(out=W_sb[:, 4 * h:4 * h + 4, :], in_=wA_v[:, 4 * h:4 * h + 4, :]),
              "W", writes=["W"])
    xT_v = xT.rearrange("(k p) n -> p k n", p=128)

    def load_x(b):
        i = b % 2
        for h in range(2):
            if x_bf16:
                P.dma("sp" if h == 0 else "act", lambda e, h=h: e.dma_start(
                    out=xb[i][:, 8 * h:8 * h + 8, :], in_=xT_v[:, 8 * h:8 * h + 8, b * 512:(b + 1) * 512]),
                    "xb%d" % i, writes=["xb%d" % i])
            else:
                P.dma("pool", lambda e, h=h: e.dma_start(
                    out=xb[i][:, 8 * h:8 * h + 8, :], in_=xT_v[:, 8 * h:8 * h + 8, b * 512:(b + 1) * 512]),
                    "xb%d" % i, writes=["xb%d" % i])

    P.op("dve", lambda e: e.tensor_tensor(out=lamt[:, 0:64], in0=lamv[:, 0:64], in1=lamv[:, 64:128], op=ALU.mult),
         reads=["lamv"], writes=["lamt0"])
    P.op("dve", lambda e: e.tensor_tensor(out=lamt[:, 64:128], in0=lamv[:, 128:192], in1=lamv[:, 192:256], op=ALU.mult),
         reads=["lamv"], writes=["lamt1"])
    P.op("dve", lambda e: e.reduce_sum(out=lams[:, 0:1], in_=lamt[:, 0:64], axis=mybir.AxisListType.X),
         reads=["lamt0"], writes=["lams0"])
    P.op("dve", lambda e: e.reduce_sum(out=lams[:, 1:2], in_=lamt[:, 64:128], axis=mybir.AxisListType.X),
         reads=["lamt1"], writes=["lams1"])
    P.op("act", lambda e: e.activation(out=lams[:, 0:2], in_=lams[:, 0:2], func=AF.Exp),
         reads=["lams0", "lams1"], writes=["lams0", "lams1"])
    P.op("dve", lambda e: e.tensor_tensor(out=lams[:, 2:3], in0=lams[:, 1:2], in1=lams[:, 0:1], op=ALU.subtract),
         reads=["lams0", "lams1"], writes=["neglam"])
    P.op("dve", lambda e: e.tensor_tensor(out=lams[:, 2:3], in0=lams[:, 2:3], in1=colp[:, 3:4], op=ALU.subtract),
         reads=["neglam", "colp"], writes=["neglam"])
    P.op("dve", lambda e: e.tensor_tensor(out=lams[:, 3:4], in0=colp[:, 2:3], in1=colp[:, 4:5], op=ALU.mult),
         reads=["colp"], writes=["coef"])
    P.op("act", lambda e: e.activation(out=Acol[:], in_=dtp[:, 1:2], func=AF.Exp), reads=["dtp"], writes=["Acol"])
    P.op("dve", lambda e: e.tensor_scalar(out=Acol[:], in0=Acol[:], scalar1=-1.0, scalar2=None, op0=ALU.mult),
         reads=["Acol"], writes=["Acol"])

    def silu_from(src_ap, src_keys, dst_ap, dst_key, t, tkey):
        P.op("act", lambda e: e.activation(out=t, in_=src_ap, func=AF.Exp, scale=-1.0), reads=src_keys, writes=[tkey])
        P.op("act", lambda e: e.activation(out=t, in_=t, func=AF.Ln, bias=1.0, scale=1.0), reads=[tkey], writes=[tkey])
        P.op("act", lambda e: e.activation(out=t, in_=t, func=AF.Exp, scale=-1.0), reads=[tkey], writes=[tkey])
        P.op("dve", lambda e: e.tensor_tensor(out=dst_ap, in0=t, in1=src_ap, op=ALU.mult),
             reads=[tkey] + list(src_keys), writes=[dst_key])

    def inproj(b):
        xbi = xb[b % 2]
        xk = "xb%d" % (b % 2)
        bank = [0]

        def nextbank():
            bank[0] ^= 1
            return ps[bank[0]], "ps%d" % bank[0]

        def proj(c0):
            pt, pk = nextbank()
            for kc in range(16):
                P.op("pe", lambda e, kc=kc: e.matmul(pt[:, :], lhsT=W_sb[:, kc, c0:c0 + 128], rhs=xbi[:, kc, :],
                                                     start=(kc == 0), stop=(kc == 15)),
                     reads=["W", xk], writes=[pk])
            return pt, pk

        pt, pk = proj(0)
        P.op("act", lambda e, pt=pt: e.activation(out=QT[:], in_=pt[:, :], func=AF.Copy), reads=[pk], writes=["QT"])
        pt, pk = proj(128)
        P.op("act", lambda e, pt=pt: e.activation(out=KT[:, b * 512:(b + 1) * 512], in_=pt[:, :], func=AF.Copy),
             reads=[pk], writes=["KT%d" % b])
        pt, pk = nextbank()
        for i in range(4):
            for kc in range(16):
                P.op("pe", lambda e, pt=pt, i=i, kc=kc: e.matmul(
                    pt[:, i * 128:(i + 1) * 128], lhsT=xbi[:, kc, i * 128:(i + 1) * 128], rhs=W_sb[:, kc, 256:384],
                    start=(kc == 0), stop=(kc == 15)), reads=["W", xk], writes=[pk])
        P.op("act", lambda e, pt=pt: e.activation(
            out=V[:, b * 4:(b + 1) * 4, :].rearrange("p a d -> p (a d)"), in_=pt[:, :], func=AF.Copy),
            reads=[pk], writes=["V%d" % b])
        pt, pk = proj(384)
        silu_from(pt[:, :], [pk], sg[:], "sg", tmp["A"][:], "tA")
        pt, pk = proj(512)
        silu_from(pt[:, :], [pk], sz[:], "sz", tmp["B"][:], "tB")
        for t in range(3):
            pt, pk = proj(640 + 128 * t)
            rk = "raw%d" % t
            P.op("act", lambda e, pt=pt, t=t: e.activation(out=raw[:, t, 3:515], in_=pt[:, :], func=AF.Copy),
                 reads=[pk], writes=[rk])
            acc = tmp["C"][:]
            P.op("dve", lambda e, t=t: e.tensor_scalar(
                out=acc, in0=raw[:, t, 3:515], scalar1=cw[:, 4 * t + 3:4 * t + 4], scalar2=cb[:, t:t + 1],
                op0=ALU.mult, op1=ALU.add), reads=[rk, "cw", "cb"], writes=["tC"])
            for j in range(3):
                P.op("dve", lambda e, t=t, j=j: e.scalar_tensor_tensor(
                    out=acc, in0=raw[:, t, j:j + 512], scalar=cw[:, 4 * t + j:4 * t + j + 1], in1=acc,
                    op0=ALU.mult, op1=ALU.add), reads=[rk, "cw", "tC"], writes=["tC"])
            P.op("dve", lambda e, t=t: e.tensor_copy(out=raw[:, t, 0:3], in_=raw[:, t, 512:515]),
                 reads=[rk, "tC"], writes=[rk])
            if t == 0:
                silu_from(acc, ["tC"], xsT[:], "xsT", tmp["D"][:], "tD")
                for u in range(2):
                    P.op("dve", lambda e, u=u: e.tensor_copy(
                        out=xsdup[:].rearrange("p (c u l) -> p c u l", c=8, u=2)[:, :, u, :],
                        in_=xsT[:].rearrange("p (c l) -> p c l", c=8)), reads=["xsT"], writes=["xsdup"])
            elif t == 1:
                silu_from(acc, ["tC"], tmp["E"][:], "tE", tmp["D"][:], "tD")
                for u in range(2):
                    P.op("dve", lambda e, u=u: e.tensor_copy(
                        out=BTdup[:].rearrange("p (c u l) -> p c u l", c=8, u=2)[:, :, u, :],
                        in_=tmp["E"][:].rearrange("p (c l) -> p c l", c=8)), reads=["tE"], writes=["BTdup"])
            else:
                silu_from(acc, ["tC"], CT[:], "CT", tmp["D"][:], "tD")
        pt, pk = nextbank()
        for kc in range(16):
            P.op("pe", lambda e, pt=pt, kc=kc: e.matmul(pt[0:2, :], lhsT=W_sb[:, kc, 1024:1026], rhs=xbi[:, kc, :],
                                                        start=(kc == 0), stop=(kc == 15)),
                 reads=["W", xk], writes=[pk])
        P.op("act", lambda e, pt=pt: e.activation(out=dt_sb[:], in_=pt[0:2, :], func=AF.Exp, bias=dtp[:, 0:1], scale=1.0),
             reads=[pk, "dtp"], writes=["dt"])
        P.op("act", lambda e: e.activation(out=dt_sb[:], in_=dt_sb[:], func=AF.Ln, bias=1.0, scale=1.0),
             reads=["dt"], writes=["dt"])
        P.op("dve", lambda e: e.tensor_scalar(out=a_sb[:], in0=dt_sb[:], scalar1=Acol[:, 0:1], scalar2=None, op0=ALU.mult),
             reads=["dt", "Acol"], writes=["a"])
        for c in range(8):
            P.op("dve", lambda e, c=c: e.tensor_tensor_scan(
                out=acs[:, c * 64:(c + 1) * 64], data0=ones_f[0:2, 0:64], data1=a_sb[:, c * 64:(c + 1) * 64],
                initial=0.0, op0=ALU.mult, op1=ALU.add), reads=["a", "ones_f"], writes=["acs"])
        for h in range(2):
            P.op("dve", lambda e, h=h: e.tensor_scalar(
                out=dtBD[:].rearrange("p (c u l) -> p c u l", c=8, u=2)[:, :, h, :],
                in0=dt_sb[:].rearrange("p (c l) -> p c l", c=8), scalar1=eye2[:, h:h + 1], scalar2=None, op0=ALU.mult),
                reads=["dt", "eye2"], writes=["dtBD"])
            P.op("dve", lambda e, h=h: e.tensor_scalar(
                out=acsBD[:].rearrange("p (c u l) -> p c u l", c=8, u=2)[:, :, h, :],
                in0=acs[:].rearrange("p (c l) -> p c l", c=8), scalar1=eye2[:, h:h + 1], scalar2=None, op0=ALU.mult),
                reads=["acs", "eye2"], writes=["acsBD"])
        for c in range(8):
            P.op("dve", lambda e, c=c: e.tensor_scalar(
                out=lastbc[:, c, :], in0=ones_f[0:2, :], scalar1=acs[:, c * 64 + 63:c * 64 + 64], scalar2=None, op0=ALU.mult),
                reads=["acs", "ones_f"], writes=["lastbc"])

    def attention_steps(b):
        nkb = 4 * b + 4

        def scores(kb):
            j = kb - 4 * b
            c0 = max(0, j) * 128
            st = kb % 2
            for m in range(2):
                P.op("pe", lambda e, kb=kb, m=m, c0=c0: e.matmul(
                    psS[m][:, c0:512], lhsT=KT[m * 64:(m + 1) * 64, kb * 128:(kb + 1) * 128],
                    rhs=QT[m * 64:(m + 1) * 64, c0:512], start=True, stop=True),
                    reads=["KT%d" % (kb // 4), "QT"], writes=["ps%d" % m])
                P.op("act", lambda e, m=m, c0=c0, st=st: e.activation(
                    out=E[st][m][:, c0:512], in_=psS[m][:, c0:512], func=AF.Exp, scale=0.125),
                    reads=["ps%d" % m], writes=["E%d%d" % (st, m)])
                if j >= 0:
                    P.op("pool", lambda e, m=m, c0=c0, st=st: e.memset(E[st][m][64:128, c0:c0 + 64], 0.0),
                         reads=[], writes=["E%d%d" % (st, m)])

        def accum(kb):
            j = kb - 4 * b
            c0 = max(0, j) * 128
            st = kb % 2
            for m in range(2):
                P.op("pe", lambda e, kb=kb, m=m, c0=c0, st=st: e.matmul(
                    psO[m][:, c0:512], lhsT=V[:, kb, :], rhs=E[st][m][:, c0:512],
                    start=(kb == 0), stop=(kb == nkb - 1)),
                    reads=["V%d" % (kb // 4), "E%d%d" % (st, m)], writes=["ps%d" % (2 + m)])
                P.op("pe", lambda e, kb=kb, m=m, c0=c0, st=st: e.matmul(
                    psD[m][:, c0:512], lhsT=ones_b[:], rhs=E[st][m][:, c0:512],
                    start=(kb == 0), stop=(kb == nkb - 1)),
                    reads=["ones_b", "E%d%d" % (st, m)], writes=["ps%d" % (4 + m)])

        scores(0)
        for kb in range(nkb):
            if kb + 1 < nkb:
                scores(kb + 1)
            accum(kb)
            yield
        tA, tB, tC, tD = tmp["A"][:], tmp["B"][:], tmp["C"][:], tmp["D"][:]
        P.op("act", lambda e: e.activation(out=tA, in_=psD[0][:, :], func=AF.Ln), reads=["ps4"], writes=["tA"])
        P.op("act", lambda e: e.activation(out=tB, in_=psD[1][:, :], func=AF.Ln), reads=["ps5"], writes=["tB"])
        P.op("act", lambda e: e.activation(out=tA, in_=tA, func=AF.Exp, scale=-1.0), reads=["tA"], writes=["tA"])
        P.op("act", lambda e: e.activation(out=tB, in_=tB, func=AF.Exp, scale=-1.0), reads=["tB"], writes=["tB"])
        P.op("dve", lambda e: e.tensor_tensor(out=tA, in0=tA, in1=psO[0][:, :], op=ALU.mult), reads=["tA", "ps2"], writes=["tA"])
        P.op("dve", lambda e: e.tensor_tensor(out=tB, in0=tB, in1=psO[1][:, :], op=ALU.mult), reads=["tB", "ps3"], writes=["tB"])
        P.op("dve", lambda e: e.scalar_tensor_tensor(out=tA, in0=tB, scalar=lams[:, 2:3], in1=tA, op0=ALU.mult, op1=ALU.add),
             reads=["tA", "tB", "neglam"], writes=["tA"])
        P.op("act", lambda e: e.activation(out=tB, in_=tA, func=AF.Square), reads=["tA"], writes=["tB"])
        P.op("pe", lambda e: e.matmul(psS[0][:, :], lhsT=ones_f[:], rhs=tB, start=True, stop=True),
             reads=["ones_f", "tB"], writes=["ps0"])
        P.op("act", lambda e: e.activation(out=tB, in_=psS[0][:, :], func=AF.Ln, bias=EPS, scale=1.0 / 128),
             reads=["ps0"], writes=["tB"])
        P.op("act", lambda e: e.activation(out=tB, in_=tB, func=AF.Exp, scale=-0.5), reads=["tB"], writes=["tB"])
        P.op("dve", lambda e: e.tensor_tensor(out=tA, in0=tA, in1=tB, op=ALU.mult), reads=["tA", "tB"], writes=["tA"])
        P.op("dve", lambda e: e.scalar_tensor_tensor(out=yatt_sb[:], in0=tA, scalar=lams[:, 3:4], in1=sg[:],
                                                     op0=ALU.mult, op1=ALU.mult),
             reads=["tA", "coef", "sg"], writes=["yatt_sb"])
        P.dma("sp", lambda e: e.dma_start(out=yatt[:, b * 512:(b + 1) * 512], in_=yatt_sb[:]), "yatt",
              reads=["yatt_sb"], is_output=True)
        yield

    def ssd_steps(b):
        for c in range(8):
            P.op("pe", lambda e, c=c: e.matmul(rcol[:, 2 * c:2 * c + 1], lhsT=dtBD[:, c * 128:(c + 1) * 128],
                                               rhs=ones_f[0:2, 0:1], start=True, stop=True),
                 reads=["dtBD", "ones_f"], writes=["ps6"])
            P.op("pe", lambda e, c=c: e.matmul(rcol[:, 2 * c + 1:2 * c + 2], lhsT=acsBD[:, c * 128:(c + 1) * 128],
                                               rhs=ones_f[0:2, 0:1], start=True, stop=True),
                 reads=["acsBD", "ones_f"], writes=["ps6"])
        P.op("act", lambda e: e.activation(out=cols[:], in_=rcol, func=AF.Copy), reads=["ps6"], writes=["cols"])
        P.op("dve", lambda e: e.tensor_scalar(
            out=negc[:], in0=cols[:].rearrange("p (c t) -> p c t", t=2)[:, :, 1], scalar1=-1.0, scalar2=None, op0=ALU.mult),
            reads=["cols"], writes=["negc"])
        for c in range(8):
            cs = slice(c * 64, (c + 1) * 64)
            cd = slice(c * 128, (c + 1) * 128)
            P.op("pe", lambda e, cs=cs, cd=cd: e.matmul(rG, lhsT=BTdup[:, cd], rhs=CT[:, cs], start=True, stop=True),
                 reads=["BTdup", "CT"], writes=["ps6"])
            P.op("pe", lambda e, cs=cs: e.matmul(rR, lhsT=hsel[:], rhs=acs[:, cs], start=True, stop=True),
                 reads=["hsel", "acs"], writes=["ps6"])
            P.op("pe", lambda e, cd=cd: e.matmul(rX, lhsT=xsdup[:, cd], rhs=identb[:], start=True, stop=True),
                 reads=["xsdup", "identb"], writes=["ps6"])
            P.op("pe", lambda e, c=c: e.matmul(rC, lhsT=lastbc[:, c, :], rhs=hsel[:], start=True, stop=True),
                 reads=["lastbc", "hsel"], writes=["ps7"])
            P.op("pe", lambda e, cd=cd: e.matmul(rB, lhsT=BTdup[:, cd], rhs=identb[:], start=True, stop=True),
                 reads=["BTdup", "identb"], writes=["ps7"])
            P.op("dve", lambda e: e.tensor_tensor(out=Gm[:], in0=rG, in1=mask2[:], op=ALU.mult),
                 reads=["ps6", "mask2"], writes=["Gm"])
            P.op("dve", lambda e, c=c: e.tensor_scalar(out=Dm[:], in0=rR, scalar1=negc[:, c:c + 1], scalar2=0.0,
                                                       op0=ALU.add, op1=ALU.min),
                 reads=["ps6", "negc"], writes=["Dm"])
            P.op("act", lambda e: e.activation(out=E2[:], in_=Dm[:], func=AF.Exp), reads=["Dm"], writes=["E2"])
            P.op("act", lambda e: e.activation(out=R2e[:], in_=rR, func=AF.Exp), reads=["ps6"], writes=["R2e"])
            P.op("dve", lambda e, c=c: e.scalar_tensor_tensor(
                out=XdtBD[:], in0=rX, scalar=cols[:, 2 * c:2 * c + 1], in1=bdm[:], op0=ALU.mult, op1=ALU.mult),
                reads=["ps6", "cols", "bdm"], writes=["XdtBD"])
            P.op("dve", lambda e: e.tensor_tensor(out=M2[:], in0=Gm[:], in1=E2[:], op=ALU.mult),
                 reads=["Gm", "E2"], writes=["M2"])
            P.op("act", lambda e: e.activation(out=XdecBD[:], in_=XdtBD[:], func=AF.Identity, scale=E2[:, 63:64]),
                 reads=["XdtBD", "E2"], writes=["XdecBD"])
            P.op("act", lambda e: e.activation(out=Btok[:], in_=rB, func=AF.Copy), reads=["ps7"], writes=["Btok"])
            P.op("act", lambda e: e.activation(out=CD[:], in_=rC, func=AF.Exp), reads=["ps7"], writes=["CD"])
            P.op("act", lambda e: e.activation(out=Sbf[:], in_=Sst[:], func=AF.Copy), reads=["Sst"], writes=["Sbf"])
            P.op("pe", lambda e, cs=cs: e.matmul(rYo, lhsT=Sbf[:], rhs=CT[:, cs], start=True, stop=True),
                 reads=["Sbf", "CT"], writes=["ps7"])
            P.op("pe", lambda e: e.matmul(rYd, lhsT=XdtBD[:], rhs=M2[:], start=True, stop=True),
                 reads=["XdtBD", "M2"], writes=["ps7"])
            P.op("pe", lambda e: e.matmul(rS, lhsT=Btok[:], rhs=XdecBD[:], start=True, stop=True),
                 reads=["Btok", "XdecBD"], writes=["ps7"])
            P.op("dve", lambda e: e.tensor_tensor(out=y1[:], in0=rYo, in1=R2e[:], op=ALU.mult),
                 reads=["ps7", "R2e"], writes=["y1"])
            P.op("dve", lambda e, cs=cs: e.tensor_tensor(out=ysT[:, cs], in0=y1[:], in1=rYd, op=ALU.add),
                 reads=["y1", "ps7"], writes=["ysT"])
            P.op("dve", lambda e: e.tensor_tensor(out=Sst[:], in0=Sst[:], in1=CD[:], op=ALU.mult),
                 reads=["Sst", "CD"], writes=["Sst"])
            P.op("dve", lambda e: e.tensor_tensor(out=Sst[:], in0=Sst[:], in1=rS, op=ALU.add),
                 reads=["Sst", "ps7"], writes=["Sst"])
            yield
        tE, tF = tmp["E"][:], tmp["F"][:]
        P.op("dve", lambda e: e.scalar_tensor_tensor(out=tE, in0=xsT[:], scalar=colp[:, 0:1], in1=ysT[:],
                                                     op0=ALU.mult, op1=ALU.add),
             reads=["xsT", "colp", "ysT"], writes=["tE"])
        P.op("dve", lambda e: e.tensor_tensor(out=tE, in0=tE, in1=sz[:], op=ALU.mult), reads=["tE", "sz"], writes=["tE"])
        P.op("act", lambda e: e.activation(out=tF, in_=tE, func=AF.Square), reads=["tE"], writes=["tF"])
        P.op("pe", lambda e: e.matmul(ps[6][0:1, :], lhsT=ones_f[:, 0:1], rhs=tF, start=True, stop=True),
             reads=["ones_f", "tF"], writes=["ps6"])
        P.op("act", lambda e: e.activation(out=ssq_sb[:], in_=ps[6][0:1, :], func=AF.Copy), reads=["ps6"], writes=["ssq_sb"])
        P.op("act", lambda e: e.activation(out=yssm_sb[:], in_=tE, func=AF.Identity, scale=colp[:, 1:2]),
             reads=["tE", "colp"], writes=["yssm_sb"])
        P.dma("sp", lambda e: e.dma_start(out=ssq[0:1, b * 512:(b + 1) * 512], in_=ssq_sb[:]), "ssqo",
              reads=["ssq_sb"], is_output=True)
        P.dma("sp", lambda e: e.dma_start(out=yssm[:, b * 512:(b + 1) * 512], in_=yssm_sb[:]), "yssm",
              reads=["yssm_sb"], is_output=True)
        yield

    def interleave(b):
        att = attention_steps(b)
        sd = ssd_steps(b)
        nk = 4 * b + 5
        ns = 9
        ia = isd = 0
        for _ in att:
            pass
        for _ in sd:
            pass
        return
        while ia < nk or isd < ns:
            if isd >= ns or (ia < nk and ia * ns <= isd * nk):
                next(att, None)
                ia += 1
            else:
                next(sd, None)
                isd += 1

    load_x(0)
    for b in range(NBLK):
        if b + 1 < NBLK:
            load_x(b + 1)
        inproj(b)
        interleave(b)
    P.emit()
    return nc


_NC = {}
BF = ml_dtypes.bfloat16


def _get(name, builder):
    if name not in _NC:
        _NC[name] = builder()
    return _NC[name]


def _launch(nc, in_maps):
    res = run_bass_kernel_spmd(nc, in_maps, core_ids=list(range(len(in_maps))))
    return res.results


_IDENTF = np.eye(128, dtype=np.float32)


def run_odd(xtok, xT_bf, w_in, w_grp, b_grp, scale, w_out, lng, lnb, cores=range(NCORES)):
    nc = _get("odd", lambda: build_tail(True))
    bg = np.ascontiguousarray(b_grp.reshape(16, 128).T)
    osc = np.ascontiguousarray(scale.reshape(16, 128).T)
    in_maps = []
    for c in cores:
        t0 = c * TPC
        xT_c = np.zeros((D, HALO + TPC), BF)
        xT_c[:, HALO:] = xT_bf[:, t0:t0 + TPC]
        if c > 0:
            xT_c[:, :HALO] = xT_bf[:, t0 - HALO:t0]
        inv = np.zeros((4, 16), np.float32)
        for gi in range(4):
            for i in range(16):
                inv[gi, i] = 1.0 / min(t0 + i + 1, 2 ** (gi + 1))
        in_maps.append(dict(xres=np.ascontiguousarray(xtok[t0:t0 + TPC]), w_out=w_out, lng=lng, lnb=lnb,
                            identf=_IDENTF, xT=xT_c, w_in=w_in, w_grp=w_grp, bgrp=bg, oscale=osc,
                            invtab=inv.reshape(-1)))
    res = _launch(nc, in_maps)
    xo = np.concatenate([r["xo"] for r in res], 0)
    xoT = np.concatenate([r["xoT"] for r in res], 1)
    return xo, xoT


def _even_consts():
    mask2 = np.zeros((128, 64), np.float32)
    for u in range(2):
        for s_ in range(64):
            mask2[u * 64 + s_, s_:] = 1.0
    bdm = np.zeros((128, 128), np.float32)
    bdm[:64, :64] = 1.0
    bdm[64:, 64:] = 1.0
    hsel = np.zeros((2, 128), np.float32)
    hsel[0, :64] = 1.0
    hsel[1, 64:] = 1.0
    return dict(identf=_IDENTF, mask2=mask2, bdmask=bdm, hsel=hsel, eye2=np.eye(2, dtype=np.float32))


def run_even(xT, w_in, conv_w, conv_b, dt_bias, a_log, d_skip, ssm_norm_g, lq1, lk1, lq2, lk2, subln_g,
             lam_init, cores=range(NCORES), nblk=S // 512):
    x_bf16 = xT.dtype != np.float32
    nc = _get(("even", x_bf16, nblk), lambda: build_even(x_bf16, nblk))
    consts = _even_consts()
    ar = np.arange(128)
    in_maps = []
    for c in cores:
        g = c // 4
        idx = np.concatenate([c * 128 + ar, 1024 + c * 128 + ar, 2048 + c * 128 + ar, 3072 + c * 128 + ar,
                              4096 + c * 128 + ar, 5120 + c * 128 + ar, 6144 + g * 128 + ar, 6400 + g * 128 + ar,
                              6656 + 2 * c + np.arange(2)])
        wA = np.ascontiguousarray(w_in[:, idx])
        chans = [c * 128 + ar, 1024 + g * 128 + ar, 1280 + g * 128 + ar]
        cw = np.zeros((128, 12), np.float32)
        cb = np.zeros((128, 3), np.float32)
        for t in range(3):
            cw[:, 4 * t:4 * t + 4] = conv_w[:, chans[t]].T
            cb[:, t] = conv_b[chans[t]]
        colp = np.zeros((128, 8), np.float32)
        colp[:, 0] = np.repeat(d_skip[2 * c:2 * c + 2], 64)
        colp[:, 1] = ssm_norm_g[c * 128:(c + 1) * 128]
        colp[:, 2] = subln_g
        colp[:, 3] = lam_init
        colp[:, 4] = 1.0 - lam_init
        dtp = np.stack([dt_bias[2 * c:2 * c + 2], a_log[2 * c:2 * c + 2]], 1).astype(np.float32)
        lamv = np.concatenate([lq1, lk1, lq2, lk2]).astype(np.float32)
        m = dict(xT=xT, wA=wA, cw=cw, cb=cb, colp=colp, dtp=dtp, lamv=lamv)
        m.update(consts)
        in_maps.append(m)
    res = _launch(nc, in_maps)
    return [r["yatt"] for r in res], [r["yssm"] for r in res], [r["ssq"] for r in res]


def _sel8():
    sel = np.zeros((8, 256), np.float32)
    sel[0:4, 0:128] = 1.0
    sel[4:8, 128:256] = 1.0
    return sel


def run_evtail(xtok, yT, ssq8, w_out, lng, lnb, cores=range(NCORES)):
    nc = _get("evtail", lambda: build_tail(False))
    sel = _sel8()
    in_maps = []
    for c in cores:
        t0 = c * TPC
        in_maps.append(dict(xres=np.ascontiguousarray(xtok[t0:t0 + TPC]), w_out=w_out, lng=lng, lnb=lnb,
                            identf=_IDENTF, yT=np.ascontiguousarray(yT[:, t0:t0 + TPC]),
                            ssq8=np.ascontiguousarray(ssq8[:, t0:t0 + TPC]), sel=sel))
    res = _launch(nc, in_maps)
    xo = np.concatenate([r["xo"] for r in res], 0)
    xoT = np.concatenate([r["xoT"] for r in res], 1)
    return xo, xoT


def kernel(x, ev_w_in, ev_conv_w, ev_conv_b, ev_dt_bias, ev_a_log, ev_d_skip, ev_ssm_norm_g,
           ev_lambda_q1, ev_lambda_k1, ev_lambda_q2, ev_lambda_k2, ev_subln_g, ev_w_out,
           od_w_in, od_w_grp, od_b_grp, od_scale, od_w_out, ln_g, ln_b):
    f = lambda a: np.asarray(a, dtype=np.float32)
    xtok = np.ascontiguousarray(f(x)[0])
    xT = np.ascontiguousarray(xtok.T)
    for l in range(DEPTH):
        i = l // 2
        if l % 2 == 0:
            lam_init = 0.8 - 0.6 * math.exp(-0.3 * l)
            ya, ys, sq = run_even(xT, f(ev_w_in[i]), f(ev_conv_w[i]), f(ev_conv_b[i]), f(ev_dt_bias[i]), f(ev_a_log[i]),
                                  f(ev_d_skip[i]), f(ev_ssm_norm_g[i]), f(ev_lambda_q1[i]), f(ev_lambda_k1[i]),
                                  f(ev_lambda_q2[i]), f(ev_lambda_k2[i]), f(ev_subln_g[i]), lam_init)
            yT = np.concatenate(list(ya) + list(ys), 0)
            ssq8 = np.concatenate(list(sq), 0)
            xtok, xT = run_evtail(xtok, yT, ssq8, f(ev_w_out[i]), f(ln_g[l]), f(ln_b[l]))
        else:
            xtok, xT = run_odd(xtok, xT, f(od_w_in[i]), f(od_w_grp[i]), f(od_b_grp[i]), f(od_scale[i]), f(od_w_out[i]),
                               f(ln_g[l]), f(ln_b[l]))
    return np.ascontiguousarray(xtok[None]).astype(np.float32)
```

```python
import math
import numpy as np
import ml_dtypes
import concourse.bass as bass
import concourse.mybir as mybir
from concourse.bass_utils import run_bass_kernel_spmd

F32 = mybir.dt.float32
BF16 = mybir.dt.bfloat16
AF = mybir.ActivationFunctionType
ALU = mybir.AluOpType

NCORES = 8
D = 2048
S = 8192
TPC = S // NCORES
HALO = 16
DEPTH = 4
ALPHA = (2.0 * DEPTH) ** 0.25
EPS = 1e-5
NPJ = 1026


class Prog:
    ENG = ("pe", "act", "dve", "pool", "sp")

    def __init__(self, nc):
        self.nc = nc
        self.q = {e: [] for e in self.ENG}
        self.sem = {e: nc.alloc_semaphore("sem_" + e) for e in ("pe", "act", "dve", "pool")}
        self.cnt = {e: 0 for e in self.sem}
        self.last_w = {}
        self.reads = {}
        self.waited = {e: {} for e in self.ENG}
        self.dsem = {}
        self.out_tokens = []

    def _deps(self, eng, reads, writes):
        toks = []
        for k in list(reads) + list(writes):
            t = self.last_w.get(k)
            if t is not None:
                toks.append(t)
        for k in writes:
            toks.extend(self.reads.get(k, ()))
        waits = {}
        for (sem, val, src) in toks:
            if src == "pe" and eng == "pe":
                continue
            sid = id(sem)
            if self.waited[eng].get(sid, (None, 0))[1] >= val:
                continue
            if sid not in waits or waits[sid][1] < val:
                waits[sid] = (sem, val)
        for sid, (sem, val) in waits.items():
            self.waited[eng][sid] = (sem, val)
        return list(waits.values())

    def _commit(self, tok, reads, writes):
        for k in reads:
            self.reads.setdefault(k, []).append(tok)
        for k in writes:
            self.last_w[k] = tok
            self.reads[k] = []

    def op(self, eng, fn, reads=(), writes=()):
        waits = self._deps(eng, reads, writes)
        self.cnt[eng] += 1
        tok = (self.sem[eng], self.cnt[eng], eng)
        self.q[eng].append((waits, fn, (self.sem[eng], 1)))
        self._commit(tok, reads, writes)

    def dma(self, queue, fn, semname, reads=(), writes=(), is_output=False):
        if semname not in self.dsem:
            self.dsem[semname] = [self.nc.alloc_semaphore("d_" + semname), 0]
        ent = self.dsem[semname]
        waits = self._deps(queue, reads, writes)
        ent[1] += 16
        tok = (ent[0], ent[1], "dma")
        self.q[queue].append((waits, fn, (ent[0], 16)))
        self._commit(tok, reads, writes)
        if is_output:
            self.out_tokens.append(tok)

    def emit(self):
        nc = self.nc
        fin = {}
        for (sem, val, _) in self.out_tokens:
            if id(sem) not in fin or fin[id(sem)][1] < val:
                fin[id(sem)] = (sem, val)
        q = self.q

        def run(e, name, final=False):
            for waits, fn, (sem, amt) in q[name]:
                for (ws, wv) in waits:
                    e.wait_ge(ws, wv)
                fn(e).then_inc(sem, amt)
            if final:
                for (ws, wv) in fin.values():
                    e.wait_ge(ws, wv)

        with nc.Block() as block:
            @block.sync
            def _(e):
                run(e, "sp", final=True)

            @block.tensor
            def _(e):
                run(e, "pe")

            @block.scalar
            def _(e):
                run(e, "act")

            @block.vector
            def _(e):
                run(e, "dve")

            @block.gpsimd
            def _(e):
                run(e, "pool")


def PK(pt):
    return pt.name


def _din(nc, name, shape, dt=F32):
    return nc.dram_tensor(name, list(shape), dt, kind="ExternalInput").ap()


def _dout(nc, name, shape, dt=F32):
    return nc.dram_tensor(name, list(shape), dt, kind="ExternalOutput").ap()


def build_tail(front):
    nc = bass.Bass("TRN2", target_bir_lowering=False)
    P = Prog(nc)
    NT = TPC // 128
    xres = _din(nc, "xres", [TPC, D])
    w_out = _din(nc, "w_out", [D, D])
    lng = _din(nc, "lng", [D])
    lnb = _din(nc, "lnb", [D])
    xo = _dout(nc, "xo", [TPC, D])
    xoT = _dout(nc, "xoT", [D, TPC], BF16)
    identf_d = _din(nc, "identf", [128, 128])
    if front:
        xT = _din(nc, "xT", [D, HALO + TPC], BF16)
        w_in = _din(nc, "w_in", [D, 2 * D])
        w_grp = _din(nc, "w_grp", [4, 512, 512])
        bgrp = _din(nc, "bgrp", [128, 16])
        oscale = _din(nc, "oscale", [128, 16])
        invtab = _din(nc, "invtab", [4 * 16])
    else:
        yT_d = _din(nc, "yT", [D, TPC], BF16)
        ssq8 = _din(nc, "ssq8", [8, TPC])
        sel_d = _din(nc, "sel", [8, 2 * 128])

    NTOK = HALO + TPC
    XT_E = 16 * NTOK
    WB_E = 16 * 512
    big = nc.alloc_sbuf_tensor("big", [128, max(XT_E + 2 * WB_E, 16 * D)], BF16)
    wout_sb = big[:, 0:16 * D].rearrange("p (k n) -> p k n", k=16)
    yT = nc.alloc_sbuf_tensor("yT_sb", [128, 16, TPC], BF16)
    lng_sb = nc.alloc_sbuf_tensor("lng_sb", [128, D], F32)
    lnb_sb = nc.alloc_sbuf_tensor("lnb_sb", [128, D], F32)
    xr_sb = nc.alloc_sbuf_tensor("xr_sb", [128, D], F32)
    z_sb = nc.alloc_sbuf_tensor("z_sb", [128, D], F32)
    zT_sb = nc.alloc_sbuf_tensor("zT_sb", [128, 4, 512], BF16)
    xoT_v = xoT.rearrange("(k p) n -> p k n", p=128)
    stats = nc.alloc_sbuf_tensor("stats", [128, 4, 6], F32)
    mv = nc.alloc_sbuf_tensor("mv", [128, 2], F32)
    rstd = nc.alloc_sbuf_tensor("rstd", [128, 1], F32)
    nmr = nc.alloc_sbuf_tensor("nmr", [128, 1], F32)
    identf = nc.alloc_sbuf_tensor("identf_sb", [128, 128], F32)
    ps = [nc.alloc_psum_tensor("ps%d" % i, [128, 512], F32) for i in range(8)]

    P.dma("sp", lambda e: e.dma_start(out=lng_sb[:], in_=lng.partition_broadcast(128)), "c1", writes=["lng"])
    P.dma("sp", lambda e: e.dma_start(out=lnb_sb[:], in_=lnb.partition_broadcast(128)), "c2", writes=["lnb"])
    P.dma("sp", lambda e: e.dma_start(out=identf[:], in_=identf_d), "c3", writes=["identf"])

    if front:
        xT_sb = big[:, 0:XT_E].rearrange("p (k n) -> p k n", k=16)
        wbuf = [big[:, XT_E + i * WB_E: XT_E + (i + 1) * WB_E].rearrange("p (k n) -> p k n", k=16) for i in range(2)]
        wg_sb = nc.alloc_sbuf_tensor("wg_sb", [128, 4, 4, 512], BF16)
        bg_sb = nc.alloc_sbuf_tensor("bg_sb", [128, 16], F32)
        os_sb = nc.alloc_sbuf_tensor("os_sb", [128, 16], F32)
        inv_sb = nc.alloc_sbuf_tensor("inv_sb", [128, 4, 16], F32)
        vT = nc.alloc_sbuf_tensor("vT", [128, NTOK], F32)
        sA = nc.alloc_sbuf_tensor("sA", [128, NTOK], F32)
        sB = nc.alloc_sbuf_tensor("sB", [128, NTOK], F32)
        pooled = nc.alloc_sbuf_tensor("pooled", [128, 4, TPC], BF16)
        ge = nc.alloc_sbuf_tensor("ge", [128, TPC], F32)
        gs = nc.alloc_sbuf_tensor("gs", [128, TPC], F32)
        mt = nc.alloc_sbuf_tensor("mt", [128, TPC], F32)

        xT_v = xT.rearrange("(k p) n -> p k n", p=128)
        for h in range(4):
            P.dma("sp" if h % 2 == 0 else "act", lambda e, h=h: e.dma_start(out=xT_sb[:, 4 * h:4 * h + 4, :], in_=xT_v[:, 4 * h:4 * h + 4, :]),
                  "xT", writes=["xT"])
        wg_v = w_grp.rearrange("g (c p) d -> p g c d", p=128)
        P.dma("sp", lambda e: e.dma_start(out=bg_sb[:], in_=bgrp), "c4", writes=["bg"])
        P.dma("sp", lambda e: e.dma_start(out=os_sb[:], in_=oscale), "c5", writes=["os"])
        P.dma("sp", lambda e: e.dma_start(out=inv_sb[:].rearrange("p g i -> p (g i)"), in_=invtab.partition_broadcast(128)),
              "c6", writes=["inv"])
        w_in_v = w_in.rearrange("(k p) n -> p k n", p=128)

        def load_w(i, col0):
            b = i % 2
            P.dma("pool", lambda e: e.dma_start(out=wbuf[b][:, :, :], in_=w_in_v[:, :, col0:col0 + 512]),
                  "wbuf%d" % b, writes=["wbuf%d" % b])

        order = []
        for gi in range(4):
            order.append(gi * 512)
            order.append(D + gi * 512)
        load_w(0, order[0])
        for g in range(4):
            P.dma("pool", lambda e, g=g: e.dma_start(out=wg_sb[:, g, :, :], in_=wg_v[:, g, :, :]), "wg", writes=["wg"])
        load_w(1, order[1])

        for gi in range(4):
            wv = 2 * gi
            wgt = 2 * gi + 1
            w = 2 ** (gi + 1)
            for ft in range(4):
                b = wv % 2
                regs = [(ps[0], 0, HALO, 512), (ps[1], 0, HALO + 512, 512), (ps[2], 0, 0, HALO)]
                for (pt, pc, tc0, n) in regs:
                    for kc in range(16):
                        P.op("pe", lambda e, pt=pt, pc=pc, tc0=tc0, n=n, kc=kc, b=b, ft=ft: e.matmul(
                            pt[:, pc:pc + n], lhsT=wbuf[b][:, kc, ft * 128:(ft + 1) * 128],
                            rhs=xT_sb[:, kc, tc0:tc0 + n], start=(kc == 0), stop=(kc == 15)),
                            reads=["wbuf%d" % b, "xT"], writes=[PK(pt)])
                P.op("act", lambda e: e.activation(out=vT[:, HALO:HALO + 512], in_=ps[0][:, :], func=AF.Copy),
                     reads=["ps0"], writes=["vT"])
                P.op("act", lambda e: e.activation(out=vT[:, HALO + 512:NTOK], in_=ps[1][:, :], func=AF.Copy),
                     reads=["ps1"], writes=["vT"])
                P.op("act", lambda e: e.activation(out=vT[:, 0:HALO], in_=ps[2][:, 0:HALO], func=AF.Copy),
                     reads=["ps2"], writes=["vT"])
                src, skey = vT, "vT"
                dsts = [(sA, "sA"), (sB, "sB")]
                sh = 1
                lo = 0
                for st in range(gi + 1):
                    dst, dkey = dsts[st % 2]
                    lo = lo + sh
                    P.op("dve",
                         lambda e, dst=dst, src=src, lo=lo, sh=sh: e.tensor_tensor(
                             out=dst[:, lo:NTOK], in0=src[:, lo:NTOK], in1=src[:, lo - sh:NTOK - sh], op=ALU.add),
                         reads=[skey], writes=[dkey])
                    src, skey = dst, dkey
                    sh *= 2
                P.op("dve", lambda e, src=src, w=w, ft=ft: e.scalar_tensor_tensor(
                    out=pooled[:, ft, HALO:TPC], in0=src[:, 2 * HALO:NTOK], scalar=1.0 / w,
                    in1=vT[:, 2 * HALO:NTOK], op0=ALU.mult, op1=ALU.subtract),
                    reads=[skey, "vT"], writes=["pooled%d" % ft])
                P.op("dve", lambda e, src=src, gi=gi: e.tensor_tensor(
                    out=src[:, HALO:2 * HALO], in0=src[:, HALO:2 * HALO], in1=inv_sb[:, gi, :], op=ALU.mult),
                    reads=[skey, "inv"], writes=[skey])
                P.op("dve", lambda e, src=src, ft=ft: e.tensor_tensor(
                    out=pooled[:, ft, 0:HALO], in0=src[:, HALO:2 * HALO], in1=vT[:, HALO:2 * HALO], op=ALU.subtract),
                    reads=[skey, "vT"], writes=["pooled%d" % ft])
            if gi < 3:
                load_w(wv + 2, order[wv + 2])
            for dti in range(4):
                dt_ = gi * 4 + dti
                b = wgt % 2
                for hf in range(2):
                    pt = ps[3 + hf]
                    for kc in range(16):
                        P.op("pe", lambda e, pt=pt, hf=hf, kc=kc, b=b, dti=dti: e.matmul(
                            pt[:, :], lhsT=wbuf[b][:, kc, dti * 128:(dti + 1) * 128],
                            rhs=xT_sb[:, kc, HALO + hf * 512:HALO + (hf + 1) * 512],
                            start=(kc == 0), stop=(kc == 15)),
                            reads=["wbuf%d" % b, "xT"], writes=[PK(pt)])
                for hf in range(2):
                    pt = ps[5 + hf]
                    for cc in range(4):
                        P.op("pe", lambda e, pt=pt, hf=hf, cc=cc, gi=gi, dti=dti: e.matmul(
                            pt[:, :], lhsT=wg_sb[:, gi, cc, dti * 128:(dti + 1) * 128],
                            rhs=pooled[:, cc, hf * 512:(hf + 1) * 512], start=(cc == 0), stop=(cc == 3)),
                            reads=["wg", "pooled%d" % cc], writes=[PK(pt)])
                for hf in range(2):
                    sl = slice(hf * 512, (hf + 1) * 512)
                    P.op("act", lambda e, hf=hf, sl=sl: e.activation(out=ge[:, sl], in_=ps[3 + hf][:, :], func=AF.Exp, scale=-1.0),
                         reads=["ps%d" % (3 + hf)], writes=["ge%d" % hf])
                    P.op("act", lambda e, sl=sl: e.activation(out=ge[:, sl], in_=ge[:, sl], func=AF.Ln, bias=1.0, scale=1.0),
                         reads=["ge%d" % hf], writes=["ge%d" % hf])
                    P.op("act", lambda e, sl=sl: e.activation(out=ge[:, sl], in_=ge[:, sl], func=AF.Exp, scale=-1.0),
                         reads=["ge%d" % hf], writes=["ge%d" % hf])
                    P.op("dve", lambda e, hf=hf, sl=sl: e.tensor_tensor(out=gs[:, sl], in0=ge[:, sl], in1=ps[3 + hf][:, :], op=ALU.mult),
                         reads=["ge%d" % hf, "ps%d" % (3 + hf)], writes=["gs%d" % hf])
                    P.op("act", lambda e, hf=hf, sl=sl, dt_=dt_: e.activation(
                        out=mt[:, sl], in_=ps[5 + hf][:, :], func=AF.Identity,
                        bias=bg_sb[:, dt_:dt_ + 1], scale=1.0),
                        reads=["ps%d" % (5 + hf), "bg"], writes=["mt%d" % hf])
                    P.op("dve", lambda e, sl=sl, dt_=dt_: e.scalar_tensor_tensor(
                        out=yT[:, dt_, sl], in0=mt[:, sl], scalar=os_sb[:, dt_:dt_ + 1], in1=gs[:, sl],
                        op0=ALU.mult, op1=ALU.mult),
                        reads=["mt%d" % hf, "gs%d" % hf, "os"], writes=["yT"])
            if gi < 3:
                load_w(wgt + 2, order[wgt + 2])
    else:
        ssq_sb = nc.alloc_sbuf_tensor("ssq_sb", [8, TPC], F32)
        sel_sb = nc.alloc_sbuf_tensor("sel_sb", [8, 256], F32)
        rs_sb = nc.alloc_sbuf_tensor("rs_sb", [128, 2, TPC], F32)
        yT_v = yT_d.rearrange("(k p) n -> p k n", p=128)
        for h in range(4):
            P.dma("sp", lambda e, h=h: e.dma_start(out=yT[:, 4 * h:4 * h + 4, :], in_=yT_v[:, 4 * h:4 * h + 4, :]),
                  "yT", writes=["yT"])
        P.dma("sp", lambda e: e.dma_start(out=ssq_sb[:], in_=ssq8), "c7", writes=["ssq"])
        P.dma("sp", lambda e: e.dma_start(out=sel_sb[:], in_=sel_d), "c8", writes=["sel"])
        for g in range(2):
            for hf in range(2):
                pt = ps[2 * g + hf]
                P.op("pe", lambda e, pt=pt, g=g, hf=hf: e.matmul(
                    pt[:, :], lhsT=sel_sb[:, g * 128:(g + 1) * 128], rhs=ssq_sb[:, hf * 512:(hf + 1) * 512],
                    start=True, stop=True), reads=["sel", "ssq"], writes=[PK(pt)])
                P.op("act", lambda e, pt=pt, g=g, hf=hf: e.activation(
                    out=rs_sb[:, g, hf * 512:(hf + 1) * 512], in_=pt[:, :], func=AF.Ln, bias=EPS, scale=1.0 / 512),
                    reads=[PK(pt)], writes=["rs%d%d" % (g, hf)])
                P.op("act", lambda e, g=g, hf=hf: e.activation(
                    out=rs_sb[:, g, hf * 512:(hf + 1) * 512], in_=rs_sb[:, g, hf * 512:(hf + 1) * 512],
                    func=AF.Exp, scale=-0.5),
                    reads=["rs%d%d" % (g, hf)], writes=["rs%d%d" % (g, hf)])
            for kc in range(4):
                k = 8 + 4 * g + kc
                P.op("dve", lambda e, k=k, g=g: e.tensor_tensor(
                    out=yT[:, k, :], in0=yT[:, k, :], in1=rs_sb[:, g, :], op=ALU.mult),
                    reads=["yT", "rs%d0" % g, "rs%d1" % g], writes=["yT"])

    w_out_v = w_out.rearrange("(k p) n -> p k n", p=128)
    fkeys = ["xT", "wbuf0", "wbuf1"] if front else []
    for h in range(4):
        P.dma("pool", lambda e, h=h: e.dma_start(out=wout_sb[:, 4 * h:4 * h + 4, :], in_=w_out_v[:, 4 * h:4 * h + 4, :]),
              "wout", writes=["wout"] + fkeys)
    for tt in range(NT):
        P.dma("sp", lambda e, tt=tt: e.dma_start(out=xr_sb[:], in_=xres[tt * 128:(tt + 1) * 128, :]), "xr", writes=["xr"])
        for nb in range(4):
            pt = ps[4 * (tt % 2) + nb]
            for kc in range(16):
                P.op("pe", lambda e, pt=pt, kc=kc, tt=tt, nb=nb: e.matmul(
                    pt[:, :], lhsT=yT[:, kc, tt * 128:(tt + 1) * 128], rhs=wout_sb[:, kc, nb * 512:(nb + 1) * 512],
                    start=(kc == 0), stop=(kc == 15)), reads=["yT", "wout"], writes=[PK(pt)])
            P.op("dve", lambda e, pt=pt, nb=nb: e.scalar_tensor_tensor(
                out=z_sb[:, nb * 512:(nb + 1) * 512], in0=xr_sb[:, nb * 512:(nb + 1) * 512], scalar=ALPHA,
                in1=pt[:, :], op0=ALU.mult, op1=ALU.add),
                reads=["xr", PK(pt)], writes=["z%d" % nb])
            P.op("dve", lambda e, nb=nb: e.bn_stats(out=stats[:, nb, :], in_=z_sb[:, nb * 512:(nb + 1) * 512]),
                 reads=["z%d" % nb], writes=["stats%d" % nb])
        zk = ["z%d" % i for i in range(4)]
        P.op("dve", lambda e: e.bn_aggr(out=mv[:], in_=stats[:].rearrange("p a b -> p (a b)")),
             reads=["stats%d" % i for i in range(4)], writes=["mv"])
        P.op("act", lambda e: e.activation(out=rstd[:], in_=mv[:, 1:2], func=AF.Ln, bias=EPS, scale=1.0),
             reads=["mv"], writes=["rstd"])
        P.op("act", lambda e: e.activation(out=rstd[:], in_=rstd[:], func=AF.Exp, scale=-0.5),
             reads=["rstd"], writes=["rstd"])
        P.op("dve", lambda e: e.scalar_tensor_tensor(out=nmr[:], in0=mv[:, 0:1], scalar=-1.0, in1=rstd[:],
                                                     op0=ALU.mult, op1=ALU.mult),
             reads=["mv", "rstd"], writes=["nmr"])
        P.op("act", lambda e: e.activation(out=z_sb[:], in_=z_sb[:], func=AF.Identity, bias=nmr[:], scale=rstd[:]),
             reads=zk + ["nmr", "rstd"], writes=zk)
        P.op("dve", lambda e: e.tensor_tensor(out=z_sb[:], in0=z_sb[:], in1=lng_sb[:], op=ALU.mult),
             reads=zk + ["lng"], writes=zk)
        P.op("dve", lambda e: e.tensor_tensor(out=z_sb[:], in0=z_sb[:], in1=lnb_sb[:], op=ALU.add),
             reads=zk + ["lnb"], writes=zk)
        P.dma("sp", lambda e, tt=tt: e.dma_start(out=xo[tt * 128:(tt + 1) * 128, :], in_=z_sb[:]), "xo",
              reads=zk, is_output=True)
        for grp in range(4):
            pk = "ps%d" % (4 * ((tt + 1) % 2) + grp)
            pt = ps[4 * ((tt + 1) % 2) + grp]
            for j in range(4):
                kc = grp * 4 + j
                P.op("pe", lambda e, pt=pt, j=j, kc=kc: e.transpose(
                    pt[:, j * 128:(j + 1) * 128], z_sb[:, kc * 128:(kc + 1) * 128], identf[:]),
                    reads=zk + ["identf"], writes=[pk])
            P.op("act", lambda e, pt=pt, grp=grp: e.activation(out=zT_sb[:, grp, :], in_=pt[:, :], func=AF.Copy),
                 reads=[pk], writes=["zT%d" % grp])
            P.dma("act", lambda e, grp=grp, tt=tt: e.dma_start(
                out=xoT_v[:, grp * 4:grp * 4 + 4, tt * 128:(tt + 1) * 128],
                in_=zT_sb[:, grp, :].rearrange("p (k n) -> p k n", k=4)), "xoT%d" % grp,
                reads=["zT%d" % grp], is_output=True)

    P.emit()
    return nc


def build_even(x_bf16, nblk=S // 512):
    nc = bass.Bass("TRN2", target_bir_lowering=False)
    P = Prog(nc)
    NBLK = nblk
    S = nblk * 512
    xT = _din(nc, "xT", [D, S], BF16 if x_bf16 else F32)
    wA = _din(nc, "wA", [D, NPJ])
    cw_d = _din(nc, "cw", [128, 12])
    cb_d = _din(nc, "cb", [128, 3])
    colp_d = _din(nc, "colp", [128, 8])
    dtp_d = _din(nc, "dtp", [2, 2])
    lamv_d = _din(nc, "lamv", [256])
    identf_d = _din(nc, "identf", [128, 128])
    mask2_d = _din(nc, "mask2", [128, 64])
    bdm_d = _din(nc, "bdmask", [128, 128])
    hsel_d = _din(nc, "hsel", [2, 128])
    eye2_d = _din(nc, "eye2", [2, 2])
    yatt = _dout(nc, "yatt", [128, S], BF16)
    yssm = _dout(nc, "yssm", [128, S], BF16)
    ssq = _dout(nc, "ssq", [1, S])

    A = nc.alloc_sbuf_tensor
    W_sb = A("W_sb", [128, 16, NPJ], BF16)
    xb = [A("xb%d" % i, [128, 16, 512], BF16) for i in range(2)]
    KT = A("KT", [128, S], BF16)
    V = A("V", [128, S // 128, 128], BF16)
    QT = A("QT", [128, 512], BF16)
    sg = A("sg", [128, 512], F32)
    sz = A("sz", [128, 512], F32)
    raw = A("raw", [128, 3, 515], F32)
    xsT = A("xsT", [128, 512], F32)
    xsdup = A("xsdup", [128, 1024], BF16)
    BTdup = A("BTdup", [128, 1024], BF16)
    CT = A("CT", [128, 512], BF16)
    tmp = {k: A("t" + k, [128, 512], F32) for k in "ABCDEF"}
    E = [[A("E%d%d" % (i, m), [128, 512], BF16) for m in range(2)] for i in range(2)]
    ysT = A("ysT", [128, 512], F32)
    yatt_sb = A("yatt_sb", [128, 512], BF16)
    yssm_sb = A("yssm_sb", [128, 512], BF16)
    ssq_sb = A("ssq_sb", [1, 512], F32)
    cw = A("cw_sb", [128, 12], F32)
    cb = A("cb_sb", [128, 3], F32)
    colp = A("colp_sb", [128, 8], F32)
    lamv = A("lamv_sb", [128, 256], F32)
    lamt = A("lamt", [128, 128], F32)
    lams = A("lams", [128, 4], F32)
    identb = A("identb", [128, 128], BF16)
    mask2 = A("mask2_sb", [128, 64], F32)
    bdm = A("bdm_sb", [128, 128], F32)
    ones_b = A("ones_b", [128, 128], BF16)
    ones_f = A("ones_f", [128, 128], F32)
    hsel = A("hsel_sb", [2, 128], F32)
    eye2 = A("eye2_sb", [2, 2], F32)
    dtp = A("dtp_sb", [2, 2], F32)
    Acol = A("Acol", [2, 1], F32)
    dt_sb = A("dt_sb", [2, 512], F32)
    a_sb = A("a_sb", [2, 512], F32)
    acs = A("acs", [2, 512], F32)
    dtBD = A("dtBD", [2, 1024], F32)
    acsBD = A("acsBD", [2, 1024], F32)
    lastbc = A("lastbc", [2, 8, 128], F32)
    cols = A("cols", [128, 16], F32)
    negc = A("negc", [128, 8], F32)
    Sst = A("Sst", [128, 128], F32)
    Sbf = A("Sbf", [128, 128], BF16)
    Gm = A("Gm", [128, 64], F32)
    Dm = A("Dm", [128, 64], F32)
    E2 = A("E2", [128, 64], F32)
    M2 = A("M2", [128, 64], BF16)
    R2e = A("R2e", [128, 64], F32)
    XdtBD = A("XdtBD", [128, 128], BF16)
    XdecBD = A("XdecBD", [128, 128], BF16)
    Btok = A("Btok", [128, 128], BF16)
    CD = A("CD", [128, 128], F32)
    y1 = A("y1", [128, 64], F32)
    ps = [nc.alloc_psum_tensor("ps%d" % i, [128, 512], F32) for i in range(8)]
    psS = [ps[0], ps[1]]
    psO = [ps[2], ps[3]]
    psD = [ps[4], ps[5]]
    rG, rR, rX, rcol = ps[6][:, 0:64], ps[6][:, 64:128], ps[6][:, 128:256], ps[6][:, 256:272]
    rB, rC, rS = ps[7][:, 0:128], ps[7][:, 128:256], ps[7][:, 256:384]
    rYo, rYd = ps[7][:, 384:448], ps[7][:, 448:512]

    def ld(q, dst, src, key):
        P.dma(q, lambda e: e.dma_start(out=dst, in_=src), key, writes=[key])

    ld("sp", cw[:], cw_d, "cw")
    ld("sp", cb[:], cb_d, "cb")
    ld("sp", colp[:], colp_d, "colp")
    ld("sp", dtp[:], dtp_d, "dtp")
    ld("sp", lamv[:], lamv_d.partition_broadcast(128), "lamv")
    ld("sp", mask2[:], mask2_d, "mask2")
    ld("sp", bdm[:], bdm_d, "bdm")
    ld("sp", hsel[:], hsel_d, "hsel")
    ld("sp", eye2[:], eye2_d, "eye2")
    ld("pool", identb[:], identf_d, "identb")
    P.op("pool", lambda e: e.memset(ones_b[:], 1.0), writes=["ones_b"])
    P.op("pool", lambda e: e.memset(ones_f[:], 1.0), writes=["ones_f"])
    P.op("pool", lambda e: e.memset(raw[:], 0.0), writes=["raw0", "raw1", "raw2"])
    P.op("pool", lambda e: e.memset(Sst[:], 0.0), writes=["Sst"])
    wA_v = wA.rearrange("(k p) n -> p k n", p=128)
    for h in range(4):
        P.dma("pool", lambda e, h=h: e.dma_start(out=W_sb[:, 4 * h:4 * h + 4, :], in_=wA_v[:, 4 * h:4 * h + 4, :]),
              "W", writes=["W"])
    xT_v = xT.rearrange("(k p) n -> p k n", p=128)

    def load_x(b):
        i = b % 2
        for h in range(2):
            if x_bf16:
                P.dma("sp" if h == 0 else "act", lambda e, h=h: e.dma_start(
                    out=xb[i][:, 8 * h:8 * h + 8, :], in_=xT_v[:, 8 * h:8 * h + 8, b * 512:(b + 1) * 512]),
                    "xb%d" % i, writes=["xb%d" % i])
            else:
                P.dma("pool", lambda e, h=h: e.dma_start(
                    out=xb[i][:, 8 * h:8 * h + 8, :], in_=xT_v[:, 8 * h:8 * h + 8, b * 512:(b + 1) * 512]),
                    "xb%d" % i, writes=["xb%d" % i])

    P.op("dve", lambda e: e.tensor_tensor(out=lamt[:, 0:64], in0=lamv[:, 0:64], in1=lamv[:, 64:128], op=ALU.mult),
         reads=["lamv"], writes=["lamt0"])
    P.op("dve", lambda e: e.tensor_tensor(out=lamt[:, 64:128], in0=lamv[:, 128:192], in1=lamv[:, 192:256], op=ALU.mult),
         reads=["lamv"], writes=["lamt1"])
    P.op("dve", lambda e: e.reduce_sum(out=lams[:, 0:1], in_=lamt[:, 0:64], axis=mybir.AxisListType.X),
         reads=["lamt0"], writes=["lams0"])
    P.op("dve", lambda e: e.reduce_sum(out=lams[:, 1:2], in_=lamt[:, 64:128], axis=mybir.AxisListType.X),
         reads=["lamt1"], writes=["lams1"])
    P.op("act", lambda e: e.activation(out=lams[:, 0:2], in_=lams[:, 0:2], func=AF.Exp),
         reads=["lams0", "lams1"], writes=["lams0", "lams1"])
    P.op("dve", lambda e: e.tensor_tensor(out=lams[:, 2:3], in0=lams[:, 1:2], in1=lams[:, 0:1], op=ALU.subtract),
         reads=["lams0", "lams1"], writes=["neglam"])
    P.op("dve", lambda e: e.tensor_tensor(out=lams[:, 2:3], in0=lams[:, 2:3], in1=colp[:, 3:4], op=ALU.subtract),
         reads=["neglam", "colp"], writes=["neglam"])
    P.op("dve", lambda e: e.tensor_tensor(out=lams[:, 3:4], in0=colp[:, 2:3], in1=colp[:, 4:5], op=ALU.mult),
         reads=["colp"], writes=["coef"])
    P.op("act", lambda e: e.activation(out=Acol[:], in_=dtp[:, 1:2], func=AF.Exp), reads=["dtp"], writes=["Acol"])
    P.op("dve", lambda e: e.tensor_scalar(out=Acol[:], in0=Acol[:], scalar1=-1.0, scalar2=None, op0=ALU.mult),
         reads=["Acol"], writes=["Acol"])

    def silu_from(src_ap, src_keys, dst_ap, dst_key, t, tkey):
        P.op("act", lambda e: e.activation(out=t, in_=src_ap, func=AF.Exp, scale=-1.0), reads=src_keys, writes=[tkey])
        P.op("act", lambda e: e.activation(out=t, in_=t, func=AF.Ln, bias=1.0, scale=1.0), reads=[tkey], writes=[tkey])
        P.op("act", lambda e: e.activation(out=t, in_=t, func=AF.Exp, scale=-1.0), reads=[tkey], writes=[tkey])
        P.op("dve", lambda e: e.tensor_tensor(out=dst_ap, in0=t, in1=src_ap, op=ALU.mult),
             reads=[tkey] + list(src_keys), writes=[dst_key])

    def inproj(b):
        xbi = xb[b % 2]
        xk = "xb%d" % (b % 2)
        bank = [0]

        def nextbank():
            bank[0] ^= 1
            return ps[bank[0]], "ps%d" % bank[0]

        def proj(c0):
            pt, pk = nextbank()
            for kc in range(16):
                P.op("pe", lambda e, kc=kc: e.matmul(pt[:, :], lhsT=W_sb[:, kc, c0:c0 + 128], rhs=xbi[:, kc, :],
                                                     start=(kc == 0), stop=(kc == 15)),
                     reads=["W", xk], writes=[pk])
            return pt, pk

        pt, pk = proj(0)
        P.op("act", lambda e, pt=pt: e.activation(out=QT[:], in_=pt[:, :], func=AF.Copy), reads=[pk], writes=["QT"])
        pt, pk = proj(128)
        P.op("act", lambda e, pt=pt: e.activation(out=KT[:, b * 512:(b + 1) * 512], in_=pt[:, :], func=AF.Copy),
             reads=[pk], writes=["KT%d" % b])
        pt, pk = nextbank()
        for i in range(4):
            for kc in range(16):
                P.op("pe", lambda e, pt=pt, i=i, kc=kc: e.matmul(
                    pt[:, i * 128:(i + 1) * 128], lhsT=xbi[:, kc, i * 128:(i + 1) * 128], rhs=W_sb[:, kc, 256:384],
                    start=(kc == 0), stop=(kc == 15)), reads=["W", xk], writes=[pk])
        P.op("act", lambda e, pt=pt: e.activation(
            out=V[:, b * 4:(b + 1) * 4, :].rearrange("p a d -> p (a d)"), in_=pt[:, :], func=AF.Copy),
            reads=[pk], writes=["V%d" % b])
        pt, pk = proj(384)
        silu_from(pt[:, :], [pk], sg[:], "sg", tmp["A"][:], "tA")
        pt, pk = proj(512)
        silu_from(pt[:, :], [pk], sz[:], "sz", tmp["B"][:], "tB")
        for t in range(3):
            pt, pk = proj(640 + 128 * t)
            rk = "raw%d" % t
            P.op("act", lambda e, pt=pt, t=t: e.activation(out=raw[:, t, 3:515], in_=pt[:, :], func=AF.Copy),
                 reads=[pk], writes=[rk])
            acc = tmp["C"][:]
            P.op("dve", lambda e, t=t: e.tensor_scalar(
                out=acc, in0=raw[:, t, 3:515], scalar1=cw[:, 4 * t + 3:4 * t + 4], scalar2=cb[:, t:t + 1],
                op0=ALU.mult, op1=ALU.add), reads=[rk, "cw", "cb"], writes=["tC"])
            for j in range(3):
                P.op("dve", lambda e, t=t, j=j: e.scalar_tensor_tensor(
                    out=acc, in0=raw[:, t, j:j + 512], scalar=cw[:, 4 * t + j:4 * t + j + 1], in1=acc,
                    op0=ALU.mult, op1=ALU.add), reads=[rk, "cw", "tC"], writes=["tC"])
            P.op("dve", lambda e, t=t: e.tensor_copy(out=raw[:, t, 0:3], in_=raw[:, t, 512:515]),
                 reads=[rk, "tC"], writes=[rk])
            if t == 0:
                silu_from(acc, ["tC"], xsT[:], "xsT", tmp["D"][:], "tD")
                for u in range(2):
                    P.op("dve", lambda e, u=u: e.tensor_copy(
                        out=xsdup[:].rearrange("p (c u l) -> p c u l", c=8, u=2)[:, :, u, :],
                        in_=xsT[:].rearrange("p (c l) -> p c l", c=8)), reads=["xsT"], writes=["xsdup"])
            elif t == 1:
                silu_from(acc, ["tC"], tmp["E"][:], "tE", tmp["D"][:], "tD")
                for u in range(2):
                    P.op("dve", lambda e, u=u: e.tensor_copy(
                        out=BTdup[:].rearrange("p (c u l) -> p c u l", c=8, u=2)[:, :, u, :],
                        in_=tmp["E"][:].rearrange("p (c l) -> p c l", c=8)), reads=["tE"], writes=["BTdup"])
            else:
                silu_from(acc, ["tC"], CT[:], "CT", tmp["D"][:], "tD")
        pt, pk = nextbank()
        for kc in range(16):
            P.op("pe", lambda e, pt=pt, kc=kc: e.matmul(pt[0:2, :], lhsT=W_sb[:, kc, 1024:1026], rhs=xbi[:, kc, :],
                                                        start=(kc == 0), stop=(kc == 15)),
                 reads=["W", xk], writes=[pk])
        P.op("act", lambda e, pt=pt: e.activation(out=dt_sb[:], in_=pt[0:2, :], func=AF.Exp, bias=dtp[:, 0:1], scale=1.0),
             reads=[pk, "dtp"], writes=["dt"])
        P.op("act", lambda e: e.activation(out=dt_sb[:], in_=dt_sb[:], func=AF.Ln, bias=1.0, scale=1.0),
             reads=["dt"], writes=["dt"])
        P.op("dve", lambda e: e.tensor_scalar(out=a_sb[:], in0=dt_sb[:], scalar1=Acol[:, 0:1], scalar2=None, op0=ALU.mult),
             reads=["dt", "Acol"], writes=["a"])
        for c in range(8):
            P.op("dve", lambda e, c=c: e.tensor_tensor_scan(
                out=acs[:, c * 64:(c + 1) * 64], data0=ones_f[0:2, 0:64], data1=a_sb[:, c * 64:(c + 1) * 64],
                initial=0.0, op0=ALU.mult, op1=ALU.add), reads=["a", "ones_f"], writes=["acs"])
        for h in range(2):
            P.op("dve", lambda e, h=h: e.tensor_scalar(
                out=dtBD[:].rearrange("p (c u l) -> p c u l", c=8, u=2)[:, :, h, :],
                in0=dt_sb[:].rearrange("p (c l) -> p c l", c=8), scalar1=eye2[:, h:h + 1], scalar2=None, op0=ALU.mult),
                reads=["dt", "eye2"], writes=["dtBD"])
            P.op("dve", lambda e, h=h: e.tensor_scalar(
                out=acsBD[:].rearrange("p (c u l) -> p c u l", c=8, u=2)[:, :, h, :],
                in0=acs[:].rearrange("p (c l) -> p c l", c=8), scalar1=eye2[:, h:h + 1], scalar2=None, op0=ALU.mult),
                reads=["acs", "eye2"], writes=["acsBD"])
        for c in range(8):
            P.op("dve", lambda e, c=c: e.tensor_scalar(
                out=lastbc[:, c, :], in0=ones_f[0:2, :], scalar1=acs[:, c * 64 + 63:c * 64 + 64], scalar2=None, op0=ALU.mult),
                reads=["acs", "ones_f"], writes=["lastbc"])

    def attention_steps(b):
        nkb = 4 * b + 4

        def scores(kb):
            j = kb - 4 * b
            c0 = max(0, j) * 128
            st = kb % 2
            for m in range(2):
                P.op("pe", lambda e, kb=kb, m=m, c0=c0: e.matmul(
                    psS[m][:, c0:512], lhsT=KT[m * 64:(m + 1) * 64, kb * 128:(kb + 1) * 128],
                    rhs=QT[m * 64:(m + 1) * 64, c0:512], start=True, stop=True),
                    reads=["KT%d" % (kb // 4), "QT"], writes=["ps%d" % m])
                P.op("act", lambda e, m=m, c0=c0, st=st: e.activation(
                    out=E[st][m][:, c0:512], in_=psS[m][:, c0:512], func=AF.Exp, scale=0.125),
                    reads=["ps%d" % m], writes=["E%d%d" % (st, m)])
                if j >= 0:
                    P.op("pool", lambda e, m=m, c0=c0, st=st: e.memset(E[st][m][64:128, c0:c0 + 64], 0.0),
                         reads=[], writes=["E%d%d" % (st, m)])

        def accum(kb):
            j = kb - 4 * b
            c0 = max(0, j) * 128
            st = kb % 2
            for m in range(2):
                P.op("pe", lambda e, kb=kb, m=m, c0=c0, st=st: e.matmul(
                    psO[m][:, c0:512], lhsT=V[:, kb, :], rhs=E[st][m][:, c0:512],
                    start=(kb == 0), stop=(kb == nkb - 1)),
                    reads=["V%d" % (kb // 4), "E%d%d" % (st, m)], writes=["ps%d" % (2 + m)])
                P.op("pe", lambda e, kb=kb, m=m, c0=c0, st=st: e.matmul(
                    psD[m][:, c0:512], lhsT=ones_b[:], rhs=E[st][m][:, c0:512],
                    start=(kb == 0), stop=(kb == nkb - 1)),
                    reads=["ones_b", "E%d%d" % (st, m)], writes=["ps%d" % (4 + m)])

        scores(0)
        for kb in range(nkb):
            if kb + 1 < nkb:
                scores(kb + 1)
            accum(kb)
            yield
        tA, tB, tC, tD = tmp["A"][:], tmp["B"][:], tmp["C"][:], tmp["D"][:]
        P.op("act", lambda e: e.activation(out=tA, in_=psD[0][:, :], func=AF.Ln), reads=["ps4"], writes=["tA"])
        P.op("act", lambda e: e.activation(out=tB, in_=psD[1][:, :], func=AF.Ln), reads=["ps5"], writes=["tB"])
        P.op("act", lambda e: e.activation(out=tA, in_=tA, func=AF.Exp, scale=-1.0), reads=["tA"], writes=["tA"])
        P.op("act", lambda e: e.activation(out=tB, in_=tB, func=AF.Exp, scale=-1.0), reads=["tB"], writes=["tB"])
        P.op("dve", lambda e: e.tensor_tensor(out=tA, in0=tA, in1=psO[0][:, :], op=ALU.mult), reads=["tA", "ps2"], writes=["tA"])
        P.op("dve", lambda e: e.tensor_tensor(out=tB, in0=tB, in1=psO[1][:, :], op=ALU.mult), reads=["tB", "ps3"], writes=["tB"])
        P.op("dve", lambda e: e.scalar_tensor_tensor(out=tA, in0=tB, scalar=lams[:, 2:3], in1=tA, op0=ALU.mult, op1=ALU.add),
             reads=["tA", "tB", "neglam"], writes=["tA"])
        P.op("act", lambda e: e.activation(out=tB, in_=tA, func=AF.Square), reads=["tA"], writes=["tB"])
        P.op("pe", lambda e: e.matmul(psS[0][:, :], lhsT=ones_f[:], rhs=tB, start=True, stop=True),
             reads=["ones_f", "tB"], writes=["ps0"])
        P.op("act", lambda e: e.activation(out=tB, in_=psS[0][:, :], func=AF.Ln, bias=EPS, scale=1.0 / 128),
             reads=["ps0"], writes=["tB"])
        P.op("act", lambda e: e.activation(out=tB, in_=tB, func=AF.Exp, scale=-0.5), reads=["tB"], writes=["tB"])
        P.op("dve", lambda e: e.tensor_tensor(out=tA, in0=tA, in1=tB, op=ALU.mult), reads=["tA", "tB"], writes=["tA"])
        P.op("dve", lambda e: e.scalar_tensor_tensor(out=yatt_sb[:], in0=tA, scalar=lams[:, 3:4], in1=sg[:],
                                                     op0=ALU.mult, op1=ALU.mult),
             reads=["tA", "coef", "sg"], writes=["yatt_sb"])
        P.dma("sp", lambda e: e.dma_start(out=yatt[:, b * 512:(b + 1) * 512], in_=yatt_sb[:]), "yatt",
              reads=["yatt_sb"], is_output=True)
        yield

    def ssd_steps(b):
        for c in range(8):
            P.op("pe", lambda e, c=c: e.matmul(rcol[:, 2 * c:2 * c + 1], lhsT=dtBD[:, c * 128:(c + 1) * 128],
                                               rhs=ones_f[0:2, 0:1], start=True, stop=True),
                 reads=["dtBD", "ones_f"], writes=["ps6"])
            P.op("pe", lambda e, c=c: e.matmul(rcol[:, 2 * c + 1:2 * c + 2], lhsT=acsBD[:, c * 128:(c + 1) * 128],
                                               rhs=ones_f[0:2, 0:1], start=True, stop=True),
                 reads=["acsBD", "ones_f"], writes=["ps6"])
        P.op("act", lambda e: e.activation(out=cols[:], in_=rcol, func=AF.Copy), reads=["ps6"], writes=["cols"])
        P.op("dve", lambda e: e.tensor_scalar(
            out=negc[:], in0=cols[:].rearrange("p (c t) -> p c t", t=2)[:, :, 1], scalar1=-1.0, scalar2=None, op0=ALU.mult),
            reads=["cols"], writes=["negc"])
        for c in range(8):
            cs = slice(c * 64, (c + 1) * 64)
            cd = slice(c * 128, (c + 1) * 128)
            P.op("pe", lambda e, cs=cs, cd=cd: e.matmul(rG, lhsT=BTdup[:, cd], rhs=CT[:, cs], start=True, stop=True),
                 reads=["BTdup", "CT"], writes=["ps6"])
            P.op("pe", lambda e, cs=cs: e.matmul(rR, lhsT=hsel[:], rhs=acs[:, cs], start=True, stop=True),
                 reads=["hsel", "acs"], writes=["ps6"])
            P.op("pe", lambda e, cd=cd: e.matmul(rX, lhsT=xsdup[:, cd], rhs=identb[:], start=True, stop=True),
                 reads=["xsdup", "identb"], writes=["ps6"])
            P.op("pe", lambda e, c=c: e.matmul(rC, lhsT=lastbc[:, c, :], rhs=hsel[:], start=True, stop=True),
                 reads=["lastbc", "hsel"], writes=["ps7"])
            P.op("pe", lambda e, cd=cd: e.matmul(rB, lhsT=BTdup[:, cd], rhs=identb[:], start=True, stop=True),
                 reads=["BTdup", "identb"], writes=["ps7"])
            P.op("dve", lambda e: e.tensor_tensor(out=Gm[:], in0=rG, in1=mask2[:], op=ALU.mult),
                 reads=["ps6", "mask2"], writes=["Gm"])
            P.op("dve", lambda e, c=c: e.tensor_scalar(out=Dm[:], in0=rR, scalar1=negc[:, c:c + 1], scalar2=0.0,
                                                       op0=ALU.add, op1=ALU.min),
                 reads=["ps6", "negc"], writes=["Dm"])
            P.op("act", lambda e: e.activation(out=E2[:], in_=Dm[:], func=AF.Exp), reads=["Dm"], writes=["E2"])
            P.op("act", lambda e: e.activation(out=R2e[:], in_=rR, func=AF.Exp), reads=["ps6"], writes=["R2e"])
            P.op("dve", lambda e, c=c: e.scalar_tensor_tensor(
                out=XdtBD[:], in0=rX, scalar=cols[:, 2 * c:2 * c + 1], in1=bdm[:], op0=ALU.mult, op1=ALU.mult),
                reads=["ps6", "cols", "bdm"], writes=["XdtBD"])
            P.op("dve", lambda e: e.tensor_tensor(out=M2[:], in0=Gm[:], in1=E2[:], op=ALU.mult),
                 reads=["Gm", "E2"], writes=["M2"])
            P.op("act", lambda e: e.activation(out=XdecBD[:], in_=XdtBD[:], func=AF.Identity, scale=E2[:, 63:64]),
                 reads=["XdtBD", "E2"], writes=["XdecBD"])
            P.op("act", lambda e: e.activation(out=Btok[:], in_=rB, func=AF.Copy), reads=["ps7"], writes=["Btok"])
            P.op("act", lambda e: e.activation(out=CD[:], in_=rC, func=AF.Exp), reads=["ps7"], writes=["CD"])
            P.op("act", lambda e: e.activation(out=Sbf[:], in_=Sst[:], func=AF.Copy), reads=["Sst"], writes=["Sbf"])
            P.op("pe", lambda e, cs=cs: e.matmul(rYo, lhsT=Sbf[:], rhs=CT[:, cs], start=True, stop=True),
                 reads=["Sbf", "CT"], writes=["ps7"])
            P.op("pe", lambda e: e.matmul(rYd, lhsT=XdtBD[:], rhs=M2[:], start=True, stop=True),
                 reads=["XdtBD", "M2"], writes=["ps7"])
            P.op("pe", lambda e: e.matmul(rS, lhsT=Btok[:], rhs=XdecBD[:], start=True, stop=True),
                 reads=["Btok", "XdecBD"], writes=["ps7"])
            P.op("dve", lambda e: e.tensor_tensor(out=y1[:], in0=rYo, in1=R2e[:], op=ALU.mult),
                 reads=["ps7", "R2e"], writes=["y1"])
            P.op("dve", lambda e, cs=cs: e.tensor_tensor(out=ysT[:, cs], in0=y1[:], in1=rYd, op=ALU.add),
                 reads=["y1", "ps7"], writes=["ysT"])
            P.op("dve", lambda e: e.tensor_tensor(out=Sst[:], in0=Sst[:], in1=CD[:], op=ALU.mult),
                 reads=["Sst", "CD"], writes=["Sst"])
            P.op("dve", lambda e: e.tensor_tensor(out=Sst[:], in0=Sst[:], in1=rS, op=ALU.add),
                 reads=["Sst", "ps7"], writes=["Sst"])
            yield
        tE, tF = tmp["E"][:], tmp["F"][:]
        P.op("dve", lambda e: e.scalar_tensor_tensor(out=tE, in0=xsT[:], scalar=colp[:, 0:1], in1=ysT[:],
                                                     op0=ALU.mult, op1=ALU.add),
             reads=["xsT", "colp", "ysT"], writes=["tE"])
        P.op("dve", lambda e: e.tensor_tensor(out=tE, in0=tE, in1=sz[:], op=ALU.mult), reads=["tE", "sz"], writes=["tE"])
        P.op("act", lambda e: e.activation(out=tF, in_=tE, func=AF.Square), reads=["tE"], writes=["tF"])
        P.op("pe", lambda e: e.matmul(ps[6][0:1, :], lhsT=ones_f[:, 0:1], rhs=tF, start=True, stop=True),
             reads=["ones_f", "tF"], writes=["ps6"])
        P.op("act", lambda e: e.activation(out=ssq_sb[:], in_=ps[6][0:1, :], func=AF.Copy), reads=["ps6"], writes=["ssq_sb"])
        P.op("act", lambda e: e.activation(out=yssm_sb[:], in_=tE, func=AF.Identity, scale=colp[:, 1:2]),
             reads=["tE", "colp"], writes=["yssm_sb"])
        P.dma("sp", lambda e: e.dma_start(out=ssq[0:1, b * 512:(b + 1) * 512], in_=ssq_sb[:]), "ssqo",
              reads=["ssq_sb"], is_output=True)
        P.dma("sp", lambda e: e.dma_start(out=yssm[:, b * 512:(b + 1) * 512], in_=yssm_sb[:]), "yssm",
              reads=["yssm_sb"], is_output=True)
        yield

    def interleave(b):
        att = attention_steps(b)
        sd = ssd_steps(b)
        nk = 4 * b + 5
        ns = 9
        ia = isd = 0
        for _ in att:
            pass
        for _ in sd:
            pass
        return
        while ia < nk or isd < ns:
            if isd >= ns or (ia < nk and ia * ns <= isd * nk):
                next(att, None)
                ia += 1
            else:
                next(sd, None)
                isd += 1

    load_x(0)
    for b in range(NBLK):
        if b + 1 < NBLK:
            load_x(b + 1)
        inproj(b)
        interleave(b)
    P.emit()
    return nc


_NC = {}
BF = ml_dtypes.bfloat16


def _get(name, builder):
    if name not in _NC:
        _NC[name] = builder()
    return _NC[name]


def _launch(nc, in_maps):
    res = run_bass_kernel_spmd(nc, in_maps, core_ids=list(range(len(in_maps))))
    return res.results


_IDENTF = np.eye(128, dtype=np.float32)


def run_odd(xtok, xT_bf, w_in, w_grp, b_grp, scale, w_out, lng, lnb, cores=range(NCORES)):
    nc = _get("odd", lambda: build_tail(True))
    bg = np.ascontiguousarray(b_grp.reshape(16, 128).T)
    osc = np.ascontiguousarray(scale.reshape(16, 128).T)
    in_maps = []
    for c in cores:
        t0 = c * TPC
        xT_c = np.zeros((D, HALO + TPC), BF)
        xT_c[:, HALO:] = xT_bf[:, t0:t0 + TPC]
        if c > 0:
            xT_c[:, :HALO] = xT_bf[:, t0 - HALO:t0]
        inv = np.zeros((4, 16), np.float32)
        for gi in range(4):
            for i in range(16):
                inv[gi, i] = 1.0 / min(t0 + i + 1, 2 ** (gi + 1))
        in_maps.append(dict(xres=np.ascontiguousarray(xtok[t0:t0 + TPC]), w_out=w_out, lng=lng, lnb=lnb,
                            identf=_IDENTF, xT=xT_c, w_in=w_in, w_grp=w_grp, bgrp=bg, oscale=osc,
                            invtab=inv.reshape(-1)))
    res = _launch(nc, in_maps)
    xo = np.concatenate([r["xo"] for r in res], 0)
    xoT = np.concatenate([r["xoT"] for r in res], 1)
    return xo, xoT


def _even_consts():
    mask2 = np.zeros((128, 64), np.float32)
    for u in range(2):
        for s_ in range(64):
            mask2[u * 64 + s_, s_:] = 1.0
    bdm = np.zeros((128, 128), np.float32)
    bdm[:64, :64] = 1.0
    bdm[64:, 64:] = 1.0
    hsel = np.zeros((2, 128), np.float32)
    hsel[0, :64] = 1.0
    hsel[1, 64:] = 1.0
    return dict(identf=_IDENTF, mask2=mask2, bdmask=bdm, hsel=hsel, eye2=np.eye(2, dtype=np.float32))


def run_even(xT, w_in, conv_w, conv_b, dt_bias, a_log, d_skip, ssm_norm_g, lq1, lk1, lq2, lk2, subln_g,
             lam_init, cores=range(NCORES), nblk=S // 512):
    x_bf16 = xT.dtype != np.float32
    nc = _get(("even", x_bf16, nblk), lambda: build_even(x_bf16, nblk))
    consts = _even_consts()
    ar = np.arange(128)
    in_maps = []
    for c in cores:
        g = c // 4
        idx = np.concatenate([c * 128 + ar, 1024 + c * 128 + ar, 2048 + c * 128 + ar, 3072 + c * 128 + ar,
                              4096 + c * 128 + ar, 5120 + c * 128 + ar, 6144 + g * 128 + ar, 6400 + g * 128 + ar,
                              6656 + 2 * c + np.arange(2)])
        wA = np.ascontiguousarray(w_in[:, idx])
        chans = [c * 128 + ar, 1024 + g * 128 + ar, 1280 + g * 128 + ar]
        cw = np.zeros((128, 12), np.float32)
        cb = np.zeros((128, 3), np.float32)
        for t in range(3):
            cw[:, 4 * t:4 * t + 4] = conv_w[:, chans[t]].T
            cb[:, t] = conv_b[chans[t]]
        colp = np.zeros((128, 8), np.float32)
        colp[:, 0] = np.repeat(d_skip[2 * c:2 * c + 2], 64)
        colp[:, 1] = ssm_norm_g[c * 128:(c + 1) * 128]
        colp[:, 2] = subln_g
        colp[:, 3] = lam_init
        colp[:, 4] = 1.0 - lam_init
        dtp = np.stack([dt_bias[2 * c:2 * c + 2], a_log[2 * c:2 * c + 2]], 1).astype(np.float32)
        lamv = np.concatenate([lq1, lk1, lq2, lk2]).astype(np.float32)
        m = dict(xT=xT, wA=wA, cw=cw, cb=cb, colp=colp, dtp=dtp, lamv=lamv)
        m.update(consts)
        in_maps.append(m)
    res = _launch(nc, in_maps)
    return [r["yatt"] for r in res], [r["yssm"] for r in res], [r["ssq"] for r in res]


def _sel8():
    sel = np.zeros((8, 256), np.float32)
    sel[0:4, 0:128] = 1.0
    sel[4:8, 128:256] = 1.0
    return sel


def run_evtail(xtok, yT, ssq8, w_out, lng, lnb, cores=range(NCORES)):
    nc = _get("evtail", lambda: build_tail(False))
    sel = _sel8()
    in_maps = []
    for c in cores:
        t0 = c * TPC
        in_maps.append(dict(xres=np.ascontiguousarray(xtok[t0:t0 + TPC]), w_out=w_out, lng=lng, lnb=lnb,
                            identf=_IDENTF, yT=np.ascontiguousarray(yT[:, t0:t0 + TPC]),
                            ssq8=np.ascontiguousarray(ssq8[:, t0:t0 + TPC]), sel=sel))
    res = _launch(nc, in_maps)
    xo = np.concatenate([r["xo"] for r in res], 0)
    xoT = np.concatenate([r["xoT"] for r in res], 1)
    return xo, xoT


def kernel(x, ev_w_in, ev_conv_w, ev_conv_b, ev_dt_bias, ev_a_log, ev_d_skip, ev_ssm_norm_g,
           ev_lambda_q1, ev_lambda_k1, ev_lambda_q2, ev_lambda_k2, ev_subln_g, ev_w_out,
           od_w_in, od_w_grp, od_b_grp, od_scale, od_w_out, ln_g, ln_b):
    f = lambda a: np.asarray(a, dtype=np.float32)
    xtok = np.ascontiguousarray(f(x)[0])
    xT = np.ascontiguousarray(xtok.T)
    for l in range(DEPTH):
        i = l // 2
        if l % 2 == 0:
            lam_init = 0.8 - 0.6 * math.exp(-0.3 * l)
            ya, ys, sq = run_even(xT, f(ev_w_in[i]), f(ev_conv_w[i]), f(ev_conv_b[i]), f(ev_dt_bias[i]), f(ev_a_log[i]),
                                  f(ev_d_skip[i]), f(ev_ssm_norm_g[i]), f(ev_lambda_q1[i]), f(ev_lambda_k1[i]),
                                  f(ev_lambda_q2[i]), f(ev_lambda_k2[i]), f(ev_subln_g[i]), lam_init)
            yT = np.concatenate(list(ya) + list(ys), 0)
            ssq8 = np.concatenate(list(sq), 0)
            xtok, xT = run_evtail(xtok, yT, ssq8, f(ev_w_out[i]), f(ln_g[l]), f(ln_b[l]))
        else:
            xtok, xT = run_odd(xtok, xT, f(od_w_in[i]), f(od_w_grp[i]), f(od_b_grp[i]), f(od_scale[i]), f(od_w_out[i]),
                               f(ln_g[l]), f(ln_b[l]))
    return np.ascontiguousarray(xtok[None]).astype(np.float32)
```

```python
import math
import numpy as np
import ml_dtypes
import concourse.bass as bass
import concourse.mybir as mybir
from concourse.bass_utils import run_bass_kernel_spmd

F32 = mybir.dt.float32
BF16 = mybir.dt.bfloat16
AF = mybir.ActivationFunctionType
ALU = mybir.AluOpType

NCORES = 8
D = 2048
S = 8192
TPC = S // NCORES
HALO = 16
DEPTH = 4
ALPHA = (2.0 * DEPTH) ** 0.25
EPS = 1e-5
NPJ = 1026


class Prog:
    ENG = ("pe", "act", "dve", "pool", "sp")

    def __init__(self, nc):
        self.nc = nc
        self.q = {e: [] for e in self.ENG}
        self.sem = {e: nc.alloc_semaphore("sem_" + e) for e in ("pe", "act", "dve", "pool")}
        self.cnt = {e: 0 for e in self.sem}
        self.last_w = {}
        self.reads = {}
        self.waited = {e: {} for e in self.ENG}
        self.dsem = {}
        self.out_tokens = []

    def _deps(self, eng, reads, writes):
        toks = []
        for k in list(reads) + list(writes):
            t = self.last_w.get(k)
            if t is not None:
                toks.append(t)
        for k in writes:
            toks.extend(self.reads.get(k, ()))
        waits = {}
        for (sem, val, src) in toks:
            if src == "pe" and eng == "pe":
                continue
            sid = id(sem)
            if self.waited[eng].get(sid, (None, 0))[1] >= val:
                continue
            if sid not in waits or waits[sid][1] < val:
                waits[sid] = (sem, val)
        for sid, (sem, val) in waits.items():
            self.waited[eng][sid] = (sem, val)
        return list(waits.values())

    def _commit(self, tok, reads, writes):
        for k in reads:
            self.reads.setdefault(k, []).append(tok)
        for k in writes:
            self.last_w[k] = tok
            self.reads[k] = []

    def op(self, eng, fn, reads=(), writes=()):
        waits = self._deps(eng, reads, writes)
        self.cnt[eng] += 1
        tok = (self.sem[eng], self.cnt[eng], eng)
        self.q[eng].append((waits, fn, (self.sem[eng], 1)))
        self._commit(tok, reads, writes)

    def dma(self, queue, fn, semname, reads=(), writes=(), is_output=False):
        if semname not in self.dsem:
            self.dsem[semname] = [self.nc.alloc_semaphore("d_" + semname), 0]
        ent = self.dsem[semname]
        waits = self._deps(queue, reads, writes)
        ent[1] += 16
        tok = (ent[0], ent[1], "dma")
        self.q[queue].append((waits, fn, (ent[0], 16)))
        self._commit(tok, reads, writes)
        if is_output:
            self.out_tokens.append(tok)

    def emit(self):
        nc = self.nc
        fin = {}
        for (sem, val, _) in self.out_tokens:
            if id(sem) not in fin or fin[id(sem)][1] < val:
                fin[id(sem)] = (sem, val)
        q = self.q

        def run(e, name, final=False):
            for waits, fn, (sem, amt) in q[name]:
                for (ws, wv) in waits:
                    e.wait_ge(ws, wv)
                fn(e).then_inc(sem, amt)
            if final:
                for (ws, wv) in fin.values():
                    e.wait_ge(ws, wv)

        with nc.Block() as block:
            @block.sync
            def _(e):
                run(e, "sp", final=True)

            @block.tensor
            def _(e):
                run(e, "pe")

            @block.scalar
            def _(e):
                run(e, "act")

            @block.vector
            def _(e):
                run(e, "dve")

            @block.gpsimd
            def _(e):
                run(e, "pool")


def PK(pt):
    return pt.name


def _din(nc, name, shape, dt=F32):
    return nc.dram_tensor(name, list(shape), dt, kind="ExternalInput").ap()


def _dout(nc, name, shape, dt=F32):
    return nc.dram_tensor(name, list(shape), dt, kind="ExternalOutput").ap()


def build_tail(front):
    nc = bass.Bass("TRN2", target_bir_lowering=False)
    P = Prog(nc)
    NT = TPC // 128
    xres = _din(nc, "xres", [TPC, D])
    w_out = _din(nc, "w_out", [D, D])
    lng = _din(nc, "lng", [D])
    lnb = _din(nc, "lnb", [D])
    xo = _dout(nc, "xo", [TPC, D])
    xoT = _dout(nc, "xoT", [D, TPC], BF16)
    identf_d = _din(nc, "identf", [128, 128])
    if front:
        xT = _din(nc, "xT", [D, HALO + TPC], BF16)
        w_in = _din(nc, "w_in", [D, 2 * D])
        w_grp = _din(nc, "w_grp", [4, 512, 512])
        bgrp = _din(nc, "bgrp", [128, 16])
        oscale = _din(nc, "oscale", [128, 16])
        invtab = _din(nc, "invtab", [4 * 16])
    else:
        yT_d = _din(nc, "yT", [D, TPC], BF16)
        ssq8 = _din(nc, "ssq8", [8, TPC])
        sel_d = _din(nc, "sel", [8, 2 * 128])

    NTOK = HALO + TPC
    XT_E = 16 * NTOK
    WB_E = 16 * 512
    big = nc.alloc_sbuf_tensor("big", [128, max(XT_E + 2 * WB_E, 16 * D)], BF16)
    wout_sb = big[:, 0:16 * D].rearrange("p (k n) -> p k n", k=16)
    yT = nc.alloc_sbuf_tensor("yT_sb", [128, 16, TPC], BF16)
    lng_sb = nc.alloc_sbuf_tensor("lng_sb", [128, D], F32)
    lnb_sb = nc.alloc_sbuf_tensor("lnb_sb", [128, D], F32)
    xr2 = [nc.alloc_sbuf_tensor("xr_sb%d" % i, [128, D], F32) for i in range(2)]
    z2 = [nc.alloc_sbuf_tensor("z_sb%d" % i, [128, D], F32) for i in range(2)]
    zT_sb = nc.alloc_sbuf_tensor("zT_sb", [128, 4, 512], BF16)
    xoT_v = xoT.rearrange("(k p) n -> p k n", p=128)
    stats = nc.alloc_sbuf_tensor("stats", [128, 4, 6], F32)
    mv = nc.alloc_sbuf_tensor("mv", [128, 2], F32)
    rstd = nc.alloc_sbuf_tensor("rstd", [128, 1], F32)
    nmr = nc.alloc_sbuf_tensor("nmr", [128, 1], F32)
    identf = nc.alloc_sbuf_tensor("identf_sb", [128, 128], F32)
    ps = [nc.alloc_psum_tensor("ps%d" % i, [128, 512], F32) for i in range(8)]

    P.dma("sp", lambda e: e.dma_start(out=lng_sb[:], in_=lng.partition_broadcast(128)), "c1", writes=["lng"])
    P.dma("sp", lambda e: e.dma_start(out=lnb_sb[:], in_=lnb.partition_broadcast(128)), "c2", writes=["lnb"])
    P.dma("sp", lambda e: e.dma_start(out=identf[:], in_=identf_d), "c3", writes=["identf"])

    if front:
        xT_sb = big[:, 0:XT_E].rearrange("p (k n) -> p k n", k=16)
        wbuf = [big[:, XT_E + i * WB_E: XT_E + (i + 1) * WB_E].rearrange("p (k n) -> p k n", k=16) for i in range(2)]
        wg_sb = nc.alloc_sbuf_tensor("wg_sb", [128, 4, 4, 512], BF16)
        bg_sb = nc.alloc_sbuf_tensor("bg_sb", [128, 16], F32)
        os_sb = nc.alloc_sbuf_tensor("os_sb", [128, 16], F32)
        inv_sb = nc.alloc_sbuf_tensor("inv_sb", [128, 4, 16], F32)
        vT = nc.alloc_sbuf_tensor("vT", [128, NTOK], F32)
        sA = nc.alloc_sbuf_tensor("sA", [128, NTOK], F32)
        sB = nc.alloc_sbuf_tensor("sB", [128, NTOK], F32)
        pooled = nc.alloc_sbuf_tensor("pooled", [128, 4, TPC], BF16)
        ge = nc.alloc_sbuf_tensor("ge", [128, TPC], F32)
        gs = nc.alloc_sbuf_tensor("gs", [128, TPC], F32)
        mt = nc.alloc_sbuf_tensor("mt", [128, TPC], F32)

        xT_v = xT.rearrange("(k p) n -> p k n", p=128)
        for h in range(4):
            P.dma("sp" if h % 2 == 0 else "act", lambda e, h=h: e.dma_start(out=xT_sb[:, 4 * h:4 * h + 4, :], in_=xT_v[:, 4 * h:4 * h + 4, :]),
                  "xT", writes=["xT"])
        wg_v = w_grp.rearrange("g (c p) d -> p g c d", p=128)
        P.dma("sp", lambda e: e.dma_start(out=bg_sb[:], in_=bgrp), "c4", writes=["bg"])
        P.dma("sp", lambda e: e.dma_start(out=os_sb[:], in_=oscale), "c5", writes=["os"])
        P.dma("sp", lambda e: e.dma_start(out=inv_sb[:].rearrange("p g i -> p (g i)"), in_=invtab.partition_broadcast(128)),
              "c6", writes=["inv"])
        w_in_v = w_in.rearrange("(k p) n -> p k n", p=128)

        def load_w(i, col0):
            b = i % 2
            P.dma("pool", lambda e: e.dma_start(out=wbuf[b][:, :, :], in_=w_in_v[:, :, col0:col0 + 512]),
                  "wbuf%d" % b, writes=["wbuf%d" % b])

        order = []
        for gi in range(4):
            order.append(gi * 512)
            order.append(D + gi * 512)
        load_w(0, order[0])
        for g in range(4):
            P.dma("pool", lambda e, g=g: e.dma_start(out=wg_sb[:, g, :, :], in_=wg_v[:, g, :, :]), "wg", writes=["wg"])
        load_w(1, order[1])

        for gi in range(4):
            wv = 2 * gi
            wgt = 2 * gi + 1
            w = 2 ** (gi + 1)
            for ft in range(4):
                b = wv % 2
                regs = [(ps[0], 0, HALO, 512), (ps[1], 0, HALO + 512, 512), (ps[2], 0, 0, HALO)]
                for (pt, pc, tc0, n) in regs:
                    for kc in range(16):
                        P.op("pe", lambda e, pt=pt, pc=pc, tc0=tc0, n=n, kc=kc, b=b, ft=ft: e.matmul(
                            pt[:, pc:pc + n], lhsT=wbuf[b][:, kc, ft * 128:(ft + 1) * 128],
                            rhs=xT_sb[:, kc, tc0:tc0 + n], start=(kc == 0), stop=(kc == 15)),
                            reads=["wbuf%d" % b, "xT"], writes=[PK(pt)])
                P.op("act", lambda e: e.activation(out=vT[:, HALO:HALO + 512], in_=ps[0][:, :], func=AF.Copy),
                     reads=["ps0"], writes=["vT"])
                P.op("act", lambda e: e.activation(out=vT[:, HALO + 512:NTOK], in_=ps[1][:, :], func=AF.Copy),
                     reads=["ps1"], writes=["vT"])
                P.op("act", lambda e: e.activation(out=vT[:, 0:HALO], in_=ps[2][:, 0:HALO], func=AF.Copy),
                     reads=["ps2"], writes=["vT"])
                src, skey = vT, "vT"
                dsts = [(sA, "sA"), (sB, "sB")]
                sh = 1
                lo = 0
                for st in range(gi + 1):
                    dst, dkey = dsts[st % 2]
                    lo = lo + sh
                    P.op("dve",
                         lambda e, dst=dst, src=src, lo=lo, sh=sh: e.tensor_tensor(
                             out=dst[:, lo:NTOK], in0=src[:, lo:NTOK], in1=src[:, lo - sh:NTOK - sh], op=ALU.add),
                         reads=[skey], writes=[dkey])
                    src, skey = dst, dkey
                    sh *= 2
                P.op("dve", lambda e, src=src, w=w, ft=ft: e.scalar_tensor_tensor(
                    out=pooled[:, ft, HALO:TPC], in0=src[:, 2 * HALO:NTOK], scalar=1.0 / w,
                    in1=vT[:, 2 * HALO:NTOK], op0=ALU.mult, op1=ALU.subtract),
                    reads=[skey, "vT"], writes=["pooled%d" % ft])
                P.op("dve", lambda e, src=src, gi=gi: e.tensor_tensor(
                    out=src[:, HALO:2 * HALO], in0=src[:, HALO:2 * HALO], in1=inv_sb[:, gi, :], op=ALU.mult),
                    reads=[skey, "inv"], writes=[skey])
                P.op("dve", lambda e, src=src, ft=ft: e.tensor_tensor(
                    out=pooled[:, ft, 0:HALO], in0=src[:, HALO:2 * HALO], in1=vT[:, HALO:2 * HALO], op=ALU.subtract),
                    reads=[skey, "vT"], writes=["pooled%d" % ft])
            if gi < 3:
                load_w(wv + 2, order[wv + 2])
            for dti in range(4):
                dt_ = gi * 4 + dti
                b = wgt % 2
                for hf in range(2):
                    pt = ps[3 + hf]
                    for kc in range(16):
                        P.op("pe", lambda e, pt=pt, hf=hf, kc=kc, b=b, dti=dti: e.matmul(
                            pt[:, :], lhsT=wbuf[b][:, kc, dti * 128:(dti + 1) * 128],
                            rhs=xT_sb[:, kc, HALO + hf * 512:HALO + (hf + 1) * 512],
                            start=(kc == 0), stop=(kc == 15)),
                            reads=["wbuf%d" % b, "xT"], writes=[PK(pt)])
                for hf in range(2):
                    pt = ps[5 + hf]
                    for cc in range(4):
                        P.op("pe", lambda e, pt=pt, hf=hf, cc=cc, gi=gi, dti=dti: e.matmul(
                            pt[:, :], lhsT=wg_sb[:, gi, cc, dti * 128:(dti + 1) * 128],
                            rhs=pooled[:, cc, hf * 512:(hf + 1) * 512], start=(cc == 0), stop=(cc == 3)),
                            reads=["wg", "pooled%d" % cc], writes=[PK(pt)])
                for hf in range(2):
                    sl = slice(hf * 512, (hf + 1) * 512)
                    P.op("act", lambda e, hf=hf, sl=sl: e.activation(out=ge[:, sl], in_=ps[3 + hf][:, :], func=AF.Exp, scale=-1.0),
                         reads=["ps%d" % (3 + hf)], writes=["ge%d" % hf])
                    P.op("act", lambda e, sl=sl: e.activation(out=ge[:, sl], in_=ge[:, sl], func=AF.Ln, bias=1.0, scale=1.0),
                         reads=["ge%d" % hf], writes=["ge%d" % hf])
                    P.op("act", lambda e, sl=sl: e.activation(out=ge[:, sl], in_=ge[:, sl], func=AF.Exp, scale=-1.0),
                         reads=["ge%d" % hf], writes=["ge%d" % hf])
                    P.op("dve", lambda e, hf=hf, sl=sl: e.tensor_tensor(out=gs[:, sl], in0=ge[:, sl], in1=ps[3 + hf][:, :], op=ALU.mult),
                         reads=["ge%d" % hf, "ps%d" % (3 + hf)], writes=["gs%d" % hf])
                    P.op("act", lambda e, hf=hf, sl=sl, dt_=dt_: e.activation(
                        out=mt[:, sl], in_=ps[5 + hf][:, :], func=AF.Identity,
                        bias=bg_sb[:, dt_:dt_ + 1], scale=1.0),
                        reads=["ps%d" % (5 + hf), "bg"], writes=["mt%d" % hf])
                    P.op("dve", lambda e, sl=sl, dt_=dt_: e.scalar_tensor_tensor(
                        out=yT[:, dt_, sl], in0=mt[:, sl], scalar=os_sb[:, dt_:dt_ + 1], in1=gs[:, sl],
                        op0=ALU.mult, op1=ALU.mult),
                        reads=["mt%d" % hf, "gs%d" % hf, "os"], writes=["yT"])
            if gi < 3:
                load_w(wgt + 2, order[wgt + 2])
    else:
        ssq_sb = nc.alloc_sbuf_tensor("ssq_sb", [8, TPC], F32)
        sel_sb = nc.alloc_sbuf_tensor("sel_sb", [8, 256], F32)
        rs_sb = nc.alloc_sbuf_tensor("rs_sb", [128, 2, TPC], F32)
        yT_v = yT_d.rearrange("(k p) n -> p k n", p=128)
        for h in range(4):
            P.dma("sp", lambda e, h=h: e.dma_start(out=yT[:, 4 * h:4 * h + 4, :], in_=yT_v[:, 4 * h:4 * h + 4, :]),
                  "yT", writes=["yT"])
        P.dma("sp", lambda e: e.dma_start(out=ssq_sb[:], in_=ssq8), "c7", writes=["ssq"])
        P.dma("sp", lambda e: e.dma_start(out=sel_sb[:], in_=sel_d), "c8", writes=["sel"])
        for g in range(2):
            for hf in range(2):
                pt = ps[2 * g + hf]
                P.op("pe", lambda e, pt=pt, g=g, hf=hf: e.matmul(
                    pt[:, :], lhsT=sel_sb[:, g * 128:(g + 1) * 128], rhs=ssq_sb[:, hf * 512:(hf + 1) * 512],
                    start=True, stop=True), reads=["sel", "ssq"], writes=[PK(pt)])
                P.op("act", lambda e, pt=pt, g=g, hf=hf: e.activation(
                    out=rs_sb[:, g, hf * 512:(hf + 1) * 512], in_=pt[:, :], func=AF.Ln, bias=EPS, scale=1.0 / 512),
                    reads=[PK(pt)], writes=["rs%d%d" % (g, hf)])
                P.op("act", lambda e, g=g, hf=hf: e.activation(
                    out=rs_sb[:, g, hf * 512:(hf + 1) * 512], in_=rs_sb[:, g, hf * 512:(hf + 1) * 512],
                    func=AF.Exp, scale=-0.5),
                    reads=["rs%d%d" % (g, hf)], writes=["rs%d%d" % (g, hf)])
            for kc in range(4):
                k = 8 + 4 * g + kc
                P.op("dve", lambda e, k=k, g=g: e.tensor_tensor(
                    out=yT[:, k, :], in0=yT[:, k, :], in1=rs_sb[:, g, :], op=ALU.mult),
                    reads=["yT", "rs%d0" % g, "rs%d1" % g], writes=["yT"])

    w_out_v = w_out.rearrange("(k p) n -> p k n", p=128)
    fkeys = ["xT", "wbuf0", "wbuf1"] if front else []
    for h in range(4):
        P.dma("pool", lambda e, h=h: e.dma_start(out=wout_sb[:, 4 * h:4 * h + 4, :], in_=w_out_v[:, 4 * h:4 * h + 4, :]),
              "wout", writes=["wout"] + fkeys)
    def tail_mm(tt):
        q = tt % 2
        P.dma("sp", lambda e: e.dma_start(out=xr2[q][:], in_=xres[tt * 128:(tt + 1) * 128, :]), "xr%d" % q,
              writes=["xr%d" % q])
        for nb in range(4):
            pt = ps[4 * q + nb]
            for kc in range(16):
                P.op("pe", lambda e, pt=pt, kc=kc, nb=nb: e.matmul(
                    pt[:, :], lhsT=yT[:, kc, tt * 128:(tt + 1) * 128], rhs=wout_sb[:, kc, nb * 512:(nb + 1) * 512],
                    start=(kc == 0), stop=(kc == 15)), reads=["yT", "wout"], writes=[PK(pt)])

    def tail_post(tt, after_ln=None):
        q = tt % 2
        z_sb = z2[q]
        zk = ["z%d_%d" % (q, i) for i in range(4)]
        for nb in range(4):
            pt = ps[4 * q + nb]
            P.op("dve", lambda e, pt=pt, nb=nb: e.scalar_tensor_tensor(
                out=z_sb[:, nb * 512:(nb + 1) * 512], in0=xr2[q][:, nb * 512:(nb + 1) * 512], scalar=ALPHA,
                in1=pt[:, :], op0=ALU.mult, op1=ALU.add),
                reads=["xr%d" % q, PK(pt)], writes=[zk[nb]])
            P.op("dve", lambda e, nb=nb: e.bn_stats(out=stats[:, nb, :], in_=z_sb[:, nb * 512:(nb + 1) * 512]),
                 reads=[zk[nb]], writes=["stats%d" % nb])
        if after_ln is not None:
            after_ln()
        P.op("dve", lambda e: e.bn_aggr(out=mv[:], in_=stats[:].rearrange("p a b -> p (a b)")),
             reads=["stats%d" % i for i in range(4)], writes=["mv"])
        P.op("act", lambda e: e.activation(out=rstd[:], in_=mv[:, 1:2], func=AF.Ln, bias=EPS, scale=1.0),
             reads=["mv"], writes=["rstd"])
        P.op("act", lambda e: e.activation(out=rstd[:], in_=rstd[:], func=AF.Exp, scale=-0.5),
             reads=["rstd"], writes=["rstd"])
        P.op("dve", lambda e: e.scalar_tensor_tensor(out=nmr[:], in0=mv[:, 0:1], scalar=-1.0, in1=rstd[:],
                                                     op0=ALU.mult, op1=ALU.mult),
             reads=["mv", "rstd"], writes=["nmr"])
        P.op("act", lambda e: e.activation(out=z_sb[:], in_=z_sb[:], func=AF.Identity, bias=nmr[:], scale=rstd[:]),
             reads=zk + ["nmr", "rstd"], writes=zk)
        P.op("dve", lambda e: e.tensor_tensor(out=z_sb[:], in0=z_sb[:], in1=lng_sb[:], op=ALU.mult),
             reads=zk + ["lng"], writes=zk)
        P.op("dve", lambda e: e.tensor_tensor(out=z_sb[:], in0=z_sb[:], in1=lnb_sb[:], op=ALU.add),
             reads=zk + ["lnb"], writes=zk)
        P.dma("sp", lambda e: e.dma_start(out=xo[tt * 128:(tt + 1) * 128, :], in_=z_sb[:]), "xo%d" % q,
              reads=zk, is_output=True)
        for grp in range(4):
            pk = "ps%d" % (4 * q + grp)
            pt = ps[4 * q + grp]
            for j in range(4):
                kc = grp * 4 + j
                P.op("pe", lambda e, pt=pt, j=j, kc=kc: e.transpose(
                    pt[:, j * 128:(j + 1) * 128], z_sb[:, kc * 128:(kc + 1) * 128], identf[:]),
                    reads=zk + ["identf"], writes=[pk])
            P.op("act", lambda e, pt=pt, grp=grp: e.activation(out=zT_sb[:, grp, :], in_=pt[:, :], func=AF.Copy),
                 reads=[pk], writes=["zT%d" % grp])
            P.dma("act", lambda e, grp=grp: e.dma_start(
                out=xoT_v[:, grp * 4:grp * 4 + 4, tt * 128:(tt + 1) * 128],
                in_=zT_sb[:, grp, :].rearrange("p (k n) -> p k n", k=4)), "xoT%d" % grp,
                reads=["zT%d" % grp], is_output=True)

    tail_mm(0)
    for tt in range(NT):
        nxt = (lambda t=tt: tail_mm(t + 1)) if tt + 1 < NT else None
        tail_post(tt, after_ln=nxt)

    P.emit()
    return nc


def build_even(x_bf16, nblk=S // 512):
    nc = bass.Bass("TRN2", target_bir_lowering=False)
    P = Prog(nc)
    NBLK = nblk
    S = nblk * 512
    xT = _din(nc, "xT", [D, S], BF16 if x_bf16 else F32)
    wA = _din(nc, "wA", [D, NPJ])
    cw_d = _din(nc, "cw", [128, 12])
    cb_d = _din(nc, "cb", [128, 3])
    colp_d = _din(nc, "colp", [128, 8])
    dtp_d = _din(nc, "dtp", [2, 2])
    lamv_d = _din(nc, "lamv", [256])
    identf_d = _din(nc, "identf", [128, 128])
    mask2_d = _din(nc, "mask2", [128, 64])
    bdm_d = _din(nc, "bdmask", [128, 128])
    hsel_d = _din(nc, "hsel", [2, 128])
    eye2_d = _din(nc, "eye2", [2, 2])
    yatt = _dout(nc, "yatt", [128, S], BF16)
    yssm = _dout(nc, "yssm", [128, S], BF16)
    ssq = _dout(nc, "ssq", [1, S])

    A = nc.alloc_sbuf_tensor
    W_sb = A("W_sb", [128, 16, NPJ], BF16)
    xb = [A("xb%d" % i, [128, 16, 512], BF16) for i in range(2)]
    KT = A("KT", [128, S], BF16)
    V = A("V", [128, S // 128, 128], BF16)
    QT = A("QT", [128, 512], BF16)
    sg = A("sg", [128, 512], F32)
    sz = A("sz", [128, 512], F32)
    raw = A("raw", [128, 3, 515], F32)
    xsT = A("xsT", [128, 512], F32)
    xsdup = A("xsdup", [128, 1024], BF16)
    BTdup = A("BTdup", [128, 1024], BF16)
    CT = A("CT", [128, 512], BF16)
    tmp = {k: A("t" + k, [128, 512], F32) for k in "ABCDEF"}
    E = [[A("E%d%d" % (i, m), [128, 512], BF16) for m in range(2)] for i in range(2)]
    ysT = A("ysT", [128, 512], F32)
    yatt_sb = A("yatt_sb", [128, 512], BF16)
    yssm_sb = A("yssm_sb", [128, 512], BF16)
    ssq_sb = A("ssq_sb", [1, 512], F32)
    cw = A("cw_sb", [128, 12], F32)
    cb = A("cb_sb", [128, 3], F32)
    colp = A("colp_sb", [128, 8], F32)
    lamv = A("lamv_sb", [128, 256], F32)
    lamt = A("lamt", [128, 128], F32)
    lams = A("lams", [128, 4], F32)
    identb = A("identb", [128, 128], BF16)
    mask2 = A("mask2_sb", [128, 64], F32)
    bdm = A("bdm_sb", [128, 128], F32)
    ones_b = A("ones_b", [128, 128], BF16)
    ones_f = A("ones_f", [128, 128], F32)
    hsel = A("hsel_sb", [2, 128], F32)
    eye2 = A("eye2_sb", [2, 2], F32)
    dtp = A("dtp_sb", [2, 2], F32)
    Acol = A("Acol", [2, 1], F32)
    dt_sb = A("dt_sb", [2, 512], F32)
    a_sb = A("a_sb", [2, 512], F32)
    acs = A("acs", [2, 512], F32)
    dtBD = A("dtBD", [2, 1024], F32)
    acsBD = A("acsBD", [2, 1024], F32)
    lastbc = A("lastbc", [2, 8, 128], F32)
    cols = A("cols", [128, 16], F32)
    negc = A("negc", [128, 8], F32)
    Sst = A("Sst", [128, 128], F32)
    Sbf = A("Sbf", [128, 128], BF16)
    Gm = A("Gm", [128, 64], F32)
    Dm = A("Dm", [128, 64], F32)
    E2 = A("E2", [128, 64], F32)
    M2 = A("M2", [128, 64], BF16)
    R2e = A("R2e", [128, 64], F32)
    XdtBD = A("XdtBD", [128, 128], BF16)
    XdecBD = A("XdecBD", [128, 128], BF16)
    Btok = A("Btok", [128, 128], BF16)
    CD = A("CD", [128, 128], F32)
    y1 = A("y1", [128, 64], F32)
    ps = [nc.alloc_psum_tensor("ps%d" % i, [128, 512], F32) for i in range(8)]
    psS = [ps[0], ps[1]]
    psO = [ps[2], ps[3]]
    psD = [ps[4], ps[5]]
    rG, rR, rX, rcol = ps[6][:, 0:64], ps[6][:, 64:128], ps[6][:, 128:256], ps[6][:, 256:272]
    rB, rC, rS = ps[7][:, 0:128], ps[7][:, 128:256], ps[7][:, 256:384]
    rYo, rYd = ps[7][:, 384:448], ps[7][:, 448:512]

    def ld(q, dst, src, key):
        P.dma(q, lambda e: e.dma_start(out=dst, in_=src), key, writes=[key])

    ld("sp", cw[:], cw_d, "cw")
    ld("sp", cb[:], cb_d, "cb")
    ld("sp", colp[:], colp_d, "colp")
    ld("sp", dtp[:], dtp_d, "dtp")
    ld("sp", lamv[:], lamv_d.partition_broadcast(128), "lamv")
    ld("sp", mask2[:], mask2_d, "mask2")
    ld("sp", bdm[:], bdm_d, "bdm")
    ld("sp", hsel[:], hsel_d, "hsel")
    ld("sp", eye2[:], eye2_d, "eye2")
    ld("pool", identb[:], identf_d, "identb")
    P.op("pool", lambda e: e.memset(ones_b[:], 1.0), writes=["ones_b"])
    P.op("pool", lambda e: e.memset(ones_f[:], 1.0), writes=["ones_f"])
    P.op("pool", lambda e: e.memset(raw[:], 0.0), writes=["raw0", "raw1", "raw2"])
    P.op("pool", lambda e: e.memset(Sst[:], 0.0), writes=["Sst"])
    wA_v = wA.rearrange("(k p) n -> p k n", p=128)
    for h in range(4):
        P.dma("pool", lambda e, h=h: e.dma_start(out=W_sb[:, 4 * h:4 * h + 4, :], in_=wA_v[:, 4 * h:4 * h + 4, :]),
              "W", writes=["W"])
    xT_v = xT.rearrange("(k p) n -> p k n", p=128)

    def load_x(b):
        i = b % 2
        for h in range(2):
            if x_bf16:
                P.dma("sp" if h == 0 else "act", lambda e, h=h: e.dma_start(
                    out=xb[i][:, 8 * h:8 * h + 8, :], in_=xT_v[:, 8 * h:8 * h + 8, b * 512:(b + 1) * 512]),
                    "xb%d" % i, writes=["xb%d" % i])
            else:
                P.dma("pool", lambda e, h=h: e.dma_start(
                    out=xb[i][:, 8 * h:8 * h + 8, :], in_=xT_v[:, 8 * h:8 * h + 8, b * 512:(b + 1) * 512]),
                    "xb%d" % i, writes=["xb%d" % i])

    P.op("dve", lambda e: e.tensor_tensor(out=lamt[:, 0:64], in0=lamv[:, 0:64], in1=lamv[:, 64:128], op=ALU.mult),
         reads=["lamv"], writes=["lamt0"])
    P.op("dve", lambda e: e.tensor_tensor(out=lamt[:, 64:128], in0=lamv[:, 128:192], in1=lamv[:, 192:256], op=ALU.mult),
         reads=["lamv"], writes=["lamt1"])
    P.op("dve", lambda e: e.reduce_sum(out=lams[:, 0:1], in_=lamt[:, 0:64], axis=mybir.AxisListType.X),
         reads=["lamt0"], writes=["lams0"])
    P.op("dve", lambda e: e.reduce_sum(out=lams[:, 1:2], in_=lamt[:, 64:128], axis=mybir.AxisListType.X),
         reads=["lamt1"], writes=["lams1"])
    P.op("act", lambda e: e.activation(out=lams[:, 0:2], in_=lams[:, 0:2], func=AF.Exp),
         reads=["lams0", "lams1"], writes=["lams0", "lams1"])
    P.op("dve", lambda e: e.tensor_tensor(out=lams[:, 2:3], in0=lams[:, 1:2], in1=lams[:, 0:1], op=ALU.subtract),
         reads=["lams0", "lams1"], writes=["neglam"])
    P.op("dve", lambda e: e.tensor_tensor(out=lams[:, 2:3], in0=lams[:, 2:3], in1=colp[:, 3:4], op=ALU.subtract),
         reads=["neglam", "colp"], writes=["neglam"])
    P.op("dve", lambda e: e.tensor_tensor(out=lams[:, 3:4], in0=colp[:, 2:3], in1=colp[:, 4:5], op=ALU.mult),
         reads=["colp"], writes=["coef"])
    P.op("act", lambda e: e.activation(out=Acol[:], in_=dtp[:, 1:2], func=AF.Exp), reads=["dtp"], writes=["Acol"])
    P.op("dve", lambda e: e.tensor_scalar(out=Acol[:], in0=Acol[:], scalar1=-1.0, scalar2=None, op0=ALU.mult),
         reads=["Acol"], writes=["Acol"])

    def silu_from(src_ap, src_keys, dst_ap, dst_key, t, tkey):
        P.op("act", lambda e: e.activation(out=t, in_=src_ap, func=AF.Exp, scale=-1.0), reads=src_keys, writes=[tkey])
        P.op("act", lambda e: e.activation(out=t, in_=t, func=AF.Ln, bias=1.0, scale=1.0), reads=[tkey], writes=[tkey])
        P.op("act", lambda e: e.activation(out=t, in_=t, func=AF.Exp, scale=-1.0), reads=[tkey], writes=[tkey])
        P.op("dve", lambda e: e.tensor_tensor(out=dst_ap, in0=t, in1=src_ap, op=ALU.mult),
             reads=[tkey] + list(src_keys), writes=[dst_key])

    def inproj(b):
        xbi = xb[b % 2]
        xk = "xb%d" % (b % 2)
        bank = [0]

        def nextbank():
            bank[0] ^= 1
            return ps[bank[0]], "ps%d" % bank[0]

        def proj(c0):
            pt, pk = nextbank()
            for kc in range(16):
                P.op("pe", lambda e, kc=kc: e.matmul(pt[:, :], lhsT=W_sb[:, kc, c0:c0 + 128], rhs=xbi[:, kc, :],
                                                     start=(kc == 0), stop=(kc == 15)),
                     reads=["W", xk], writes=[pk])
            return pt, pk

        pt, pk = proj(0)
        P.op("act", lambda e, pt=pt: e.activation(out=QT[:], in_=pt[:, :], func=AF.Copy), reads=[pk], writes=["QT"])
        pt, pk = proj(128)
        P.op("act", lambda e, pt=pt: e.activation(out=KT[:, b * 512:(b + 1) * 512], in_=pt[:, :], func=AF.Copy),
             reads=[pk], writes=["KT%d" % b])
        pt, pk = nextbank()
        for i in range(4):
            for kc in range(16):
                P.op("pe", lambda e, pt=pt, i=i, kc=kc: e.matmul(
                    pt[:, i * 128:(i + 1) * 128], lhsT=xbi[:, kc, i * 128:(i + 1) * 128], rhs=W_sb[:, kc, 256:384],
                    start=(kc == 0), stop=(kc == 15)), reads=["W", xk], writes=[pk])
        P.op("act", lambda e, pt=pt: e.activation(
            out=V[:, b * 4:(b + 1) * 4, :].rearrange("p a d -> p (a d)"), in_=pt[:, :], func=AF.Copy),
            reads=[pk], writes=["V%d" % b])
        pt, pk = proj(384)
        silu_from(pt[:, :], [pk], sg[:], "sg", tmp["A"][:], "tA")
        pt, pk = proj(512)
        silu_from(pt[:, :], [pk], sz[:], "sz", tmp["B"][:], "tB")
        for t in range(3):
            pt, pk = proj(640 + 128 * t)
            rk = "raw%d" % t
            P.op("act", lambda e, pt=pt, t=t: e.activation(out=raw[:, t, 3:515], in_=pt[:, :], func=AF.Copy),
                 reads=[pk], writes=[rk])
            acc = tmp["C"][:]
            P.op("dve", lambda e, t=t: e.tensor_scalar(
                out=acc, in0=raw[:, t, 3:515], scalar1=cw[:, 4 * t + 3:4 * t + 4], scalar2=cb[:, t:t + 1],
                op0=ALU.mult, op1=ALU.add), reads=[rk, "cw", "cb"], writes=["tC"])
            for j in range(3):
                P.op("dve", lambda e, t=t, j=j: e.scalar_tensor_tensor(
                    out=acc, in0=raw[:, t, j:j + 512], scalar=cw[:, 4 * t + j:4 * t + j + 1], in1=acc,
                    op0=ALU.mult, op1=ALU.add), reads=[rk, "cw", "tC"], writes=["tC"])
            P.op("dve", lambda e, t=t: e.tensor_copy(out=raw[:, t, 0:3], in_=raw[:, t, 512:515]),
                 reads=[rk, "tC"], writes=[rk])
            if t == 0:
                silu_from(acc, ["tC"], xsT[:], "xsT", tmp["D"][:], "tD")
                for u in range(2):
                    P.op("dve", lambda e, u=u: e.tensor_copy(
                        out=xsdup[:].rearrange("p (c u l) -> p c u l", c=8, u=2)[:, :, u, :],
                        in_=xsT[:].rearrange("p (c l) -> p c l", c=8)), reads=["xsT"], writes=["xsdup"])
            elif t == 1:
                silu_from(acc, ["tC"], tmp["E"][:], "tE", tmp["D"][:], "tD")
                for u in range(2):
                    P.op("dve", lambda e, u=u: e.tensor_copy(
                        out=BTdup[:].rearrange("p (c u l) -> p c u l", c=8, u=2)[:, :, u, :],
                        in_=tmp["E"][:].rearrange("p (c l) -> p c l", c=8)), reads=["tE"], writes=["BTdup"])
            else:
                silu_from(acc, ["tC"], CT[:], "CT", tmp["D"][:], "tD")
        pt, pk = nextbank()
        for kc in range(16):
            P.op("pe", lambda e, pt=pt, kc=kc: e.matmul(pt[0:2, :], lhsT=W_sb[:, kc, 1024:1026], rhs=xbi[:, kc, :],
                                                        start=(kc == 0), stop=(kc == 15)),
                 reads=["W", xk], writes=[pk])
        P.op("act", lambda e, pt=pt: e.activation(out=dt_sb[:], in_=pt[0:2, :], func=AF.Exp, bias=dtp[:, 0:1], scale=1.0),
             reads=[pk, "dtp"], writes=["dt"])
        P.op("act", lambda e: e.activation(out=dt_sb[:], in_=dt_sb[:], func=AF.Ln, bias=1.0, scale=1.0),
             reads=["dt"], writes=["dt"])
        P.op("dve", lambda e: e.tensor_scalar(out=a_sb[:], in0=dt_sb[:], scalar1=Acol[:, 0:1], scalar2=None, op0=ALU.mult),
             reads=["dt", "Acol"], writes=["a"])
        for c in range(8):
            P.op("dve", lambda e, c=c: e.tensor_tensor_scan(
                out=acs[:, c * 64:(c + 1) * 64], data0=ones_f[0:2, 0:64], data1=a_sb[:, c * 64:(c + 1) * 64],
                initial=0.0, op0=ALU.mult, op1=ALU.add), reads=["a", "ones_f"], writes=["acs"])
        for h in range(2):
            P.op("dve", lambda e, h=h: e.tensor_scalar(
                out=dtBD[:].rearrange("p (c u l) -> p c u l", c=8, u=2)[:, :, h, :],
                in0=dt_sb[:].rearrange("p (c l) -> p c l", c=8), scalar1=eye2[:, h:h + 1], scalar2=None, op0=ALU.mult),
                reads=["dt", "eye2"], writes=["dtBD"])
            P.op("dve", lambda e, h=h: e.tensor_scalar(
                out=acsBD[:].rearrange("p (c u l) -> p c u l", c=8, u=2)[:, :, h, :],
                in0=acs[:].rearrange("p (c l) -> p c l", c=8), scalar1=eye2[:, h:h + 1], scalar2=None, op0=ALU.mult),
                reads=["acs", "eye2"], writes=["acsBD"])
        for c in range(8):
            P.op("dve", lambda e, c=c: e.tensor_scalar(
                out=lastbc[:, c, :], in0=ones_f[0:2, :], scalar1=acs[:, c * 64 + 63:c * 64 + 64], scalar2=None, op0=ALU.mult),
                reads=["acs", "ones_f"], writes=["lastbc"])

    def attention_steps(b):
        nkb = 4 * b + 4

        def scores(kb):
            j = kb - 4 * b
            c0 = max(0, j) * 128
            st = kb % 2
            for m in range(2):
                P.op("pe", lambda e, kb=kb, m=m, c0=c0: e.matmul(
                    psS[m][:, c0:512], lhsT=KT[m * 64:(m + 1) * 64, kb * 128:(kb + 1) * 128],
                    rhs=QT[m * 64:(m + 1) * 64, c0:512], start=True, stop=True),
                    reads=["KT%d" % (kb // 4), "QT"], writes=["ps%d" % m])
                P.op("act", lambda e, m=m, c0=c0, st=st: e.activation(
                    out=E[st][m][:, c0:512], in_=psS[m][:, c0:512], func=AF.Exp, scale=0.125),
                    reads=["ps%d" % m], writes=["E%d%d" % (st, m)])
                if j >= 0:
                    P.op("pool", lambda e, m=m, c0=c0, st=st: e.memset(E[st][m][64:128, c0:c0 + 64], 0.0),
                         reads=[], writes=["E%d%d" % (st, m)])

        def accum(kb):
            j = kb - 4 * b
            c0 = max(0, j) * 128
            st = kb % 2
            for m in range(2):
                P.op("pe", lambda e, kb=kb, m=m, c0=c0, st=st: e.matmul(
                    psO[m][:, c0:512], lhsT=V[:, kb, :], rhs=E[st][m][:, c0:512],
                    start=(kb == 0), stop=(kb == nkb - 1)),
                    reads=["V%d" % (kb // 4), "E%d%d" % (st, m)], writes=["ps%d" % (2 + m)])
                P.op("pe", lambda e, kb=kb, m=m, c0=c0, st=st: e.matmul(
                    psD[m][:, c0:512], lhsT=ones_b[:], rhs=E[st][m][:, c0:512],
                    start=(kb == 0), stop=(kb == nkb - 1)),
                    reads=["ones_b", "E%d%d" % (st, m)], writes=["ps%d" % (4 + m)])

        scores(0)
        for kb in range(nkb):
            if kb + 1 < nkb:
                scores(kb + 1)
            accum(kb)
            yield
        tA, tB, tC, tD = tmp["A"][:], tmp["B"][:], tmp["C"][:], tmp["D"][:]
        P.op("act", lambda e: e.activation(out=tA, in_=psD[0][:, :], func=AF.Ln), reads=["ps4"], writes=["tA"])
        P.op("act", lambda e: e.activation(out=tB, in_=psD[1][:, :], func=AF.Ln), reads=["ps5"], writes=["tB"])
        P.op("act", lambda e: e.activation(out=tA, in_=tA, func=AF.Exp, scale=-1.0), reads=["tA"], writes=["tA"])
        P.op("act", lambda e: e.activation(out=tB, in_=tB, func=AF.Exp, scale=-1.0), reads=["tB"], writes=["tB"])
        P.op("dve", lambda e: e.tensor_tensor(out=tA, in0=tA, in1=psO[0][:, :], op=ALU.mult), reads=["tA", "ps2"], writes=["tA"])
        P.op("dve", lambda e: e.tensor_tensor(out=tB, in0=tB, in1=psO[1][:, :], op=ALU.mult), reads=["tB", "ps3"], writes=["tB"])
        P.op("dve", lambda e: e.scalar_tensor_tensor(out=tA, in0=tB, scalar=lams[:, 2:3], in1=tA, op0=ALU.mult, op1=ALU.add),
             reads=["tA", "tB", "neglam"], writes=["tA"])
        P.op("act", lambda e: e.activation(out=tB, in_=tA, func=AF.Square), reads=["tA"], writes=["tB"])
        P.op("pe", lambda e: e.matmul(psS[0][:, :], lhsT=ones_f[:], rhs=tB, start=True, stop=True),
             reads=["ones_f", "tB"], writes=["ps0"])
        P.op("act", lambda e: e.activation(out=tB, in_=psS[0][:, :], func=AF.Ln, bias=EPS, scale=1.0 / 128),
             reads=["ps0"], writes=["tB"])
        P.op("act", lambda e: e.activation(out=tB, in_=tB, func=AF.Exp, scale=-0.5), reads=["tB"], writes=["tB"])
        P.op("dve", lambda e: e.tensor_tensor(out=tA, in0=tA, in1=tB, op=ALU.mult), reads=["tA", "tB"], writes=["tA"])
        P.op("dve", lambda e: e.scalar_tensor_tensor(out=yatt_sb[:], in0=tA, scalar=lams[:, 3:4], in1=sg[:],
                                                     op0=ALU.mult, op1=ALU.mult),
             reads=["tA", "coef", "sg"], writes=["yatt_sb"])
        P.dma("sp", lambda e: e.dma_start(out=yatt[:, b * 512:(b + 1) * 512], in_=yatt_sb[:]), "yatt",
              reads=["yatt_sb"], is_output=True)
        yield

    def ssd_steps(b):
        for c in range(8):
            P.op("pe", lambda e, c=c: e.matmul(rcol[:, 2 * c:2 * c + 1], lhsT=dtBD[:, c * 128:(c + 1) * 128],
                                               rhs=ones_f[0:2, 0:1], start=True, stop=True),
                 reads=["dtBD", "ones_f"], writes=["ps6"])
            P.op("pe", lambda e, c=c: e.matmul(rcol[:, 2 * c + 1:2 * c + 2], lhsT=acsBD[:, c * 128:(c + 1) * 128],
                                               rhs=ones_f[0:2, 0:1], start=True, stop=True),
                 reads=["acsBD", "ones_f"], writes=["ps6"])
        P.op("act", lambda e: e.activation(out=cols[:], in_=rcol, func=AF.Copy), reads=["ps6"], writes=["cols"])
        P.op("dve", lambda e: e.tensor_scalar(
            out=negc[:], in0=cols[:].rearrange("p (c t) -> p c t", t=2)[:, :, 1], scalar1=-1.0, scalar2=None, op0=ALU.mult),
            reads=["cols"], writes=["negc"])
        for c in range(8):
            cs = slice(c * 64, (c + 1) * 64)
            cd = slice(c * 128, (c + 1) * 128)
            P.op("pe", lambda e, cs=cs, cd=cd: e.matmul(rG, lhsT=BTdup[:, cd], rhs=CT[:, cs], start=True, stop=True),
                 reads=["BTdup", "CT"], writes=["ps6"])
            P.op("pe", lambda e, cs=cs: e.matmul(rR, lhsT=hsel[:], rhs=acs[:, cs], start=True, stop=True),
                 reads=["hsel", "acs"], writes=["ps6"])
            P.op("pe", lambda e, cd=cd: e.matmul(rX, lhsT=xsdup[:, cd], rhs=identb[:], start=True, stop=True),
                 reads=["xsdup", "identb"], writes=["ps6"])
            P.op("pe", lambda e, c=c: e.matmul(rC, lhsT=lastbc[:, c, :], rhs=hsel[:], start=True, stop=True),
                 reads=["lastbc", "hsel"], writes=["ps7"])
            P.op("pe", lambda e, cd=cd: e.matmul(rB, lhsT=BTdup[:, cd], rhs=identb[:], start=True, stop=True),
                 reads=["BTdup", "identb"], writes=["ps7"])
            P.op("dve", lambda e: e.tensor_tensor(out=Gm[:], in0=rG, in1=mask2[:], op=ALU.mult),
                 reads=["ps6", "mask2"], writes=["Gm"])
            P.op("dve", lambda e, c=c: e.tensor_scalar(out=Dm[:], in0=rR, scalar1=negc[:, c:c + 1], scalar2=0.0,
                                                       op0=ALU.add, op1=ALU.min),
                 reads=["ps6", "negc"], writes=["Dm"])
            P.op("act", lambda e: e.activation(out=E2[:], in_=Dm[:], func=AF.Exp), reads=["Dm"], writes=["E2"])
            P.op("act", lambda e: e.activation(out=R2e[:], in_=rR, func=AF.Exp), reads=["ps6"], writes=["R2e"])
            P.op("dve", lambda e, c=c: e.scalar_tensor_tensor(
                out=XdtBD[:], in0=rX, scalar=cols[:, 2 * c:2 * c + 1], in1=bdm[:], op0=ALU.mult, op1=ALU.mult),
                reads=["ps6", "cols", "bdm"], writes=["XdtBD"])
            P.op("dve", lambda e: e.tensor_tensor(out=M2[:], in0=Gm[:], in1=E2[:], op=ALU.mult),
                 reads=["Gm", "E2"], writes=["M2"])
            P.op("act", lambda e: e.activation(out=XdecBD[:], in_=XdtBD[:], func=AF.Identity, scale=E2[:, 63:64]),
                 reads=["XdtBD", "E2"], writes=["XdecBD"])
            P.op("act", lambda e: e.activation(out=Btok[:], in_=rB, func=AF.Copy), reads=["ps7"], writes=["Btok"])
            P.op("act", lambda e: e.activation(out=CD[:], in_=rC, func=AF.Exp), reads=["ps7"], writes=["CD"])
            P.op("act", lambda e: e.activation(out=Sbf[:], in_=Sst[:], func=AF.Copy), reads=["Sst"], writes=["Sbf"])
            P.op("pe", lambda e, cs=cs: e.matmul(rYo, lhsT=Sbf[:], rhs=CT[:, cs], start=True, stop=True),
                 reads=["Sbf", "CT"], writes=["ps7"])
            P.op("pe", lambda e: e.matmul(rYd, lhsT=XdtBD[:], rhs=M2[:], start=True, stop=True),
                 reads=["XdtBD", "M2"], writes=["ps7"])
            P.op("pe", lambda e: e.matmul(rS, lhsT=Btok[:], rhs=XdecBD[:], start=True, stop=True),
                 reads=["Btok", "XdecBD"], writes=["ps7"])
            P.op("dve", lambda e: e.tensor_tensor(out=y1[:], in0=rYo, in1=R2e[:], op=ALU.mult),
                 reads=["ps7", "R2e"], writes=["y1"])
            P.op("dve", lambda e, cs=cs: e.tensor_tensor(out=ysT[:, cs], in0=y1[:], in1=rYd, op=ALU.add),
                 reads=["y1", "ps7"], writes=["ysT"])
            P.op("dve", lambda e: e.tensor_tensor(out=Sst[:], in0=Sst[:], in1=CD[:], op=ALU.mult),
                 reads=["Sst", "CD"], writes=["Sst"])
            P.op("dve", lambda e: e.tensor_tensor(out=Sst[:], in0=Sst[:], in1=rS, op=ALU.add),
                 reads=["Sst", "ps7"], writes=["Sst"])
            yield
        tE, tF = tmp["E"][:], tmp["F"][:]
        P.op("dve", lambda e: e.scalar_tensor_tensor(out=tE, in0=xsT[:], scalar=colp[:, 0:1], in1=ysT[:],
                                                     op0=ALU.mult, op1=ALU.add),
             reads=["xsT", "colp", "ysT"], writes=["tE"])
        P.op("dve", lambda e: e.tensor_tensor(out=tE, in0=tE, in1=sz[:], op=ALU.mult), reads=["tE", "sz"], writes=["tE"])
        P.op("act", lambda e: e.activation(out=tF, in_=tE, func=AF.Square), reads=["tE"], writes=["tF"])
        P.op("pe", lambda e: e.matmul(ps[6][0:1, :], lhsT=ones_f[:, 0:1], rhs=tF, start=True, stop=True),
             reads=["ones_f", "tF"], writes=["ps6"])
        P.op("act", lambda e: e.activation(out=ssq_sb[:], in_=ps[6][0:1, :], func=AF.Copy), reads=["ps6"], writes=["ssq_sb"])
        P.op("act", lambda e: e.activation(out=yssm_sb[:], in_=tE, func=AF.Identity, scale=colp[:, 1:2]),
             reads=["tE", "colp"], writes=["yssm_sb"])
        P.dma("sp", lambda e: e.dma_start(out=ssq[0:1, b * 512:(b + 1) * 512], in_=ssq_sb[:]), "ssqo",
              reads=["ssq_sb"], is_output=True)
        P.dma("sp", lambda e: e.dma_start(out=yssm[:, b * 512:(b + 1) * 512], in_=yssm_sb[:]), "yssm",
              reads=["yssm_sb"], is_output=True)
        yield

    def interleave(b):
        att = attention_steps(b)
        sd = ssd_steps(b)
        nk = 4 * b + 5
        ns = 9
        ia = isd = 0
        for _ in att:
            pass
        for _ in sd:
            pass
        return
        while ia < nk or isd < ns:
            if isd >= ns or (ia < nk and ia * ns <= isd * nk):
                next(att, None)
                ia += 1
            else:
                next(sd, None)
                isd += 1

    load_x(0)
    for b in range(NBLK):
        if b + 1 < NBLK:
            load_x(b + 1)
        inproj(b)
        interleave(b)
    P.emit()
    return nc


_NC = {}
BF = ml_dtypes.bfloat16


def _get(name, builder):
    if name not in _NC:
        _NC[name] = builder()
    return _NC[name]


def _launch(nc, in_maps):
    res = run_bass_kernel_spmd(nc, in_maps, core_ids=list(range(len(in_maps))))
    return res.results


_IDENTF = np.eye(128, dtype=np.float32)


def run_odd(xtok, xT_bf, w_in, w_grp, b_grp, scale, w_out, lng, lnb, cores=range(NCORES)):
    nc = _get("odd", lambda: build_tail(True))
    bg = np.ascontiguousarray(b_grp.reshape(16, 128).T)
    osc = np.ascontiguousarray(scale.reshape(16, 128).T)
    in_maps = []
    for c in cores:
        t0 = c * TPC
        xT_c = np.zeros((D, HALO + TPC), BF)
        xT_c[:, HALO:] = xT_bf[:, t0:t0 + TPC]
        if c > 0:
            xT_c[:, :HALO] = xT_bf[:, t0 - HALO:t0]
        inv = np.zeros((4, 16), np.float32)
        for gi in range(4):
            for i in range(16):
                inv[gi, i] = 1.0 / min(t0 + i + 1, 2 ** (gi + 1))
        in_maps.append(dict(xres=np.ascontiguousarray(xtok[t0:t0 + TPC]), w_out=w_out, lng=lng, lnb=lnb,
                            identf=_IDENTF, xT=xT_c, w_in=w_in, w_grp=w_grp, bgrp=bg, oscale=osc,
                            invtab=inv.reshape(-1)))
    res = _launch(nc, in_maps)
    xo = np.concatenate([r["xo"] for r in res], 0)
    xoT = np.concatenate([r["xoT"] for r in res], 1)
    return xo, xoT


def _even_consts():
    mask2 = np.zeros((128, 64), np.float32)
    for u in range(2):
        for s_ in range(64):
            mask2[u * 64 + s_, s_:] = 1.0
    bdm = np.zeros((128, 128), np.float32)
    bdm[:64, :64] = 1.0
    bdm[64:, 64:] = 1.0
    hsel = np.zeros((2, 128), np.float32)
    hsel[0, :64] = 1.0
    hsel[1, 64:] = 1.0
    return dict(identf=_IDENTF, mask2=mask2, bdmask=bdm, hsel=hsel, eye2=np.eye(2, dtype=np.float32))


def run_even(xT, w_in, conv_w, conv_b, dt_bias, a_log, d_skip, ssm_norm_g, lq1, lk1, lq2, lk2, subln_g,
             lam_init, cores=range(NCORES), nblk=S // 512):
    x_bf16 = xT.dtype != np.float32
    nc = _get(("even", x_bf16, nblk), lambda: build_even(x_bf16, nblk))
    consts = _even_consts()
    ar = np.arange(128)
    in_maps = []
    for c in cores:
        g = c // 4
        idx = np.concatenate([c * 128 + ar, 1024 + c * 128 + ar, 2048 + c * 128 + ar, 3072 + c * 128 + ar,
                              4096 + c * 128 + ar, 5120 + c * 128 + ar, 6144 + g * 128 + ar, 6400 + g * 128 + ar,
                              6656 + 2 * c + np.arange(2)])
        wA = np.ascontiguousarray(w_in[:, idx])
        chans = [c * 128 + ar, 1024 + g * 128 + ar, 1280 + g * 128 + ar]
        cw = np.zeros((128, 12), np.float32)
        cb = np.zeros((128, 3), np.float32)
        for t in range(3):
            cw[:, 4 * t:4 * t + 4] = conv_w[:, chans[t]].T
            cb[:, t] = conv_b[chans[t]]
        colp = np.zeros((128, 8), np.float32)
        colp[:, 0] = np.repeat(d_skip[2 * c:2 * c + 2], 64)
        colp[:, 1] = ssm_norm_g[c * 128:(c + 1) * 128]
        colp[:, 2] = subln_g
        colp[:, 3] = lam_init
        colp[:, 4] = 1.0 - lam_init
        dtp = np.stack([dt_bias[2 * c:2 * c + 2], a_log[2 * c:2 * c + 2]], 1).astype(np.float32)
        lamv = np.concatenate([lq1, lk1, lq2, lk2]).astype(np.float32)
        m = dict(xT=xT, wA=wA, cw=cw, cb=cb, colp=colp, dtp=dtp, lamv=lamv)
        m.update(consts)
        in_maps.append(m)
    res = _launch(nc, in_maps)
    return [r["yatt"] for r in res], [r["yssm"] for r in res], [r["ssq"] for r in res]


def _sel8():
    sel = np.zeros((8, 256), np.float32)
    sel[0:4, 0:128] = 1.0
    sel[4:8, 128:256] = 1.0
    return sel


def run_evtail(xtok, yT, ssq8, w_out, lng, lnb, cores=range(NCORES)):
    nc = _get("evtail", lambda: build_tail(False))
    sel = _sel8()
    in_maps = []
    for c in cores:
        t0 = c * TPC
        in_maps.append(dict(xres=np.ascontiguousarray(xtok[t0:t0 + TPC]), w_out=w_out, lng=lng, lnb=lnb,
                            identf=_IDENTF, yT=np.ascontiguousarray(yT[:, t0:t0 + TPC]),
                            ssq8=np.ascontiguousarray(ssq8[:, t0:t0 + TPC]), sel=sel))
    res = _launch(nc, in_maps)
    xo = np.concatenate([r["xo"] for r in res], 0)
    xoT = np.concatenate([r["xoT"] for r in res], 1)
    return xo, xoT


def kernel(x, ev_w_in, ev_conv_w, ev_conv_b, ev_dt_bias, ev_a_log, ev_d_skip, ev_ssm_norm_g,
           ev_lambda_q1, ev_lambda_k1, ev_lambda_q2, ev_lambda_k2, ev_subln_g, ev_w_out,
           od_w_in, od_w_grp, od_b_grp, od_scale, od_w_out, ln_g, ln_b):
    f = lambda a: np.asarray(a, dtype=np.float32)
    xtok = np.ascontiguousarray(f(x)[0])
    xT = np.ascontiguousarray(xtok.T)
    for l in range(DEPTH):
        i = l // 2
        if l % 2 == 0:
            lam_init = 0.8 - 0.6 * math.exp(-0.3 * l)
            ya, ys, sq = run_even(xT, f(ev_w_in[i]), f(ev_conv_w[i]), f(ev_conv_b[i]), f(ev_dt_bias[i]), f(ev_a_log[i]),
                                  f(ev_d_skip[i]), f(ev_ssm_norm_g[i]), f(ev_lambda_q1[i]), f(ev_lambda_k1[i]),
                                  f(ev_lambda_q2[i]), f(ev_lambda_k2[i]), f(ev_subln_g[i]), lam_init)
            yT = np.concatenate(list(ya) + list(ys), 0)
            ssq8 = np.concatenate(list(sq), 0)
            xtok, xT = run_evtail(xtok, yT, ssq8, f(ev_w_out[i]), f(ln_g[l]), f(ln_b[l]))
        else:
            xtok, xT = run_odd(xtok, xT, f(od_w_in[i]), f(od_w_grp[i]), f(od_b_grp[i]), f(od_scale[i]), f(od_w_out[i]),
                               f(ln_g[l]), f(ln_b[l]))
    return np.ascontiguousarray(xtok[None]).astype(np.float32)
```

```python
import math
import numpy as np
import ml_dtypes
import concourse.bass as bass
import concourse.mybir as mybir
from concourse.bass_utils import run_bass_kernel_spmd

F32 = mybir.dt.float32
BF16 = mybir.dt.bfloat16
AF = mybir.ActivationFunctionType
ALU = mybir.AluOpType

NCORES = 8
D = 2048
S = 8192
TPC = S // NCORES
HALO = 16
DEPTH = 4
ALPHA = (2.0 * DEPTH) ** 0.25
EPS = 1e-5
NPJ = 1026


class Prog:
    ENG = ("pe", "act", "dve", "pool", "sp")

    def __init__(self, nc):
        self.nc = nc
        self.q = {e: [] for e in self.ENG}
        self.sem = {e: nc.alloc_semaphore("sem_" + e) for e in ("pe", "act", "dve", "pool")}
        self.cnt = {e: 0 for e in self.sem}
        self.last_w = {}
        self.reads = {}
        self.waited = {e: {} for e in self.ENG}
        self.dsem = {}
        self.out_tokens = []

    def _deps(self, eng, reads, writes):
        toks = []
        for k in list(reads) + list(writes):
            t = self.last_w.get(k)
            if t is not None:
                toks.append(t)
        for k in writes:
            toks.extend(self.reads.get(k, ()))
        waits = {}
        for (sem, val, src) in toks:
            if src == "pe" and eng == "pe":
                continue
            sid = id(sem)
            if self.waited[eng].get(sid, (None, 0))[1] >= val:
                continue
            if sid not in waits or waits[sid][1] < val:
                waits[sid] = (sem, val)
        for sid, (sem, val) in waits.items():
            self.waited[eng][sid] = (sem, val)
        return list(waits.values())

    def _commit(self, tok, reads, writes):
        for k in reads:
            self.reads.setdefault(k, []).append(tok)
        for k in writes:
            self.last_w[k] = tok
            self.reads[k] = []

    def op(self, eng, fn, reads=(), writes=()):
        waits = self._deps(eng, reads, writes)
        self.cnt[eng] += 1
        tok = (self.sem[eng], self.cnt[eng], eng)
        self.q[eng].append((waits, fn, (self.sem[eng], 1)))
        self._commit(tok, reads, writes)

    def dma(self, queue, fn, semname, reads=(), writes=(), is_output=False):
        if semname not in self.dsem:
            self.dsem[semname] = [self.nc.alloc_semaphore("d_" + semname), 0]
        ent = self.dsem[semname]
        waits = self._deps(queue, reads, writes)
        ent[1] += 16
        tok = (ent[0], ent[1], "dma")
        self.q[queue].append((waits, fn, (ent[0], 16)))
        self._commit(tok, reads, writes)
        if is_output:
            self.out_tokens.append(tok)

    def emit(self):
        nc = self.nc
        fin = {}
        for (sem, val, _) in self.out_tokens:
            if id(sem) not in fin or fin[id(sem)][1] < val:
                fin[id(sem)] = (sem, val)
        q = self.q

        def run(e, name, final=False):
            for waits, fn, (sem, amt) in q[name]:
                for (ws, wv) in waits:
                    e.wait_ge(ws, wv)
                fn(e).then_inc(sem, amt)
            if final:
                for (ws, wv) in fin.values():
                    e.wait_ge(ws, wv)

        with nc.Block() as block:
            @block.sync
            def _(e):
                run(e, "sp", final=True)

            @block.tensor
            def _(e):
                run(e, "pe")

            @block.scalar
            def _(e):
                run(e, "act")

            @block.vector
            def _(e):
                run(e, "dve")

            @block.gpsimd
            def _(e):
                run(e, "pool")


def PK(pt):
    return pt.name


def _din(nc, name, shape, dt=F32):
    return nc.dram_tensor(name, list(shape), dt, kind="ExternalInput").ap()


def _dout(nc, name, shape, dt=F32):
    return nc.dram_tensor(name, list(shape), dt, kind="ExternalOutput").ap()


def build_tail(front):
    nc = bass.Bass("TRN2", target_bir_lowering=False)
    P = Prog(nc)
    NT = TPC // 128
    xres = _din(nc, "xres", [TPC, D])
    w_out = _din(nc, "w_out", [D, D])
    lng = _din(nc, "lng", [D])
    lnb = _din(nc, "lnb", [D])
    xo = _dout(nc, "xo", [TPC, D])
    xoT = _dout(nc, "xoT", [D, TPC], BF16)
    identf_d = _din(nc, "identf", [128, 128])
    if front:
        xT = _din(nc, "xT", [D, HALO + TPC], BF16)
        w_in = _din(nc, "w_in", [D, 2 * D])
        w_grp = _din(nc, "w_grp", [4, 512, 512])
        bgrp = _din(nc, "bgrp", [128, 16])
        oscale = _din(nc, "oscale", [128, 16])
        invtab = _din(nc, "invtab", [4 * 16])
    else:
        yT_d = _din(nc, "yT", [D, TPC], BF16)
        ssq8 = _din(nc, "ssq8", [8, TPC])
        sel_d = _din(nc, "sel", [8, 2 * 128])

    NTOK = HALO + TPC
    XT_E = 16 * NTOK
    WB_E = 16 * 512
    big = nc.alloc_sbuf_tensor("big", [128, max(XT_E + 2 * WB_E, 16 * D)], BF16)
    wout_sb = big[:, 0:16 * D].rearrange("p (k n) -> p k n", k=16)
    yT = nc.alloc_sbuf_tensor("yT_sb", [128, 16, TPC], BF16)
    lng_sb = nc.alloc_sbuf_tensor("lng_sb", [128, D], F32)
    lnb_sb = nc.alloc_sbuf_tensor("lnb_sb", [128, D], F32)
    xr2 = [nc.alloc_sbuf_tensor("xr_sb%d" % i, [128, D], F32) for i in range(2)]
    z2 = [nc.alloc_sbuf_tensor("z_sb%d" % i, [128, D], F32) for i in range(2)]
    zT_sb = nc.alloc_sbuf_tensor("zT_sb", [128, 4, 512], BF16)
    xoT_v = xoT.rearrange("(k p) n -> p k n", p=128)
    stats = nc.alloc_sbuf_tensor("stats", [128, 4, 6], F32)
    mv = nc.alloc_sbuf_tensor("mv", [128, 2], F32)
    rstd = nc.alloc_sbuf_tensor("rstd", [128, 1], F32)
    nmr = nc.alloc_sbuf_tensor("nmr", [128, 1], F32)
    identf = nc.alloc_sbuf_tensor("identf_sb", [128, 128], F32)
    ps = [nc.alloc_psum_tensor("ps%d" % i, [128, 512], F32) for i in range(8)]

    P.dma("sp", lambda e: e.dma_start(out=lng_sb[:], in_=lng.partition_broadcast(128)), "c1", writes=["lng"])
    P.dma("sp", lambda e: e.dma_start(out=lnb_sb[:], in_=lnb.partition_broadcast(128)), "c2", writes=["lnb"])
    P.dma("sp", lambda e: e.dma_start(out=identf[:], in_=identf_d), "c3", writes=["identf"])

    if front:
        xT_sb = big[:, 0:XT_E].rearrange("p (k n) -> p k n", k=16)
        wbuf = [big[:, XT_E + i * WB_E: XT_E + (i + 1) * WB_E].rearrange("p (k n) -> p k n", k=16) for i in range(2)]
        wg_sb = nc.alloc_sbuf_tensor("wg_sb", [128, 4, 4, 512], BF16)
        bg_sb = nc.alloc_sbuf_tensor("bg_sb", [128, 16], F32)
        os_sb = nc.alloc_sbuf_tensor("os_sb", [128, 16], F32)
        inv_sb = nc.alloc_sbuf_tensor("inv_sb", [128, 4, 16], F32)
        vT = nc.alloc_sbuf_tensor("vT", [128, NTOK], F32)
        sA = nc.alloc_sbuf_tensor("sA", [128, NTOK], F32)
        sB = nc.alloc_sbuf_tensor("sB", [128, NTOK], F32)
        pooled = nc.alloc_sbuf_tensor("pooled", [128, 4, TPC], BF16)
        ge = nc.alloc_sbuf_tensor("ge", [128, TPC], F32)
        gs = nc.alloc_sbuf_tensor("gs", [128, TPC], F32)
        mt = nc.alloc_sbuf_tensor("mt", [128, TPC], F32)

        xT_v = xT.rearrange("(k p) n -> p k n", p=128)
        for h in range(4):
            P.dma("sp" if h % 2 == 0 else "act", lambda e, h=h: e.dma_start(out=xT_sb[:, 4 * h:4 * h + 4, :], in_=xT_v[:, 4 * h:4 * h + 4, :]),
                  "xT", writes=["xT"])
        wg_v = w_grp.rearrange("g (c p) d -> p g c d", p=128)
        P.dma("sp", lambda e: e.dma_start(out=bg_sb[:], in_=bgrp), "c4", writes=["bg"])
        P.dma("sp", lambda e: e.dma_start(out=os_sb[:], in_=oscale), "c5", writes=["os"])
        P.dma("sp", lambda e: e.dma_start(out=inv_sb[:].rearrange("p g i -> p (g i)"), in_=invtab.partition_broadcast(128)),
              "c6", writes=["inv"])
        w_in_v = w_in.rearrange("(k p) n -> p k n", p=128)

        def load_w(i, col0):
            b = i % 2
            P.dma("pool", lambda e: e.dma_start(out=wbuf[b][:, :, :], in_=w_in_v[:, :, col0:col0 + 512]),
                  "wbuf%d" % b, writes=["wbuf%d" % b])

        order = []
        for gi in range(4):
            order.append(gi * 512)
            order.append(D + gi * 512)
        load_w(0, order[0])
        for g in range(4):
            P.dma("pool", lambda e, g=g: e.dma_start(out=wg_sb[:, g, :, :], in_=wg_v[:, g, :, :]), "wg", writes=["wg"])
        load_w(1, order[1])

        for gi in range(4):
            wv = 2 * gi
            wgt = 2 * gi + 1
            w = 2 ** (gi + 1)
            for ft in range(4):
                b = wv % 2
                regs = [(ps[0], 0, HALO, 512), (ps[1], 0, HALO + 512, 512), (ps[2], 0, 0, HALO)]
                for (pt, pc, tc0, n) in regs:
                    for kc in range(16):
                        P.op("pe", lambda e, pt=pt, pc=pc, tc0=tc0, n=n, kc=kc, b=b, ft=ft: e.matmul(
                            pt[:, pc:pc + n], lhsT=wbuf[b][:, kc, ft * 128:(ft + 1) * 128],
                            rhs=xT_sb[:, kc, tc0:tc0 + n], start=(kc == 0), stop=(kc == 15)),
                            reads=["wbuf%d" % b, "xT"], writes=[PK(pt)])
                P.op("act", lambda e: e.activation(out=vT[:, HALO:HALO + 512], in_=ps[0][:, :], func=AF.Copy),
                     reads=["ps0"], writes=["vT"])
                P.op("act", lambda e: e.activation(out=vT[:, HALO + 512:NTOK], in_=ps[1][:, :], func=AF.Copy),
                     reads=["ps1"], writes=["vT"])
                P.op("act", lambda e: e.activation(out=vT[:, 0:HALO], in_=ps[2][:, 0:HALO], func=AF.Copy),
                     reads=["ps2"], writes=["vT"])
                src, skey = vT, "vT"
                dsts = [(sA, "sA"), (sB, "sB")]
                sh = 1
                lo = 0
                for st in range(gi + 1):
                    dst, dkey = dsts[st % 2]
                    lo = lo + sh
                    P.op("dve",
                         lambda e, dst=dst, src=src, lo=lo, sh=sh: e.tensor_tensor(
                             out=dst[:, lo:NTOK], in0=src[:, lo:NTOK], in1=src[:, lo - sh:NTOK - sh], op=ALU.add),
                         reads=[skey], writes=[dkey])
                    src, skey = dst, dkey
                    sh *= 2
                P.op("dve", lambda e, src=src, w=w, ft=ft: e.scalar_tensor_tensor(
                    out=pooled[:, ft, HALO:TPC], in0=src[:, 2 * HALO:NTOK], scalar=1.0 / w,
                    in1=vT[:, 2 * HALO:NTOK], op0=ALU.mult, op1=ALU.subtract),
                    reads=[skey, "vT"], writes=["pooled%d" % ft])
                P.op("dve", lambda e, src=src, gi=gi: e.tensor_tensor(
                    out=src[:, HALO:2 * HALO], in0=src[:, HALO:2 * HALO], in1=inv_sb[:, gi, :], op=ALU.mult),
                    reads=[skey, "inv"], writes=[skey])
                P.op("dve", lambda e, src=src, ft=ft: e.tensor_tensor(
                    out=pooled[:, ft, 0:HALO], in0=src[:, HALO:2 * HALO], in1=vT[:, HALO:2 * HALO], op=ALU.subtract),
                    reads=[skey, "vT"], writes=["pooled%d" % ft])
            if gi < 3:
                load_w(wv + 2, order[wv + 2])
            for dti in range(4):
                dt_ = gi * 4 + dti
                b = wgt % 2
                for hf in range(2):
                    pt = ps[3 + hf]
                    for kc in range(16):
                        P.op("pe", lambda e, pt=pt, hf=hf, kc=kc, b=b, dti=dti: e.matmul(
                            pt[:, :], lhsT=wbuf[b][:, kc, dti * 128:(dti + 1) * 128],
                            rhs=xT_sb[:, kc, HALO + hf * 512:HALO + (hf + 1) * 512],
                            start=(kc == 0), stop=(kc == 15)),
                            reads=["wbuf%d" % b, "xT"], writes=[PK(pt)])
                for hf in range(2):
                    pt = ps[5 + hf]
                    for cc in range(4):
                        P.op("pe", lambda e, pt=pt, hf=hf, cc=cc, gi=gi, dti=dti: e.matmul(
                            pt[:, :], lhsT=wg_sb[:, gi, cc, dti * 128:(dti + 1) * 128],
                            rhs=pooled[:, cc, hf * 512:(hf + 1) * 512], start=(cc == 0), stop=(cc == 3)),
                            reads=["wg", "pooled%d" % cc], writes=[PK(pt)])
                for hf in range(2):
                    sl = slice(hf * 512, (hf + 1) * 512)
                    P.op("act", lambda e, hf=hf, sl=sl: e.activation(out=ge[:, sl], in_=ps[3 + hf][:, :], func=AF.Exp, scale=-1.0),
                         reads=["ps%d" % (3 + hf)], writes=["ge%d" % hf])
                    P.op("act", lambda e, sl=sl: e.activation(out=ge[:, sl], in_=ge[:, sl], func=AF.Ln, bias=1.0, scale=1.0),
                         reads=["ge%d" % hf], writes=["ge%d" % hf])
                    P.op("act", lambda e, sl=sl: e.activation(out=ge[:, sl], in_=ge[:, sl], func=AF.Exp, scale=-1.0),
                         reads=["ge%d" % hf], writes=["ge%d" % hf])
                    P.op("dve", lambda e, hf=hf, sl=sl: e.tensor_tensor(out=gs[:, sl], in0=ge[:, sl], in1=ps[3 + hf][:, :], op=ALU.mult),
                         reads=["ge%d" % hf, "ps%d" % (3 + hf)], writes=["gs%d" % hf])
                    P.op("act", lambda e, hf=hf, sl=sl, dt_=dt_: e.activation(
                        out=mt[:, sl], in_=ps[5 + hf][:, :], func=AF.Identity,
                        bias=bg_sb[:, dt_:dt_ + 1], scale=1.0),
                        reads=["ps%d" % (5 + hf), "bg"], writes=["mt%d" % hf])
                    P.op("dve", lambda e, sl=sl, dt_=dt_: e.scalar_tensor_tensor(
                        out=yT[:, dt_, sl], in0=mt[:, sl], scalar=os_sb[:, dt_:dt_ + 1], in1=gs[:, sl],
                        op0=ALU.mult, op1=ALU.mult),
                        reads=["mt%d" % hf, "gs%d" % hf, "os"], writes=["yT"])
            if gi < 3:
                load_w(wgt + 2, order[wgt + 2])
    else:
        ssq_sb = nc.alloc_sbuf_tensor("ssq_sb", [8, TPC], F32)
        sel_sb = nc.alloc_sbuf_tensor("sel_sb", [8, 256], F32)
        rs_sb = nc.alloc_sbuf_tensor("rs_sb", [128, 2, TPC], F32)
        yT_v = yT_d.rearrange("(k p) n -> p k n", p=128)
        for h in range(4):
            P.dma("sp", lambda e, h=h: e.dma_start(out=yT[:, 4 * h:4 * h + 4, :], in_=yT_v[:, 4 * h:4 * h + 4, :]),
                  "yT", writes=["yT"])
        P.dma("sp", lambda e: e.dma_start(out=ssq_sb[:], in_=ssq8), "c7", writes=["ssq"])
        P.dma("sp", lambda e: e.dma_start(out=sel_sb[:], in_=sel_d), "c8", writes=["sel"])
        for g in range(2):
            for hf in range(2):
                pt = ps[2 * g + hf]
                P.op("pe", lambda e, pt=pt, g=g, hf=hf: e.matmul(
                    pt[:, :], lhsT=sel_sb[:, g * 128:(g + 1) * 128], rhs=ssq_sb[:, hf * 512:(hf + 1) * 512],
                    start=True, stop=True), reads=["sel", "ssq"], writes=[PK(pt)])
                P.op("act", lambda e, pt=pt, g=g, hf=hf: e.activation(
                    out=rs_sb[:, g, hf * 512:(hf + 1) * 512], in_=pt[:, :], func=AF.Ln, bias=EPS, scale=1.0 / 512),
                    reads=[PK(pt)], writes=["rs%d%d" % (g, hf)])
                P.op("act", lambda e, g=g, hf=hf: e.activation(
                    out=rs_sb[:, g, hf * 512:(hf + 1) * 512], in_=rs_sb[:, g, hf * 512:(hf + 1) * 512],
                    func=AF.Exp, scale=-0.5),
                    reads=["rs%d%d" % (g, hf)], writes=["rs%d%d" % (g, hf)])
            for kc in range(4):
                k = 8 + 4 * g + kc
                P.op("dve", lambda e, k=k, g=g: e.tensor_tensor(
                    out=yT[:, k, :], in0=yT[:, k, :], in1=rs_sb[:, g, :], op=ALU.mult),
                    reads=["yT", "rs%d0" % g, "rs%d1" % g], writes=["yT"])

    w_out_v = w_out.rearrange("(k p) n -> p k n", p=128)
    fkeys = ["xT", "wbuf0", "wbuf1"] if front else []
    for nb in range(4):
        P.dma("pool", lambda e, nb=nb: e.dma_start(out=wout_sb[:, :, nb * 512:(nb + 1) * 512],
                                                   in_=w_out_v[:, :, nb * 512:(nb + 1) * 512]),
              "wout%d" % nb, writes=["wout%d" % nb] + (fkeys if nb == 0 else []))
    def tail_mm(tt):
        q = tt % 2
        P.dma("sp", lambda e: e.dma_start(out=xr2[q][:], in_=xres[tt * 128:(tt + 1) * 128, :]), "xr%d" % q,
              writes=["xr%d" % q])
        for nb in range(4):
            pt = ps[4 * q + nb]
            for kc in range(16):
                P.op("pe", lambda e, pt=pt, kc=kc, nb=nb: e.matmul(
                    pt[:, :], lhsT=yT[:, kc, tt * 128:(tt + 1) * 128], rhs=wout_sb[:, kc, nb * 512:(nb + 1) * 512],
                    start=(kc == 0), stop=(kc == 15)), reads=["yT", "wout%d" % nb], writes=[PK(pt)])

    def tail_post(tt, after_ln=None):
        q = tt % 2
        z_sb = z2[q]
        zk = ["z%d_%d" % (q, i) for i in range(4)]
        for nb in range(4):
            pt = ps[4 * q + nb]
            P.op("dve", lambda e, pt=pt, nb=nb: e.scalar_tensor_tensor(
                out=z_sb[:, nb * 512:(nb + 1) * 512], in0=xr2[q][:, nb * 512:(nb + 1) * 512], scalar=ALPHA,
                in1=pt[:, :], op0=ALU.mult, op1=ALU.add),
                reads=["xr%d" % q, PK(pt)], writes=[zk[nb]])
            P.op("dve", lambda e, nb=nb: e.bn_stats(out=stats[:, nb, :], in_=z_sb[:, nb * 512:(nb + 1) * 512]),
                 reads=[zk[nb]], writes=["stats%d" % nb])
        if after_ln is not None:
            after_ln()
        P.op("dve", lambda e: e.bn_aggr(out=mv[:], in_=stats[:].rearrange("p a b -> p (a b)")),
             reads=["stats%d" % i for i in range(4)], writes=["mv"])
        P.op("act", lambda e: e.activation(out=rstd[:], in_=mv[:, 1:2], func=AF.Ln, bias=EPS, scale=1.0),
             reads=["mv"], writes=["rstd"])
        P.op("act", lambda e: e.activation(out=rstd[:], in_=rstd[:], func=AF.Exp, scale=-0.5),
             reads=["rstd"], writes=["rstd"])
        P.op("dve", lambda e: e.scalar_tensor_tensor(out=nmr[:], in0=mv[:, 0:1], scalar=-1.0, in1=rstd[:],
                                                     op0=ALU.mult, op1=ALU.mult),
             reads=["mv", "rstd"], writes=["nmr"])
        P.op("act", lambda e: e.activation(out=z_sb[:], in_=z_sb[:], func=AF.Identity, bias=nmr[:], scale=rstd[:]),
             reads=zk + ["nmr", "rstd"], writes=zk)
        P.op("dve", lambda e: e.tensor_tensor(out=z_sb[:], in0=z_sb[:], in1=lng_sb[:], op=ALU.mult),
             reads=zk + ["lng"], writes=zk)
        P.op("dve", lambda e: e.tensor_tensor(out=z_sb[:], in0=z_sb[:], in1=lnb_sb[:], op=ALU.add),
             reads=zk + ["lnb"], writes=zk)
        P.dma("sp", lambda e: e.dma_start(out=xo[tt * 128:(tt + 1) * 128, :], in_=z_sb[:]), "xo%d" % q,
              reads=zk, is_output=True)
        for grp in range(4):
            pk = "ps%d" % (4 * q + grp)
            pt = ps[4 * q + grp]
            for j in range(4):
                kc = grp * 4 + j
                P.op("pe", lambda e, pt=pt, j=j, kc=kc: e.transpose(
                    pt[:, j * 128:(j + 1) * 128], z_sb[:, kc * 128:(kc + 1) * 128], identf[:]),
                    reads=zk + ["identf"], writes=[pk])
            P.op("act", lambda e, pt=pt, grp=grp: e.activation(out=zT_sb[:, grp, :], in_=pt[:, :], func=AF.Copy),
                 reads=[pk], writes=["zT%d" % grp])
            P.dma("act", lambda e, grp=grp: e.dma_start(
                out=xoT_v[:, grp * 4:grp * 4 + 4, tt * 128:(tt + 1) * 128],
                in_=zT_sb[:, grp, :].rearrange("p (k n) -> p k n", k=4)), "xoT%d" % grp,
                reads=["zT%d" % grp], is_output=True)

    tail_mm(0)
    for tt in range(NT):
        nxt = (lambda t=tt: tail_mm(t + 1)) if tt + 1 < NT else None
        tail_post(tt, after_ln=nxt)

    P.emit()
    return nc


def build_even(x_bf16, nblk=S // 512):
    nc = bass.Bass("TRN2", target_bir_lowering=False)
    P = Prog(nc)
    NBLK = nblk
    S = nblk * 512
    xT = _din(nc, "xT", [D, S], BF16 if x_bf16 else F32)
    wA = _din(nc, "wA", [D, NPJ])
    cw_d = _din(nc, "cw", [128, 12])
    cb_d = _din(nc, "cb", [128, 3])
    colp_d = _din(nc, "colp", [128, 8])
    dtp_d = _din(nc, "dtp", [2, 2])
    lamv_d = _din(nc, "lamv", [256])
    identf_d = _din(nc, "identf", [128, 128])
    mask2_d = _din(nc, "mask2", [128, 64])
    bdm_d = _din(nc, "bdmask", [128, 128])
    hsel_d = _din(nc, "hsel", [2, 128])
    eye2_d = _din(nc, "eye2", [2, 2])
    yatt = _dout(nc, "yatt", [128, S], BF16)
    yssm = _dout(nc, "yssm", [128, S], BF16)
    ssq = _dout(nc, "ssq", [1, S])

    A = nc.alloc_sbuf_tensor
    W_sb = A("W_sb", [128, 16, NPJ], BF16)
    xb = [A("xb%d" % i, [128, 16, 512], BF16) for i in range(2)]
    KT = A("KT", [128, S], BF16)
    V = A("V", [128, S // 128, 128], BF16)
    QT = A("QT", [128, 512], BF16)
    sg = A("sg", [128, 512], F32)
    sz = A("sz", [128, 512], F32)
    raw = A("raw", [128, 3, 515], F32)
    xsT = A("xsT", [128, 512], F32)
    xsdup = A("xsdup", [128, 1024], BF16)
    BTdup = A("BTdup", [128, 1024], BF16)
    CT = A("CT", [128, 512], BF16)
    tmp = {k: A("t" + k, [128, 512], F32) for k in "ABCDEF"}
    E = [[A("E%d%d" % (i, m), [128, 512], BF16) for m in range(2)] for i in range(2)]
    ysT = A("ysT", [128, 512], F32)
    yatt_sb = A("yatt_sb", [128, 512], BF16)
    yssm_sb = A("yssm_sb", [128, 512], BF16)
    ssq_sb = A("ssq_sb", [1, 512], F32)
    cw = A("cw_sb", [128, 12], F32)
    cb = A("cb_sb", [128, 3], F32)
    colp = A("colp_sb", [128, 8], F32)
    lamv = A("lamv_sb", [128, 256], F32)
    lamt = A("lamt", [128, 128], F32)
    lams = A("lams", [128, 4], F32)
    identb = A("identb", [128, 128], BF16)
    mask2 = A("mask2_sb", [128, 64], F32)
    bdm = A("bdm_sb", [128, 128], F32)
    ones_b = A("ones_b", [128, 128], BF16)
    ones_f = A("ones_f", [128, 128], F32)
    hsel = A("hsel_sb", [2, 128], F32)
    eye2 = A("eye2_sb", [2, 2], F32)
    dtp = A("dtp_sb", [2, 2], F32)
    Acol = A("Acol", [2, 1], F32)
    dt_sb = A("dt_sb", [2, 512], F32)
    a_sb = A("a_sb", [2, 512], F32)
    acs = A("acs", [2, 512], F32)
    dtBD = A("dtBD", [2, 1024], F32)
    acsBD = A("acsBD", [2, 1024], F32)
    lastbc = A("lastbc", [2, 8, 128], F32)
    cols = A("cols", [128, 16], F32)
    negc = A("negc", [128, 8], F32)
    Sst = A("Sst", [128, 128], F32)
    Sbf = A("Sbf", [128, 128], BF16)
    Gm = A("Gm", [128, 64], F32)
    Dm = A("Dm", [128, 64], F32)
    E2 = A("E2", [128, 64], F32)
    M2 = A("M2", [128, 64], BF16)
    R2e = A("R2e", [128, 64], F32)
    XdtBD = A("XdtBD", [128, 128], BF16)
    XdecBD = A("XdecBD", [128, 128], BF16)
    Btok = A("Btok", [128, 128], BF16)
    CD = A("CD", [128, 128], F32)
    y1 = A("y1", [128, 64], F32)
    ps = [nc.alloc_psum_tensor("ps%d" % i, [128, 512], F32) for i in range(8)]
    psS = [ps[0], ps[1]]
    psO = [ps[2], ps[3]]
    psD = [ps[4], ps[5]]
    rG, rR, rX, rcol = ps[6][:, 0:64], ps[6][:, 64:128], ps[6][:, 128:256], ps[6][:, 256:272]
    rB, rC, rS = ps[7][:, 0:128], ps[7][:, 128:256], ps[7][:, 256:384]
    rYo, rYd = ps[7][:, 384:448], ps[7][:, 448:512]

    def ld(q, dst, src, key):
        P.dma(q, lambda e: e.dma_start(out=dst, in_=src), key, writes=[key])

    ld("sp", cw[:], cw_d, "cw")
    ld("sp", cb[:], cb_d, "cb")
    ld("sp", colp[:], colp_d, "colp")
    ld("sp", dtp[:], dtp_d, "dtp")
    ld("sp", lamv[:], lamv_d.partition_broadcast(128), "lamv")
    ld("sp", mask2[:], mask2_d, "mask2")
    ld("sp", bdm[:], bdm_d, "bdm")
    ld("sp", hsel[:], hsel_d, "hsel")
    ld("sp", eye2[:], eye2_d, "eye2")
    ld("pool", identb[:], identf_d, "identb")
    P.op("pool", lambda e: e.memset(ones_b[:], 1.0), writes=["ones_b"])
    P.op("pool", lambda e: e.memset(ones_f[:], 1.0), writes=["ones_f"])
    P.op("pool", lambda e: e.memset(raw[:], 0.0), writes=["raw0", "raw1", "raw2"])
    P.op("pool", lambda e: e.memset(Sst[:], 0.0), writes=["Sst"])
    wA_v = wA.rearrange("(k p) n -> p k n", p=128)
    for h in range(4):
        P.dma("pool", lambda e, h=h: e.dma_start(out=W_sb[:, 4 * h:4 * h + 4, :], in_=wA_v[:, 4 * h:4 * h + 4, :]),
              "W", writes=["W"])
    xT_v = xT.rearrange("(k p) n -> p k n", p=128)

    def load_x(b):
        i = b % 2
        for h in range(2):
            if x_bf16:
                P.dma("sp" if h == 0 else "act", lambda e, h=h: e.dma_start(
                    out=xb[i][:, 8 * h:8 * h + 8, :], in_=xT_v[:, 8 * h:8 * h + 8, b * 512:(b + 1) * 512]),
                    "xb%d" % i, writes=["xb%d" % i])
            else:
                P.dma("pool", lambda e, h=h: e.dma_start(
                    out=xb[i][:, 8 * h:8 * h + 8, :], in_=xT_v[:, 8 * h:8 * h + 8, b * 512:(b + 1) * 512]),
                    "xb%d" % i, writes=["xb%d" % i])

    P.op("dve", lambda e: e.tensor_tensor(out=lamt[:, 0:64], in0=lamv[:, 0:64], in1=lamv[:, 64:128], op=ALU.mult),
         reads=["lamv"], writes=["lamt0"])
    P.op("dve", lambda e: e.tensor_tensor(out=lamt[:, 64:128], in0=lamv[:, 128:192], in1=lamv[:, 192:256], op=ALU.mult),
         reads=["lamv"], writes=["lamt1"])
    P.op("dve", lambda e: e.reduce_sum(out=lams[:, 0:1], in_=lamt[:, 0:64], axis=mybir.AxisListType.X),
         reads=["lamt0"], writes=["lams0"])
    P.op("dve", lambda e: e.reduce_sum(out=lams[:, 1:2], in_=lamt[:, 64:128], axis=mybir.AxisListType.X),
         reads=["lamt1"], writes=["lams1"])
    P.op("act", lambda e: e.activation(out=lams[:, 0:2], in_=lams[:, 0:2], func=AF.Exp),
         reads=["lams0", "lams1"], writes=["lams0", "lams1"])
    P.op("dve", lambda e: e.tensor_tensor(out=lams[:, 2:3], in0=lams[:, 1:2], in1=lams[:, 0:1], op=ALU.subtract),
         reads=["lams0", "lams1"], writes=["neglam"])
    P.op("dve", lambda e: e.tensor_tensor(out=lams[:, 2:3], in0=lams[:, 2:3], in1=colp[:, 3:4], op=ALU.subtract),
         reads=["neglam", "colp"], writes=["neglam"])
    P.op("dve", lambda e: e.tensor_tensor(out=lams[:, 3:4], in0=colp[:, 2:3], in1=colp[:, 4:5], op=ALU.mult),
         reads=["colp"], writes=["coef"])
    P.op("act", lambda e: e.activation(out=Acol[:], in_=dtp[:, 1:2], func=AF.Exp), reads=["dtp"], writes=["Acol"])
    P.op("dve", lambda e: e.tensor_scalar(out=Acol[:], in0=Acol[:], scalar1=-1.0, scalar2=None, op0=ALU.mult),
         reads=["Acol"], writes=["Acol"])

    def silu_from(src_ap, src_keys, dst_ap, dst_key, t, tkey):
        P.op("act", lambda e: e.activation(out=t, in_=src_ap, func=AF.Exp, scale=-1.0), reads=src_keys, writes=[tkey])
        P.op("act", lambda e: e.activation(out=t, in_=t, func=AF.Ln, bias=1.0, scale=1.0), reads=[tkey], writes=[tkey])
        P.op("act", lambda e: e.activation(out=t, in_=t, func=AF.Exp, scale=-1.0), reads=[tkey], writes=[tkey])
        P.op("dve", lambda e: e.tensor_tensor(out=dst_ap, in0=t, in1=src_ap, op=ALU.mult),
             reads=[tkey] + list(src_keys), writes=[dst_key])

    def inproj(b):
        xbi = xb[b % 2]
        xk = "xb%d" % (b % 2)
        bank = [0]

        def nextbank():
            bank[0] ^= 1
            return ps[bank[0]], "ps%d" % bank[0]

        def proj(c0):
            pt, pk = nextbank()
            for kc in range(16):
                P.op("pe", lambda e, kc=kc: e.matmul(pt[:, :], lhsT=W_sb[:, kc, c0:c0 + 128], rhs=xbi[:, kc, :],
                                                     start=(kc == 0), stop=(kc == 15)),
                     reads=["W", xk], writes=[pk])
            return pt, pk

        pt, pk = proj(0)
        P.op("act", lambda e, pt=pt: e.activation(out=QT[:], in_=pt[:, :], func=AF.Copy), reads=[pk], writes=["QT"])
        pt, pk = proj(128)
        P.op("act", lambda e, pt=pt: e.activation(out=KT[:, b * 512:(b + 1) * 512], in_=pt[:, :], func=AF.Copy),
             reads=[pk], writes=["KT%d" % b])
        pt, pk = nextbank()
        for i in range(4):
            for kc in range(16):
                P.op("pe", lambda e, pt=pt, i=i, kc=kc: e.matmul(
                    pt[:, i * 128:(i + 1) * 128], lhsT=xbi[:, kc, i * 128:(i + 1) * 128], rhs=W_sb[:, kc, 256:384],
                    start=(kc == 0), stop=(kc == 15)), reads=["W", xk], writes=[pk])
        P.op("act", lambda e, pt=pt: e.activation(
            out=V[:, b * 4:(b + 1) * 4, :].rearrange("p a d -> p (a d)"), in_=pt[:, :], func=AF.Copy),
            reads=[pk], writes=["V%d" % b])
        pt, pk = proj(384)
        silu_from(pt[:, :], [pk], sg[:], "sg", tmp["A"][:], "tA")
        pt, pk = proj(512)
        silu_from(pt[:, :], [pk], sz[:], "sz", tmp["B"][:], "tB")
        for t in range(3):
            pt, pk = proj(640 + 128 * t)
            rk = "raw%d" % t
            P.op("act", lambda e, pt=pt, t=t: e.activation(out=raw[:, t, 3:515], in_=pt[:, :], func=AF.Copy),
                 reads=[pk], writes=[rk])
            acc = tmp["C"][:]
            P.op("dve", lambda e, t=t: e.tensor_scalar(
                out=acc, in0=raw[:, t, 3:515], scalar1=cw[:, 4 * t + 3:4 * t + 4], scalar2=cb[:, t:t + 1],
                op0=ALU.mult, op1=ALU.add), reads=[rk, "cw", "cb"], writes=["tC"])
            for j in range(3):
                P.op("dve", lambda e, t=t, j=j: e.scalar_tensor_tensor(
                    out=acc, in0=raw[:, t, j:j + 512], scalar=cw[:, 4 * t + j:4 * t + j + 1], in1=acc,
                    op0=ALU.mult, op1=ALU.add), reads=[rk, "cw", "tC"], writes=["tC"])
            P.op("dve", lambda e, t=t: e.tensor_copy(out=raw[:, t, 0:3], in_=raw[:, t, 512:515]),
                 reads=[rk, "tC"], writes=[rk])
            if t == 0:
                silu_from(acc, ["tC"], xsT[:], "xsT", tmp["D"][:], "tD")
                for u in range(2):
                    P.op("dve", lambda e, u=u: e.tensor_copy(
                        out=xsdup[:].rearrange("p (c u l) -> p c u l", c=8, u=2)[:, :, u, :],
                        in_=xsT[:].rearrange("p (c l) -> p c l", c=8)), reads=["xsT"], writes=["xsdup"])
            elif t == 1:
                silu_from(acc, ["tC"], tmp["E"][:], "tE", tmp["D"][:], "tD")
                for u in range(2):
                    P.op("dve", lambda e, u=u: e.tensor_copy(
                        out=BTdup[:].rearrange("p (c u l) -> p c u l", c=8, u=2)[:, :, u, :],
                        in_=tmp["E"][:].rearrange("p (c l) -> p c l", c=8)), reads=["tE"], writes=["BTdup"])
            else:
                silu_from(acc, ["tC"], CT[:], "CT", tmp["D"][:], "tD")
        pt, pk = nextbank()
        for kc in range(16):
            P.op("pe", lambda e, pt=pt, kc=kc: e.matmul(pt[0:2, :], lhsT=W_sb[:, kc, 1024:1026], rhs=xbi[:, kc, :],
                                                        start=(kc == 0), stop=(kc == 15)),
                 reads=["W", xk], writes=[pk])
        P.op("act", lambda e, pt=pt: e.activation(out=dt_sb[:], in_=pt[0:2, :], func=AF.Exp, bias=dtp[:, 0:1], scale=1.0),
             reads=[pk, "dtp"], writes=["dt"])
        P.op("act", lambda e: e.activation(out=dt_sb[:], in_=dt_sb[:], func=AF.Ln, bias=1.0, scale=1.0),
             reads=["dt"], writes=["dt"])
        P.op("dve", lambda e: e.tensor_scalar(out=a_sb[:], in0=dt_sb[:], scalar1=Acol[:, 0:1], scalar2=None, op0=ALU.mult),
             reads=["dt", "Acol"], writes=["a"])
        for c in range(8):
            P.op("dve", lambda e, c=c: e.tensor_tensor_scan(
                out=acs[:, c * 64:(c + 1) * 64], data0=ones_f[0:2, 0:64], data1=a_sb[:, c * 64:(c + 1) * 64],
                initial=0.0, op0=ALU.mult, op1=ALU.add), reads=["a", "ones_f"], writes=["acs"])
        for h in range(2):
            P.op("dve", lambda e, h=h: e.tensor_scalar(
                out=dtBD[:].rearrange("p (c u l) -> p c u l", c=8, u=2)[:, :, h, :],
                in0=dt_sb[:].rearrange("p (c l) -> p c l", c=8), scalar1=eye2[:, h:h + 1], scalar2=None, op0=ALU.mult),
                reads=["dt", "eye2"], writes=["dtBD"])
            P.op("dve", lambda e, h=h: e.tensor_scalar(
                out=acsBD[:].rearrange("p (c u l) -> p c u l", c=8, u=2)[:, :, h, :],
                in0=acs[:].rearrange("p (c l) -> p c l", c=8), scalar1=eye2[:, h:h + 1], scalar2=None, op0=ALU.mult),
                reads=["acs", "eye2"], writes=["acsBD"])
        for c in range(8):
            P.op("dve", lambda e, c=c: e.tensor_scalar(
                out=lastbc[:, c, :], in0=ones_f[0:2, :], scalar1=acs[:, c * 64 + 63:c * 64 + 64], scalar2=None, op0=ALU.mult),
                reads=["acs", "ones_f"], writes=["lastbc"])

    def attention_steps(b):
        nkb = 4 * b + 4

        def scores(kb):
            j = kb - 4 * b
            c0 = max(0, j) * 128
            st = kb % 2
            for m in range(2):
                P.op("pe", lambda e, kb=kb, m=m, c0=c0: e.matmul(
                    psS[m][:, c0:512], lhsT=KT[m * 64:(m + 1) * 64, kb * 128:(kb + 1) * 128],
                    rhs=QT[m * 64:(m + 1) * 64, c0:512], start=True, stop=True),
                    reads=["KT%d" % (kb // 4), "QT"], writes=["ps%d" % m])
                P.op("act", lambda e, m=m, c0=c0, st=st: e.activation(
                    out=E[st][m][:, c0:512], in_=psS[m][:, c0:512], func=AF.Exp, scale=0.125),
                    reads=["ps%d" % m], writes=["E%d%d" % (st, m)])
                if j >= 0:
                    P.op("pool", lambda e, m=m, c0=c0, st=st: e.memset(E[st][m][64:128, c0:c0 + 64], 0.0),
                         reads=[], writes=["E%d%d" % (st, m)])

        def accum(kb):
            j = kb - 4 * b
            c0 = max(0, j) * 128
            st = kb % 2
            for m in range(2):
                P.op("pe", lambda e, kb=kb, m=m, c0=c0, st=st: e.matmul(
                    psO[m][:, c0:512], lhsT=V[:, kb, :], rhs=E[st][m][:, c0:512],
                    start=(kb == 0), stop=(kb == nkb - 1)),
                    reads=["V%d" % (kb // 4), "E%d%d" % (st, m)], writes=["ps%d" % (2 + m)])
                P.op("pe", lambda e, kb=kb, m=m, c0=c0, st=st: e.matmul(
                    psD[m][:, c0:512], lhsT=ones_b[:], rhs=E[st][m][:, c0:512],
                    start=(kb == 0), stop=(kb == nkb - 1)),
                    reads=["ones_b", "E%d%d" % (st, m)], writes=["ps%d" % (4 + m)])

        scores(0)
        for kb in range(nkb):
            if kb + 1 < nkb:
                scores(kb + 1)
            accum(kb)
            yield
        tA, tB, tC, tD = tmp["A"][:], tmp["B"][:], tmp["C"][:], tmp["D"][:]
        P.op("act", lambda e: e.activation(out=tA, in_=psD[0][:, :], func=AF.Ln), reads=["ps4"], writes=["tA"])
        P.op("act", lambda e: e.activation(out=tB, in_=psD[1][:, :], func=AF.Ln), reads=["ps5"], writes=["tB"])
        P.op("act", lambda e: e.activation(out=tA, in_=tA, func=AF.Exp, scale=-1.0), reads=["tA"], writes=["tA"])
        P.op("act", lambda e: e.activation(out=tB, in_=tB, func=AF.Exp, scale=-1.0), reads=["tB"], writes=["tB"])
        P.op("dve", lambda e: e.tensor_tensor(out=tA, in0=tA, in1=psO[0][:, :], op=ALU.mult), reads=["tA", "ps2"], writes=["tA"])
        P.op("dve", lambda e: e.tensor_tensor(out=tB, in0=tB, in1=psO[1][:, :], op=ALU.mult), reads=["tB", "ps3"], writes=["tB"])
        P.op("dve", lambda e: e.scalar_tensor_tensor(out=tA, in0=tB, scalar=lams[:, 2:3], in1=tA, op0=ALU.mult, op1=ALU.add),
             reads=["tA", "tB", "neglam"], writes=["tA"])
        P.op("act", lambda e: e.activation(out=tB, in_=tA, func=AF.Square), reads=["tA"], writes=["tB"])
        P.op("pe", lambda e: e.matmul(psS[0][:, :], lhsT=ones_f[:], rhs=tB, start=True, stop=True),
             reads=["ones_f", "tB"], writes=["ps0"])
        P.op("act", lambda e: e.activation(out=tB, in_=psS[0][:, :], func=AF.Ln, bias=EPS, scale=1.0 / 128),
             reads=["ps0"], writes=["tB"])
        P.op("act", lambda e: e.activation(out=tB, in_=tB, func=AF.Exp, scale=-0.5), reads=["tB"], writes=["tB"])
        P.op("dve", lambda e: e.tensor_tensor(out=tA, in0=tA, in1=tB, op=ALU.mult), reads=["tA", "tB"], writes=["tA"])
        P.op("dve", lambda e: e.scalar_tensor_tensor(out=yatt_sb[:], in0=tA, scalar=lams[:, 3:4], in1=sg[:],
                                                     op0=ALU.mult, op1=ALU.mult),
             reads=["tA", "coef", "sg"], writes=["yatt_sb"])
        P.dma("sp", lambda e: e.dma_start(out=yatt[:, b * 512:(b + 1) * 512], in_=yatt_sb[:]), "yatt",
              reads=["yatt_sb"], is_output=True)
        yield

    def ssd_steps(b):
        for c in range(8):
            P.op("pe", lambda e, c=c: e.matmul(rcol[:, 2 * c:2 * c + 1], lhsT=dtBD[:, c * 128:(c + 1) * 128],
                                               rhs=ones_f[0:2, 0:1], start=True, stop=True),
                 reads=["dtBD", "ones_f"], writes=["ps6"])
            P.op("pe", lambda e, c=c: e.matmul(rcol[:, 2 * c + 1:2 * c + 2], lhsT=acsBD[:, c * 128:(c + 1) * 128],
                                               rhs=ones_f[0:2, 0:1], start=True, stop=True),
                 reads=["acsBD", "ones_f"], writes=["ps6"])
        P.op("act", lambda e: e.activation(out=cols[:], in_=rcol, func=AF.Copy), reads=["ps6"], writes=["cols"])
        P.op("dve", lambda e: e.tensor_scalar(
            out=negc[:], in0=cols[:].rearrange("p (c t) -> p c t", t=2)[:, :, 1], scalar1=-1.0, scalar2=None, op0=ALU.mult),
            reads=["cols"], writes=["negc"])
        for c in range(8):
            cs = slice(c * 64, (c + 1) * 64)
            cd = slice(c * 128, (c + 1) * 128)
            P.op("pe", lambda e, cs=cs, cd=cd: e.matmul(rG, lhsT=BTdup[:, cd], rhs=CT[:, cs], start=True, stop=True),
                 reads=["BTdup", "CT"], writes=["ps6"])
            P.op("pe", lambda e, cs=cs: e.matmul(rR, lhsT=hsel[:], rhs=acs[:, cs], start=True, stop=True),
                 reads=["hsel", "acs"], writes=["ps6"])
            P.op("pe", lambda e, cd=cd: e.matmul(rX, lhsT=xsdup[:, cd], rhs=identb[:], start=True, stop=True),
                 reads=["xsdup", "identb"], writes=["ps6"])
            P.op("pe", lambda e, c=c: e.matmul(rC, lhsT=lastbc[:, c, :], rhs=hsel[:], start=True, stop=True),
                 reads=["lastbc", "hsel"], writes=["ps7"])
            P.op("pe", lambda e, cd=cd: e.matmul(rB, lhsT=BTdup[:, cd], rhs=identb[:], start=True, stop=True),
                 reads=["BTdup", "identb"], writes=["ps7"])
            P.op("dve", lambda e: e.tensor_tensor(out=Gm[:], in0=rG, in1=mask2[:], op=ALU.mult),
                 reads=["ps6", "mask2"], writes=["Gm"])
            P.op("dve", lambda e, c=c: e.tensor_scalar(out=Dm[:], in0=rR, scalar1=negc[:, c:c + 1], scalar2=0.0,
                                                       op0=ALU.add, op1=ALU.min),
                 reads=["ps6", "negc"], writes=["Dm"])
            P.op("act", lambda e: e.activation(out=E2[:], in_=Dm[:], func=AF.Exp), reads=["Dm"], writes=["E2"])
            P.op("act", lambda e: e.activation(out=R2e[:], in_=rR, func=AF.Exp), reads=["ps6"], writes=["R2e"])
            P.op("dve", lambda e, c=c: e.scalar_tensor_tensor(
                out=XdtBD[:], in0=rX, scalar=cols[:, 2 * c:2 * c + 1], in1=bdm[:], op0=ALU.mult, op1=ALU.mult),
                reads=["ps6", "cols", "bdm"], writes=["XdtBD"])
            P.op("dve", lambda e: e.tensor_tensor(out=M2[:], in0=Gm[:], in1=E2[:], op=ALU.mult),
                 reads=["Gm", "E2"], writes=["M2"])
            P.op("act", lambda e: e.activation(out=XdecBD[:], in_=XdtBD[:], func=AF.Identity, scale=E2[:, 63:64]),
                 reads=["XdtBD", "E2"], writes=["XdecBD"])
            P.op("act", lambda e: e.activation(out=Btok[:], in_=rB, func=AF.Copy), reads=["ps7"], writes=["Btok"])
            P.op("act", lambda e: e.activation(out=CD[:], in_=rC, func=AF.Exp), reads=["ps7"], writes=["CD"])
            P.op("act", lambda e: e.activation(out=Sbf[:], in_=Sst[:], func=AF.Copy), reads=["Sst"], writes=["Sbf"])
            P.op("pe", lambda e, cs=cs: e.matmul(rYo, lhsT=Sbf[:], rhs=CT[:, cs], start=True, stop=True),
                 reads=["Sbf", "CT"], writes=["ps7"])
            P.op("pe", lambda e: e.matmul(rYd, lhsT=XdtBD[:], rhs=M2[:], start=True, stop=True),
                 reads=["XdtBD", "M2"], writes=["ps7"])
            P.op("pe", lambda e: e.matmul(rS, lhsT=Btok[:], rhs=XdecBD[:], start=True, stop=True),
                 reads=["Btok", "XdecBD"], writes=["ps7"])
            P.op("dve", lambda e: e.tensor_tensor(out=y1[:], in0=rYo, in1=R2e[:], op=ALU.mult),
                 reads=["ps7", "R2e"], writes=["y1"])
            P.op("dve", lambda e, cs=cs: e.tensor_tensor(out=ysT[:, cs], in0=y1[:], in1=rYd, op=ALU.add),
                 reads=["y1", "ps7"], writes=["ysT"])
            P.op("dve", lambda e: e.tensor_tensor(out=Sst[:], in0=Sst[:], in1=CD[:], op=ALU.mult),
                 reads=["Sst", "CD"], writes=["Sst"])
            P.op("dve", lambda e: e.tensor_tensor(out=Sst[:], in0=Sst[:], in1=rS, op=ALU.add),
                 reads=["Sst", "ps7"], writes=["Sst"])
            yield
        tE, tF = tmp["E"][:], tmp["F"][:]
        P.op("dve", lambda e: e.scalar_tensor_tensor(out=tE, in0=xsT[:], scalar=colp[:, 0:1], in1=ysT[:],
                                                     op0=ALU.mult, op1=ALU.add),
             reads=["xsT", "colp", "ysT"], writes=["tE"])
        P.op("dve", lambda e: e.tensor_tensor(out=tE, in0=tE, in1=sz[:], op=ALU.mult), reads=["tE", "sz"], writes=["tE"])
        P.op("act", lambda e: e.activation(out=tF, in_=tE, func=AF.Square), reads=["tE"], writes=["tF"])
        P.op("pe", lambda e: e.matmul(ps[6][0:1, :], lhsT=ones_f[:, 0:1], rhs=tF, start=True, stop=True),
             reads=["ones_f", "tF"], writes=["ps6"])
        P.op("act", lambda e: e.activation(out=ssq_sb[:], in_=ps[6][0:1, :], func=AF.Copy), reads=["ps6"], writes=["ssq_sb"])
        P.op("act", lambda e: e.activation(out=yssm_sb[:], in_=tE, func=AF.Identity, scale=colp[:, 1:2]),
             reads=["tE", "colp"], writes=["yssm_sb"])
        P.dma("sp", lambda e: e.dma_start(out=ssq[0:1, b * 512:(b + 1) * 512], in_=ssq_sb[:]), "ssqo",
              reads=["ssq_sb"], is_output=True)
        P.dma("sp", lambda e: e.dma_start(out=yssm[:, b * 512:(b + 1) * 512], in_=yssm_sb[:]), "yssm",
              reads=["yssm_sb"], is_output=True)
        yield

    def interleave(b):
        att = attention_steps(b)
        sd = ssd_steps(b)
        nk = 4 * b + 5
        ns = 9
        ia = isd = 0
        for _ in att:
            pass
        for _ in sd:
            pass
        return
        while ia < nk or isd < ns:
            if isd >= ns or (ia < nk and ia * ns <= isd * nk):
                next(att, None)
                ia += 1
            else:
                next(sd, None)
                isd += 1

    load_x(0)
    for b in range(NBLK):
        if b + 1 < NBLK:
            load_x(b + 1)
        inproj(b)
        interleave(b)
    P.emit()
    return nc


_NC = {}
BF = ml_dtypes.bfloat16


def _get(name, builder):
    if name not in _NC:
        _NC[name] = builder()
    return _NC[name]


def _launch(nc, in_maps):
    res = run_bass_kernel_spmd(nc, in_maps, core_ids=list(range(len(in_maps))))
    return res.results


_IDENTF = np.eye(128, dtype=np.float32)


def run_odd(xtok, xT_bf, w_in, w_grp, b_grp, scale, w_out, lng, lnb, cores=range(NCORES)):
    nc = _get("odd", lambda: build_tail(True))
    bg = np.ascontiguousarray(b_grp.reshape(16, 128).T)
    osc = np.ascontiguousarray(scale.reshape(16, 128).T)
    in_maps = []
    for c in cores:
        t0 = c * TPC
        xT_c = np.zeros((D, HALO + TPC), BF)
        xT_c[:, HALO:] = xT_bf[:, t0:t0 + TPC]
        if c > 0:
            xT_c[:, :HALO] = xT_bf[:, t0 - HALO:t0]
        inv = np.zeros((4, 16), np.float32)
        for gi in range(4):
            for i in range(16):
                inv[gi, i] = 1.0 / min(t0 + i + 1, 2 ** (gi + 1))
        in_maps.append(dict(xres=np.ascontiguousarray(xtok[t0:t0 + TPC]), w_out=w_out, lng=lng, lnb=lnb,
                            identf=_IDENTF, xT=xT_c, w_in=w_in, w_grp=w_grp, bgrp=bg, oscale=osc,
                            invtab=inv.reshape(-1)))
    res = _launch(nc, in_maps)
    xo = np.concatenate([r["xo"] for r in res], 0)
    xoT = np.concatenate([r["xoT"] for r in res], 1)
    return xo, xoT


def _even_consts():
    mask2 = np.zeros((128, 64), np.float32)
    for u in range(2):
        for s_ in range(64):
            mask2[u * 64 + s_, s_:] = 1.0
    bdm = np.zeros((128, 128), np.float32)
    bdm[:64, :64] = 1.0
    bdm[64:, 64:] = 1.0
    hsel = np.zeros((2, 128), np.float32)
    hsel[0, :64] = 1.0
    hsel[1, 64:] = 1.0
    return dict(identf=_IDENTF, mask2=mask2, bdmask=bdm, hsel=hsel, eye2=np.eye(2, dtype=np.float32))


def run_even(xT, w_in, conv_w, conv_b, dt_bias, a_log, d_skip, ssm_norm_g, lq1, lk1, lq2, lk2, subln_g,
             lam_init, cores=range(NCORES), nblk=S // 512):
    x_bf16 = xT.dtype != np.float32
    nc = _get(("even", x_bf16, nblk), lambda: build_even(x_bf16, nblk))
    consts = _even_consts()
    ar = np.arange(128)
    in_maps = []
    for c in cores:
        g = c // 4
        idx = np.concatenate([c * 128 + ar, 1024 + c * 128 + ar, 2048 + c * 128 + ar, 3072 + c * 128 + ar,
                              4096 + c * 128 + ar, 5120 + c * 128 + ar, 6144 + g * 128 + ar, 6400 + g * 128 + ar,
                              6656 + 2 * c + np.arange(2)])
        wA = np.ascontiguousarray(w_in[:, idx])
        chans = [c * 128 + ar, 1024 + g * 128 + ar, 1280 + g * 128 + ar]
        cw = np.zeros((128, 12), np.float32)
        cb = np.zeros((128, 3), np.float32)
        for t in range(3):
            cw[:, 4 * t:4 * t + 4] = conv_w[:, chans[t]].T
            cb[:, t] = conv_b[chans[t]]
        colp = np.zeros((128, 8), np.float32)
        colp[:, 0] = np.repeat(d_skip[2 * c:2 * c + 2], 64)
        colp[:, 1] = ssm_norm_g[c * 128:(c + 1) * 128]
        colp[:, 2] = subln_g
        colp[:, 3] = lam_init
        colp[:, 4] = 1.0 - lam_init
        dtp = np.stack([dt_bias[2 * c:2 * c + 2], a_log[2 * c:2 * c + 2]], 1).astype(np.float32)
        lamv = np.concatenate([lq1, lk1, lq2, lk2]).astype(np.float32)
        m = dict(xT=xT, wA=wA, cw=cw, cb=cb, colp=colp, dtp=dtp, lamv=lamv)
        m.update(consts)
        in_maps.append(m)
    res = _launch(nc, in_maps)
    return [r["yatt"] for r in res], [r["yssm"] for r in res], [r["ssq"] for r in res]


def _sel8():
    sel = np.zeros((8, 256), np.float32)
    sel[0:4, 0:128] = 1.0
    sel[4:8, 128:256] = 1.0
    return sel


def run_evtail(xtok, yT, ssq8, w_out, lng, lnb, cores=range(NCORES)):
    nc = _get("evtail", lambda: build_tail(False))
    sel = _sel8()
    in_maps = []
    for c in cores:
        t0 = c * TPC
        in_maps.append(dict(xres=np.ascontiguousarray(xtok[t0:t0 + TPC]), w_out=w_out, lng=lng, lnb=lnb,
                            identf=_IDENTF, yT=np.ascontiguousarray(yT[:, t0:t0 + TPC]),
                            ssq8=np.ascontiguousarray(ssq8[:, t0:t0 + TPC]), sel=sel))
    res = _launch(nc, in_maps)
    xo = np.concatenate([r["xo"] for r in res], 0)
    xoT = np.concatenate([r["xoT"] for r in res], 1)
    return xo, xoT


def kernel(x, ev_w_in, ev_conv_w, ev_conv_b, ev_dt_bias, ev_a_log, ev_d_skip, ev_ssm_norm_g,
           ev_lambda_q1, ev_lambda_k1, ev_lambda_q2, ev_lambda_k2, ev_subln_g, ev_w_out,
           od_w_in, od_w_grp, od_b_grp, od_scale, od_w_out, ln_g, ln_b):
    f = lambda a: np.asarray(a, dtype=np.float32)
    xtok = np.ascontiguousarray(f(x)[0])
    xT = np.ascontiguousarray(xtok.T)
    for l in range(DEPTH):
        i = l // 2
        if l % 2 == 0:
            lam_init = 0.8 - 0.6 * math.exp(-0.3 * l)
            ya, ys, sq = run_even(xT, f(ev_w_in[i]), f(ev_conv_w[i]), f(ev_conv_b[i]), f(ev_dt_bias[i]), f(ev_a_log[i]),
                                  f(ev_d_skip[i]), f(ev_ssm_norm_g[i]), f(ev_lambda_q1[i]), f(ev_lambda_k1[i]),
                                  f(ev_lambda_q2[i]), f(ev_lambda_k2[i]), f(ev_subln_g[i]), lam_init)
            yT = np.concatenate(list(ya) + list(ys), 0)
            ssq8 = np.concatenate(list(sq), 0)
            xtok, xT = run_evtail(xtok, yT, ssq8, f(ev_w_out[i]), f(ln_g[l]), f(ln_b[l]))
        else:
            xtok, xT = run_odd(xtok, xT, f(od_w_in[i]), f(od_w_grp[i]), f(od_b_grp[i]), f(od_scale[i]), f(od_w_out[i]),
                               f(ln_g[l]), f(ln_b[l]))
    return np.ascontiguousarray(xtok[None]).astype(np.float32)
```

```python
import math
import numpy as np
import ml_dtypes
import concourse.bass as bass
import concourse.mybir as mybir
from concourse.bass_utils import run_bass_kernel_spmd

F32 = mybir.dt.float32
BF16 = mybir.dt.bfloat16
AF = mybir.ActivationFunctionType
ALU = mybir.AluOpType

NCORES = 8
D = 2048
S = 8192
TPC = S // NCORES
HALO = 16
DEPTH = 4
ALPHA = (2.0 * DEPTH) ** 0.25
EPS = 1e-5
NPJ = 1026


class Prog:
    ENG = ("pe", "act", "dve", "pool", "sp")

    def __init__(self, nc):
        self.nc = nc
        self.q = {e: [] for e in self.ENG}
        self.sem = {e: nc.alloc_semaphore("sem_" + e) for e in ("pe", "act", "dve", "pool")}
        self.cnt = {e: 0 for e in self.sem}
        self.last_w = {}
        self.reads = {}
        self.waited = {e: {} for e in self.ENG}
        self.dsem = {}
        self.out_tokens = []

    def _deps(self, eng, reads, writes):
        toks = []
        for k in list(reads) + list(writes):
            t = self.last_w.get(k)
            if t is not None:
                toks.append(t)
        for k in writes:
            toks.extend(self.reads.get(k, ()))
        waits = {}
        for (sem, val, src) in toks:
            if src == "pe" and eng == "pe":
                continue
            sid = id(sem)
            if self.waited[eng].get(sid, (None, 0))[1] >= val:
                continue
            if sid not in waits or waits[sid][1] < val:
                waits[sid] = (sem, val)
        for sid, (sem, val) in waits.items():
            self.waited[eng][sid] = (sem, val)
        return list(waits.values())

    def _commit(self, tok, reads, writes):
        for k in reads:
            self.reads.setdefault(k, []).append(tok)
        for k in writes:
            self.last_w[k] = tok
            self.reads[k] = []

    def op(self, eng, fn, reads=(), writes=()):
        waits = self._deps(eng, reads, writes)
        self.cnt[eng] += 1
        tok = (self.sem[eng], self.cnt[eng], eng)
        self.q[eng].append((waits, fn, (self.sem[eng], 1)))
        self._commit(tok, reads, writes)

    def dma(self, queue, fn, semname, reads=(), writes=(), is_output=False):
        if semname not in self.dsem:
            self.dsem[semname] = [self.nc.alloc_semaphore("d_" + semname), 0]
        ent = self.dsem[semname]
        waits = self._deps(queue, reads, writes)
        ent[1] += 16
        tok = (ent[0], ent[1], "dma")
        self.q[queue].append((waits, fn, (ent[0], 16)))
        self._commit(tok, reads, writes)
        if is_output:
            self.out_tokens.append(tok)

    def emit(self):
        nc = self.nc
        fin = {}
        for (sem, val, _) in self.out_tokens:
            if id(sem) not in fin or fin[id(sem)][1] < val:
                fin[id(sem)] = (sem, val)
        q = self.q

        def run(e, name, final=False):
            for waits, fn, (sem, amt) in q[name]:
                for (ws, wv) in waits:
                    e.wait_ge(ws, wv)
                fn(e).then_inc(sem, amt)
            if final:
                for (ws, wv) in fin.values():
                    e.wait_ge(ws, wv)

        with nc.Block() as block:
            @block.sync
            def _(e):
                run(e, "sp", final=True)

            @block.tensor
            def _(e):
                run(e, "pe")

            @block.scalar
            def _(e):
                run(e, "act")

            @block.vector
            def _(e):
                run(e, "dve")

            @block.gpsimd
            def _(e):
                run(e, "pool")


def PK(pt):
    return pt.name


def _din(nc, name, shape, dt=F32):
    return nc.dram_tensor(name, list(shape), dt, kind="ExternalInput").ap()


def _dout(nc, name, shape, dt=F32):
    return nc.dram_tensor(name, list(shape), dt, kind="ExternalOutput").ap()


def build_tail(front):
    nc = bass.Bass("TRN2", target_bir_lowering=False)
    P = Prog(nc)
    NT = TPC // 128
    xres = _din(nc, "xres", [TPC, D])
    w_out = _din(nc, "w_out", [D, D])
    lng = _din(nc, "lng", [D])
    lnb = _din(nc, "lnb", [D])
    xo = _dout(nc, "xo", [TPC, D])
    xoT = _dout(nc, "xoT", [D, TPC], BF16)
    identf_d = _din(nc, "identf", [128, 128])
    if front:
        xT = _din(nc, "xT", [D, HALO + TPC], BF16)
        w_in = _din(nc, "w_in", [D, 2 * D])
        w_grp = _din(nc, "w_grp", [4, 512, 512])
        bgrp = _din(nc, "bgrp", [128, 16])
        oscale = _din(nc, "oscale", [128, 16])
        invtab = _din(nc, "invtab", [4 * 16])
    else:
        yT_d = _din(nc, "yT", [D, TPC], BF16)
        ssq8 = _din(nc, "ssq8", [8, TPC])
        sel_d = _din(nc, "sel", [8, 2 * 128])

    NTOK = HALO + TPC
    XT_E = 16 * NTOK
    WB_E = 16 * 512
    big = nc.alloc_sbuf_tensor("big", [128, max(XT_E + 2 * WB_E, 16 * D)], BF16)
    wout_sb = big[:, 0:16 * D].rearrange("p (k n) -> p k n", k=16)
    yT = nc.alloc_sbuf_tensor("yT_sb", [128, 16, TPC], BF16)
    lng_sb = nc.alloc_sbuf_tensor("lng_sb", [128, D], F32)
    lnb_sb = nc.alloc_sbuf_tensor("lnb_sb", [128, D], F32)
    xr2 = [nc.alloc_sbuf_tensor("xr_sb%d" % i, [128, D], F32) for i in range(2)]
    z2 = [nc.alloc_sbuf_tensor("z_sb%d" % i, [128, D], F32) for i in range(2)]
    zT_sb = nc.alloc_sbuf_tensor("zT_sb", [128, 4, 512], BF16)
    xoT_v = xoT.rearrange("(k p) n -> p k n", p=128)
    stats = nc.alloc_sbuf_tensor("stats", [128, 4, 6], F32)
    mv = nc.alloc_sbuf_tensor("mv", [128, 2], F32)
    rstd = nc.alloc_sbuf_tensor("rstd", [128, 1], F32)
    nmr = nc.alloc_sbuf_tensor("nmr", [128, 1], F32)
    identf = nc.alloc_sbuf_tensor("identf_sb", [128, 128], F32)
    ps = [nc.alloc_psum_tensor("ps%d" % i, [128, 512], F32) for i in range(8)]

    P.dma("sp", lambda e: e.dma_start(out=lng_sb[:], in_=lng.partition_broadcast(128)), "c1", writes=["lng"])
    P.dma("sp", lambda e: e.dma_start(out=lnb_sb[:], in_=lnb.partition_broadcast(128)), "c2", writes=["lnb"])
    P.dma("sp", lambda e: e.dma_start(out=identf[:], in_=identf_d), "c3", writes=["identf"])

    if front:
        xT_sb = big[:, 0:XT_E].rearrange("p (k n) -> p k n", k=16)
        wbuf = [big[:, XT_E + i * WB_E: XT_E + (i + 1) * WB_E].rearrange("p (k n) -> p k n", k=16) for i in range(2)]
        wg_sb = nc.alloc_sbuf_tensor("wg_sb", [128, 4, 4, 512], BF16)
        bg_sb = nc.alloc_sbuf_tensor("bg_sb", [128, 16], F32)
        os_sb = nc.alloc_sbuf_tensor("os_sb", [128, 16], F32)
        inv_sb = nc.alloc_sbuf_tensor("inv_sb", [128, 4, 16], F32)
        vT = nc.alloc_sbuf_tensor("vT", [128, NTOK], F32)
        sA = nc.alloc_sbuf_tensor("sA", [128, NTOK], F32)
        sB = nc.alloc_sbuf_tensor("sB", [128, NTOK], F32)
        pooled = nc.alloc_sbuf_tensor("pooled", [128, 4, TPC], BF16)
        ge = nc.alloc_sbuf_tensor("ge", [128, TPC], F32)
        gs = nc.alloc_sbuf_tensor("gs", [128, TPC], F32)
        mt = nc.alloc_sbuf_tensor("mt", [128, TPC], F32)

        xT_v = xT.rearrange("(k p) n -> p k n", p=128)
        for h in range(4):
            P.dma("sp" if h % 2 == 0 else "act", lambda e, h=h: e.dma_start(out=xT_sb[:, 4 * h:4 * h + 4, :], in_=xT_v[:, 4 * h:4 * h + 4, :]),
                  "xT", writes=["xT"])
        wg_v = w_grp.rearrange("g (c p) d -> p g c d", p=128)
        P.dma("sp", lambda e: e.dma_start(out=bg_sb[:], in_=bgrp), "c4", writes=["bg"])
        P.dma("sp", lambda e: e.dma_start(out=os_sb[:], in_=oscale), "c5", writes=["os"])
        P.dma("sp", lambda e: e.dma_start(out=inv_sb[:].rearrange("p g i -> p (g i)"), in_=invtab.partition_broadcast(128)),
              "c6", writes=["inv"])
        w_in_v = w_in.rearrange("(k p) n -> p k n", p=128)

        def load_w(i, col0):
            b = i % 2
            P.dma("pool", lambda e: e.dma_start(out=wbuf[b][:, :, :], in_=w_in_v[:, :, col0:col0 + 512]),
                  "wbuf%d" % b, writes=["wbuf%d" % b])

        order = []
        for gi in range(4):
            order.append(gi * 512)
            order.append(D + gi * 512)
        load_w(0, order[0])
        for g in range(4):
            P.dma("pool", lambda e, g=g: e.dma_start(out=wg_sb[:, g, :, :], in_=wg_v[:, g, :, :]), "wg", writes=["wg"])
        load_w(1, order[1])

        for gi in range(4):
            wv = 2 * gi
            wgt = 2 * gi + 1
            w = 2 ** (gi + 1)
            for ft in range(4):
                b = wv % 2
                regs = [(ps[0], 0, HALO, 512), (ps[1], 0, HALO + 512, 512), (ps[2], 0, 0, HALO)]
                for (pt, pc, tc0, n) in regs:
                    for kc in range(16):
                        P.op("pe", lambda e, pt=pt, pc=pc, tc0=tc0, n=n, kc=kc, b=b, ft=ft: e.matmul(
                            pt[:, pc:pc + n], lhsT=wbuf[b][:, kc, ft * 128:(ft + 1) * 128],
                            rhs=xT_sb[:, kc, tc0:tc0 + n], start=(kc == 0), stop=(kc == 15)),
                            reads=["wbuf%d" % b, "xT"], writes=[PK(pt)])
                P.op("act", lambda e: e.activation(out=vT[:, HALO:HALO + 512], in_=ps[0][:, :], func=AF.Copy),
                     reads=["ps0"], writes=["vT"])
                P.op("act", lambda e: e.activation(out=vT[:, HALO + 512:NTOK], in_=ps[1][:, :], func=AF.Copy),
                     reads=["ps1"], writes=["vT"])
                P.op("act", lambda e: e.activation(out=vT[:, 0:HALO], in_=ps[2][:, 0:HALO], func=AF.Copy),
                     reads=["ps2"], writes=["vT"])
                src, skey = vT, "vT"
                dsts = [(sA, "sA"), (sB, "sB")]
                sh = 1
                lo = 0
                for st in range(gi + 1):
                    dst, dkey = dsts[st % 2]
                    lo = lo + sh
                    P.op("dve",
                         lambda e, dst=dst, src=src, lo=lo, sh=sh: e.tensor_tensor(
                             out=dst[:, lo:NTOK], in0=src[:, lo:NTOK], in1=src[:, lo - sh:NTOK - sh], op=ALU.add),
                         reads=[skey], writes=[dkey])
                    src, skey = dst, dkey
                    sh *= 2
                P.op("dve", lambda e, src=src, w=w, ft=ft: e.scalar_tensor_tensor(
                    out=pooled[:, ft, HALO:TPC], in0=src[:, 2 * HALO:NTOK], scalar=1.0 / w,
                    in1=vT[:, 2 * HALO:NTOK], op0=ALU.mult, op1=ALU.subtract),
                    reads=[skey, "vT"], writes=["pooled%d" % ft])
                P.op("dve", lambda e, src=src, gi=gi: e.tensor_tensor(
                    out=src[:, HALO:2 * HALO], in0=src[:, HALO:2 * HALO], in1=inv_sb[:, gi, :], op=ALU.mult),
                    reads=[skey, "inv"], writes=[skey])
                P.op("dve", lambda e, src=src, ft=ft: e.tensor_tensor(
                    out=pooled[:, ft, 0:HALO], in0=src[:, HALO:2 * HALO], in1=vT[:, HALO:2 * HALO], op=ALU.subtract),
                    reads=[skey, "vT"], writes=["pooled%d" % ft])
            if gi < 3:
                load_w(wv + 2, order[wv + 2])
            for dti in range(4):
                dt_ = gi * 4 + dti
                b = wgt % 2
                for hf in range(2):
                    pt = ps[3 + hf]
                    for kc in range(16):
                        P.op("pe", lambda e, pt=pt, hf=hf, kc=kc, b=b, dti=dti: e.matmul(
                            pt[:, :], lhsT=wbuf[b][:, kc, dti * 128:(dti + 1) * 128],
                            rhs=xT_sb[:, kc, HALO + hf * 512:HALO + (hf + 1) * 512],
                            start=(kc == 0), stop=(kc == 15)),
                            reads=["wbuf%d" % b, "xT"], writes=[PK(pt)])
                for hf in range(2):
                    pt = ps[5 + hf]
                    for cc in range(4):
                        P.op("pe", lambda e, pt=pt, hf=hf, cc=cc, gi=gi, dti=dti: e.matmul(
                            pt[:, :], lhsT=wg_sb[:, gi, cc, dti * 128:(dti + 1) * 128],
                            rhs=pooled[:, cc, hf * 512:(hf + 1) * 512], start=(cc == 0), stop=(cc == 3)),
                            reads=["wg", "pooled%d" % cc], writes=[PK(pt)])
                for hf in range(2):
                    sl = slice(hf * 512, (hf + 1) * 512)
                    P.op("act", lambda e, hf=hf, sl=sl: e.activation(out=ge[:, sl], in_=ps[3 + hf][:, :], func=AF.Exp, scale=-1.0),
                         reads=["ps%d" % (3 + hf)], writes=["ge%d" % hf])
                    P.op("act", lambda e, sl=sl: e.activation(out=ge[:, sl], in_=ge[:, sl], func=AF.Ln, bias=1.0, scale=1.0),
                         reads=["ge%d" % hf], writes=["ge%d" % hf])
                    P.op("act", lambda e, sl=sl: e.activation(out=ge[:, sl], in_=ge[:, sl], func=AF.Exp, scale=-1.0),
                         reads=["ge%d" % hf], writes=["ge%d" % hf])
                    P.op("dve", lambda e, hf=hf, sl=sl: e.tensor_tensor(out=gs[:, sl], in0=ge[:, sl], in1=ps[3 + hf][:, :], op=ALU.mult),
                         reads=["ge%d" % hf, "ps%d" % (3 + hf)], writes=["gs%d" % hf])
                    P.op("act", lambda e, hf=hf, sl=sl, dt_=dt_: e.activation(
                        out=mt[:, sl], in_=ps[5 + hf][:, :], func=AF.Identity,
                        bias=bg_sb[:, dt_:dt_ + 1], scale=1.0),
                        reads=["ps%d" % (5 + hf), "bg"], writes=["mt%d" % hf])
                    P.op("dve", lambda e, sl=sl, dt_=dt_: e.scalar_tensor_tensor(
                        out=yT[:, dt_, sl], in0=mt[:, sl], scalar=os_sb[:, dt_:dt_ + 1], in1=gs[:, sl],
                        op0=ALU.mult, op1=ALU.mult),
                        reads=["mt%d" % hf, "gs%d" % hf, "os"], writes=["yT"])
            if gi < 3:
                load_w(wgt + 2, order[wgt + 2])
    else:
        ssq_sb = nc.alloc_sbuf_tensor("ssq_sb", [8, TPC], F32)
        sel_sb = nc.alloc_sbuf_tensor("sel_sb", [8, 256], F32)
        rs_sb = nc.alloc_sbuf_tensor("rs_sb", [128, 2, TPC], F32)
        yT_v = yT_d.rearrange("(k p) n -> p k n", p=128)
        for h in range(4):
            P.dma("sp", lambda e, h=h: e.dma_start(out=yT[:, 4 * h:4 * h + 4, :], in_=yT_v[:, 4 * h:4 * h + 4, :]),
                  "yT", writes=["yT"])
        P.dma("sp", lambda e: e.dma_start(out=ssq_sb[:], in_=ssq8), "c7", writes=["ssq"])
        P.dma("sp", lambda e: e.dma_start(out=sel_sb[:], in_=sel_d), "c8", writes=["sel"])
        for g in range(2):
            for hf in range(2):
                pt = ps[2 * g + hf]
                P.op("pe", lambda e, pt=pt, g=g, hf=hf: e.matmul(
                    pt[:, :], lhsT=sel_sb[:, g * 128:(g + 1) * 128], rhs=ssq_sb[:, hf * 512:(hf + 1) * 512],
                    start=True, stop=True), reads=["sel", "ssq"], writes=[PK(pt)])
                P.op("act", lambda e, pt=pt, g=g, hf=hf: e.activation(
                    out=rs_sb[:, g, hf * 512:(hf + 1) * 512], in_=pt[:, :], func=AF.Ln, bias=EPS, scale=1.0 / 512),
                    reads=[PK(pt)], writes=["rs%d%d" % (g, hf)])
                P.op("act", lambda e, g=g, hf=hf: e.activation(
                    out=rs_sb[:, g, hf * 512:(hf + 1) * 512], in_=rs_sb[:, g, hf * 512:(hf + 1) * 512],
                    func=AF.Exp, scale=-0.5),
                    reads=["rs%d%d" % (g, hf)], writes=["rs%d%d" % (g, hf)])
            for kc in range(4):
                k = 8 + 4 * g + kc
                P.op("dve", lambda e, k=k, g=g: e.tensor_tensor(
                    out=yT[:, k, :], in0=yT[:, k, :], in1=rs_sb[:, g, :], op=ALU.mult),
                    reads=["yT", "rs%d0" % g, "rs%d1" % g], writes=["yT"])

    w_out_v = w_out.rearrange("(k p) n -> p k n", p=128)
    fkeys = ["xT", "wbuf0", "wbuf1"] if front else []
    for nb in range(4):
        P.dma("pool", lambda e, nb=nb: e.dma_start(out=wout_sb[:, :, nb * 512:(nb + 1) * 512],
                                                   in_=w_out_v[:, :, nb * 512:(nb + 1) * 512]),
              "wout%d" % nb, writes=["wout%d" % nb] + (fkeys if nb == 0 else []))
    def tail_mm(tt):
        q = tt % 2
        P.dma("sp", lambda e: e.dma_start(out=xr2[q][:], in_=xres[tt * 128:(tt + 1) * 128, :]), "xr%d" % q,
              writes=["xr%d" % q])
        for nb in range(4):
            pt = ps[4 * q + nb]
            for kc in range(16):
                P.op("pe", lambda e, pt=pt, kc=kc, nb=nb: e.matmul(
                    pt[:, :], lhsT=yT[:, kc, tt * 128:(tt + 1) * 128], rhs=wout_sb[:, kc, nb * 512:(nb + 1) * 512],
                    start=(kc == 0), stop=(kc == 15)), reads=["yT", "wout%d" % nb], writes=[PK(pt)])

    def tail_post(tt, after_ln=None):
        q = tt % 2
        z_sb = z2[q]
        zk = ["z%d_%d" % (q, i) for i in range(4)]
        for nb in range(4):
            pt = ps[4 * q + nb]
            P.op("dve", lambda e, pt=pt, nb=nb: e.scalar_tensor_tensor(
                out=z_sb[:, nb * 512:(nb + 1) * 512], in0=xr2[q][:, nb * 512:(nb + 1) * 512], scalar=ALPHA,
                in1=pt[:, :], op0=ALU.mult, op1=ALU.add),
                reads=["xr%d" % q, PK(pt)], writes=[zk[nb]])
            P.op("dve", lambda e, nb=nb: e.bn_stats(out=stats[:, nb, :], in_=z_sb[:, nb * 512:(nb + 1) * 512]),
                 reads=[zk[nb]], writes=["stats%d" % nb])
        if after_ln is not None:
            after_ln()
        P.op("dve", lambda e: e.bn_aggr(out=mv[:], in_=stats[:].rearrange("p a b -> p (a b)")),
             reads=["stats%d" % i for i in range(4)], writes=["mv"])
        P.op("act", lambda e: e.activation(out=rstd[:], in_=mv[:, 1:2], func=AF.Ln, bias=EPS, scale=1.0),
             reads=["mv"], writes=["rstd"])
        P.op("act", lambda e: e.activation(out=rstd[:], in_=rstd[:], func=AF.Exp, scale=-0.5),
             reads=["rstd"], writes=["rstd"])
        P.op("dve", lambda e: e.scalar_tensor_tensor(out=nmr[:], in0=mv[:, 0:1], scalar=-1.0, in1=rstd[:],
                                                     op0=ALU.mult, op1=ALU.mult),
             reads=["mv", "rstd"], writes=["nmr"])
        P.op("act", lambda e: e.activation(out=z_sb[:], in_=z_sb[:], func=AF.Identity, bias=nmr[:], scale=rstd[:]),
             reads=zk + ["nmr", "rstd"], writes=zk)
        P.op("dve", lambda e: e.tensor_tensor(out=z_sb[:], in0=z_sb[:], in1=lng_sb[:], op=ALU.mult),
             reads=zk + ["lng"], writes=zk)
        P.op("dve", lambda e: e.tensor_tensor(out=z_sb[:], in0=z_sb[:], in1=lnb_sb[:], op=ALU.add),
             reads=zk + ["lnb"], writes=zk)
        P.dma("sp", lambda e: e.dma_start(out=xo[tt * 128:(tt + 1) * 128, :], in_=z_sb[:]), "xo%d" % q,
              reads=zk, is_output=True)
        for grp in range(4):
            pk = "ps%d" % (4 * q + grp)
            pt = ps[4 * q + grp]
            for j in range(4):
                kc = grp * 4 + j
                P.op("pe", lambda e, pt=pt, j=j, kc=kc: e.transpose(
                    pt[:, j * 128:(j + 1) * 128], z_sb[:, kc * 128:(kc + 1) * 128], identf[:]),
                    reads=zk + ["identf"], writes=[pk])
            P.op("act", lambda e, pt=pt, grp=grp: e.activation(out=zT_sb[:, grp, :], in_=pt[:, :], func=AF.Copy),
                 reads=[pk], writes=["zT%d" % grp])
            P.dma("act", lambda e, grp=grp: e.dma_start(
                out=xoT_v[:, grp * 4:grp * 4 + 4, tt * 128:(tt + 1) * 128],
                in_=zT_sb[:, grp, :].rearrange("p (k n) -> p k n", k=4)), "xoT%d" % grp,
                reads=["zT%d" % grp], is_output=True)

    tail_mm(0)
    for tt in range(NT):
        nxt = (lambda t=tt: tail_mm(t + 1)) if tt + 1 < NT else None
        tail_post(tt, after_ln=nxt)

    P.emit()
    return nc


def build_even(x_bf16, nblk=S // 512):
    nc = bass.Bass("TRN2", target_bir_lowering=False)
    P = Prog(nc)
    NBLK = nblk
    S = nblk * 512
    xT = _din(nc, "xT", [D, S], BF16 if x_bf16 else F32)
    wA = _din(nc, "wA", [D, NPJ])
    cw_d = _din(nc, "cw", [128, 12])
    cb_d = _din(nc, "cb", [128, 3])
    colp_d = _din(nc, "colp", [128, 8])
    dtp_d = _din(nc, "dtp", [2, 2])
    lamv_d = _din(nc, "lamv", [256])
    identf_d = _din(nc, "identf", [128, 128])
    mask2_d = _din(nc, "mask2", [128, 64])
    bdm_d = _din(nc, "bdmask", [128, 128])
    hsel_d = _din(nc, "hsel", [2, 128])
    eye2_d = _din(nc, "eye2", [2, 2])
    yatt = _dout(nc, "yatt", [128, S], BF16)
    yssm = _dout(nc, "yssm", [128, S], BF16)
    ssq = _dout(nc, "ssq", [1, S])

    A = nc.alloc_sbuf_tensor
    W_sb = A("W_sb", [128, 16, NPJ], BF16)
    xb = [A("xb%d" % i, [128, 16, 512], BF16) for i in range(2)]
    KT = A("KT", [128, S], BF16)
    V = A("V", [128, S // 128, 128], BF16)
    QT = A("QT", [128, 512], BF16)
    sg = A("sg", [128, 512], F32)
    sz = A("sz", [128, 512], F32)
    raw = A("raw", [128, 3, 515], F32)
    xsT = A("xsT", [128, 512], F32)
    xsdup = A("xsdup", [128, 1024], BF16)
    BTdup = A("BTdup", [128, 1024], BF16)
    CT = A("CT", [128, 512], BF16)
    tmp = {k: A("t" + k, [128, 512], F32) for k in "ABCDEF"}
    E = [[A("E%d%d" % (i, m), [128, 512], BF16) for m in range(2)] for i in range(2)]
    ysT = A("ysT", [128, 512], F32)
    yatt_sb = A("yatt_sb", [128, 512], BF16)
    yssm_sb = A("yssm_sb", [128, 512], BF16)
    ssq_sb = A("ssq_sb", [1, 512], F32)
    cw = A("cw_sb", [128, 12], F32)
    cb = A("cb_sb", [128, 3], F32)
    colp = A("colp_sb", [128, 8], F32)
    lamv = A("lamv_sb", [128, 256], F32)
    lamt = A("lamt", [128, 128], F32)
    lams = A("lams", [128, 4], F32)
    identb = A("identb", [128, 128], BF16)
    mask2 = A("mask2_sb", [128, 64], F32)
    bdm = A("bdm_sb", [128, 128], F32)
    ones_b = A("ones_b", [128, 128], BF16)
    ones_f = A("ones_f", [128, 128], F32)
    hsel = A("hsel_sb", [2, 128], F32)
    eye2 = A("eye2_sb", [2, 2], F32)
    dtp = A("dtp_sb", [2, 2], F32)
    Acol = A("Acol", [2, 1], F32)
    dt_sb = A("dt_sb", [2, 512], F32)
    a_sb = A("a_sb", [2, 512], F32)
    acs = A("acs", [2, 512], F32)
    dtBD = A("dtBD", [2, 1024], F32)
    acsBD = A("acsBD", [2, 1024], F32)
    lastbc = A("lastbc", [2, 8, 128], F32)
    cols = A("cols", [128, 16], F32)
    negc = A("negc", [128, 8], F32)
    Sst = A("Sst", [128, 128], F32)
    Sbf = A("Sbf", [128, 128], BF16)
    Gm = A("Gm", [128, 64], F32)
    Dm = A("Dm", [128, 64], F32)
    E2 = A("E2", [128, 64], F32)
    M2 = A("M2", [128, 64], BF16)
    R2e = A("R2e", [128, 64], F32)
    XdtBD = A("XdtBD", [128, 128], BF16)
    XdecBD = A("XdecBD", [128, 128], BF16)
    Btok = A("Btok", [128, 128], BF16)
    CD = A("CD", [128, 128], F32)
    y1 = A("y1", [128, 64], F32)
    ps = [nc.alloc_psum_tensor("ps%d" % i, [128, 512], F32) for i in range(8)]
    psS = [ps[0], ps[1]]
    psO = [ps[2], ps[3]]
    psD = [ps[4], ps[5]]
    rG, rR, rX, rcol = ps[6][:, 0:64], ps[6][:, 64:128], ps[6][:, 128:256], ps[6][:, 256:272]
    rB, rC, rS = ps[7][:, 0:128], ps[7][:, 128:256], ps[7][:, 256:384]
    rYo, rYd = ps[7][:, 384:448], ps[7][:, 448:512]

    def ld(q, dst, src, key):
        P.dma(q, lambda e: e.dma_start(out=dst, in_=src), key, writes=[key])

    ld("sp", cw[:], cw_d, "cw")
    ld("sp", cb[:], cb_d, "cb")
    ld("sp", colp[:], colp_d, "colp")
    ld("sp", dtp[:], dtp_d, "dtp")
    ld("sp", lamv[:], lamv_d.partition_broadcast(128), "lamv")
    ld("sp", mask2[:], mask2_d, "mask2")
    ld("sp", bdm[:], bdm_d, "bdm")
    ld("sp", hsel[:], hsel_d, "hsel")
    ld("sp", eye2[:], eye2_d, "eye2")
    ld("pool", identb[:], identf_d, "identb")
    P.op("pool", lambda e: e.memset(ones_b[:], 1.0), writes=["ones_b"])
    P.op("pool", lambda e: e.memset(ones_f[:], 1.0), writes=["ones_f"])
    P.op("pool", lambda e: e.memset(raw[:], 0.0), writes=["raw0", "raw1", "raw2"])
    P.op("pool", lambda e: e.memset(Sst[:], 0.0), writes=["Sst"])
    wA_v = wA.rearrange("(k p) n -> p k n", p=128)
    WG = [(0, 384), (384, 768), (768, NPJ)]

    def load_W(g):
        c0, c1 = WG[g]
        P.dma("pool", lambda e: e.dma_start(out=W_sb[:, :, c0:c1], in_=wA_v[:, :, c0:c1]), "W%d" % g, writes=["W%d" % g])

    def WK(c0):
        return "W%d" % min(c0 // 384, 2)
    xT_v = xT.rearrange("(k p) n -> p k n", p=128)

    def load_x(b):
        i = b % 2
        for h in range(2):
            if x_bf16:
                P.dma("sp" if h == 0 else "act", lambda e, h=h: e.dma_start(
                    out=xb[i][:, 8 * h:8 * h + 8, :], in_=xT_v[:, 8 * h:8 * h + 8, b * 512:(b + 1) * 512]),
                    "xb%d" % i, writes=["xb%d" % i])
            else:
                P.dma("pool", lambda e, h=h: e.dma_start(
                    out=xb[i][:, 8 * h:8 * h + 8, :], in_=xT_v[:, 8 * h:8 * h + 8, b * 512:(b + 1) * 512]),
                    "xb%d" % i, writes=["xb%d" % i])

    P.op("dve", lambda e: e.tensor_tensor(out=lamt[:, 0:64], in0=lamv[:, 0:64], in1=lamv[:, 64:128], op=ALU.mult),
         reads=["lamv"], writes=["lamt0"])
    P.op("dve", lambda e: e.tensor_tensor(out=lamt[:, 64:128], in0=lamv[:, 128:192], in1=lamv[:, 192:256], op=ALU.mult),
         reads=["lamv"], writes=["lamt1"])
    P.op("dve", lambda e: e.reduce_sum(out=lams[:, 0:1], in_=lamt[:, 0:64], axis=mybir.AxisListType.X),
         reads=["lamt0"], writes=["lams0"])
    P.op("dve", lambda e: e.reduce_sum(out=lams[:, 1:2], in_=lamt[:, 64:128], axis=mybir.AxisListType.X),
         reads=["lamt1"], writes=["lams1"])
    P.op("act", lambda e: e.activation(out=lams[:, 0:2], in_=lams[:, 0:2], func=AF.Exp),
         reads=["lams0", "lams1"], writes=["lams0", "lams1"])
    P.op("dve", lambda e: e.tensor_tensor(out=lams[:, 2:3], in0=lams[:, 1:2], in1=lams[:, 0:1], op=ALU.subtract),
         reads=["lams0", "lams1"], writes=["neglam"])
    P.op("dve", lambda e: e.tensor_tensor(out=lams[:, 2:3], in0=lams[:, 2:3], in1=colp[:, 3:4], op=ALU.subtract),
         reads=["neglam", "colp"], writes=["neglam"])
    P.op("dve", lambda e: e.tensor_tensor(out=lams[:, 3:4], in0=colp[:, 2:3], in1=colp[:, 4:5], op=ALU.mult),
         reads=["colp"], writes=["coef"])
    P.op("act", lambda e: e.activation(out=Acol[:], in_=dtp[:, 1:2], func=AF.Exp), reads=["dtp"], writes=["Acol"])
    P.op("dve", lambda e: e.tensor_scalar(out=Acol[:], in0=Acol[:], scalar1=-1.0, scalar2=None, op0=ALU.mult),
         reads=["Acol"], writes=["Acol"])

    def silu_from(src_ap, src_keys, dst_ap, dst_key, t, tkey):
        P.op("act", lambda e: e.activation(out=t, in_=src_ap, func=AF.Exp, scale=-1.0), reads=src_keys, writes=[tkey])
        P.op("act", lambda e: e.activation(out=t, in_=t, func=AF.Ln, bias=1.0, scale=1.0), reads=[tkey], writes=[tkey])
        P.op("act", lambda e: e.activation(out=t, in_=t, func=AF.Exp, scale=-1.0), reads=[tkey], writes=[tkey])
        P.op("dve", lambda e: e.tensor_tensor(out=dst_ap, in0=t, in1=src_ap, op=ALU.mult),
             reads=[tkey] + list(src_keys), writes=[dst_key])

    def inproj(b):
        xbi = xb[b % 2]
        xk = "xb%d" % (b % 2)
        bank = [0]

        def nextbank():
            bank[0] ^= 1
            return ps[bank[0]], "ps%d" % bank[0]

        def proj(c0):
            pt, pk = nextbank()
            for kc in range(16):
                P.op("pe", lambda e, kc=kc: e.matmul(pt[:, :], lhsT=W_sb[:, kc, c0:c0 + 128], rhs=xbi[:, kc, :],
                                                     start=(kc == 0), stop=(kc == 15)),
                     reads=[WK(c0), xk], writes=[pk])
            return pt, pk

        pt, pk = proj(0)
        P.op("act", lambda e, pt=pt: e.activation(out=QT[:], in_=pt[:, :], func=AF.Copy), reads=[pk], writes=["QT"])
        pt, pk = proj(128)
        P.op("act", lambda e, pt=pt: e.activation(out=KT[:, b * 512:(b + 1) * 512], in_=pt[:, :], func=AF.Copy),
             reads=[pk], writes=["KT%d" % b])
        pt, pk = nextbank()
        for i in range(4):
            for kc in range(16):
                P.op("pe", lambda e, pt=pt, i=i, kc=kc: e.matmul(
                    pt[:, i * 128:(i + 1) * 128], lhsT=xbi[:, kc, i * 128:(i + 1) * 128], rhs=W_sb[:, kc, 256:384],
                    start=(kc == 0), stop=(kc == 15)), reads=["W0", xk], writes=[pk])
        P.op("act", lambda e, pt=pt: e.activation(
            out=V[:, b * 4:(b + 1) * 4, :].rearrange("p a d -> p (a d)"), in_=pt[:, :], func=AF.Copy),
            reads=[pk], writes=["V%d" % b])
        pt, pk = proj(384)
        silu_from(pt[:, :], [pk], sg[:], "sg", tmp["A"][:], "tA")
        pt, pk = proj(512)
        silu_from(pt[:, :], [pk], sz[:], "sz", tmp["B"][:], "tB")
        for t in range(3):
            pt, pk = proj(640 + 128 * t)
            rk = "raw%d" % t
            P.op("act", lambda e, pt=pt, t=t: e.activation(out=raw[:, t, 3:515], in_=pt[:, :], func=AF.Copy),
                 reads=[pk], writes=[rk])
            acc = tmp["C"][:]
            P.op("dve", lambda e, t=t: e.tensor_scalar(
                out=acc, in0=raw[:, t, 3:515], scalar1=cw[:, 4 * t + 3:4 * t + 4], scalar2=cb[:, t:t + 1],
                op0=ALU.mult, op1=ALU.add), reads=[rk, "cw", "cb"], writes=["tC"])
            for j in range(3):
                P.op("dve", lambda e, t=t, j=j: e.scalar_tensor_tensor(
                    out=acc, in0=raw[:, t, j:j + 512], scalar=cw[:, 4 * t + j:4 * t + j + 1], in1=acc,
                    op0=ALU.mult, op1=ALU.add), reads=[rk, "cw", "tC"], writes=["tC"])
            P.op("dve", lambda e, t=t: e.tensor_copy(out=raw[:, t, 0:3], in_=raw[:, t, 512:515]),
                 reads=[rk, "tC"], writes=[rk])
            if t == 0:
                silu_from(acc, ["tC"], xsT[:], "xsT", tmp["D"][:], "tD")
                for u in range(2):
                    P.op("dve", lambda e, u=u: e.tensor_copy(
                        out=xsdup[:].rearrange("p (c u l) -> p c u l", c=8, u=2)[:, :, u, :],
                        in_=xsT[:].rearrange("p (c l) -> p c l", c=8)), reads=["xsT"], writes=["xsdup"])
            elif t == 1:
                silu_from(acc, ["tC"], tmp["E"][:], "tE", tmp["D"][:], "tD")
                for u in range(2):
                    P.op("dve", lambda e, u=u: e.tensor_copy(
                        out=BTdup[:].rearrange("p (c u l) -> p c u l", c=8, u=2)[:, :, u, :],
                        in_=tmp["E"][:].rearrange("p (c l) -> p c l", c=8)), reads=["tE"], writes=["BTdup"])
            else:
                silu_from(acc, ["tC"], CT[:], "CT", tmp["D"][:], "tD")
        pt, pk = nextbank()
        for kc in range(16):
            P.op("pe", lambda e, pt=pt, kc=kc: e.matmul(pt[0:2, :], lhsT=W_sb[:, kc, 1024:1026], rhs=xbi[:, kc, :],
                                                        start=(kc == 0), stop=(kc == 15)),
                 reads=["W2", xk], writes=[pk])
        P.op("act", lambda e, pt=pt: e.activation(out=dt_sb[:], in_=pt[0:2, :], func=AF.Exp, bias=dtp[:, 0:1], scale=1.0),
             reads=[pk, "dtp"], writes=["dt"])
        P.op("act", lambda e: e.activation(out=dt_sb[:], in_=dt_sb[:], func=AF.Ln, bias=1.0, scale=1.0),
             reads=["dt"], writes=["dt"])
        P.op("dve", lambda e: e.tensor_scalar(out=a_sb[:], in0=dt_sb[:], scalar1=Acol[:, 0:1], scalar2=None, op0=ALU.mult),
             reads=["dt", "Acol"], writes=["a"])
        for c in range(8):
            P.op("dve", lambda e, c=c: e.tensor_tensor_scan(
                out=acs[:, c * 64:(c + 1) * 64], data0=ones_f[0:2, 0:64], data1=a_sb[:, c * 64:(c + 1) * 64],
                initial=0.0, op0=ALU.mult, op1=ALU.add), reads=["a", "ones_f"], writes=["acs"])
        for h in range(2):
            P.op("dve", lambda e, h=h: e.tensor_scalar(
                out=dtBD[:].rearrange("p (c u l) -> p c u l", c=8, u=2)[:, :, h, :],
                in0=dt_sb[:].rearrange("p (c l) -> p c l", c=8), scalar1=eye2[:, h:h + 1], scalar2=None, op0=ALU.mult),
                reads=["dt", "eye2"], writes=["dtBD"])
            P.op("dve", lambda e, h=h: e.tensor_scalar(
                out=acsBD[:].rearrange("p (c u l) -> p c u l", c=8, u=2)[:, :, h, :],
                in0=acs[:].rearrange("p (c l) -> p c l", c=8), scalar1=eye2[:, h:h + 1], scalar2=None, op0=ALU.mult),
                reads=["acs", "eye2"], writes=["acsBD"])
        for c in range(8):
            P.op("dve", lambda e, c=c: e.tensor_scalar(
                out=lastbc[:, c, :], in0=ones_f[0:2, :], scalar1=acs[:, c * 64 + 63:c * 64 + 64], scalar2=None, op0=ALU.mult),
                reads=["acs", "ones_f"], writes=["lastbc"])

    def attention_steps(b):
        nkb = 4 * b + 4

        def scores(kb):
            j = kb - 4 * b
            c0 = max(0, j) * 128
            st = kb % 2
            for m in range(2):
                P.op("pe", lambda e, kb=kb, m=m, c0=c0: e.matmul(
                    psS[m][:, c0:512], lhsT=KT[m * 64:(m + 1) * 64, kb * 128:(kb + 1) * 128],
                    rhs=QT[m * 64:(m + 1) * 64, c0:512], start=True, stop=True),
                    reads=["KT%d" % (kb // 4), "QT"], writes=["ps%d" % m])
                P.op("act", lambda e, m=m, c0=c0, st=st: e.activation(
                    out=E[st][m][:, c0:512], in_=psS[m][:, c0:512], func=AF.Exp, scale=0.125),
                    reads=["ps%d" % m], writes=["E%d%d" % (st, m)])
                if j >= 0:
                    P.op("pool", lambda e, m=m, c0=c0, st=st: e.memset(E[st][m][64:128, c0:c0 + 64], 0.0),
                         reads=[], writes=["E%d%d" % (st, m)])

        def accum(kb):
            j = kb - 4 * b
            c0 = max(0, j) * 128
            st = kb % 2
            for m in range(2):
                P.op("pe", lambda e, kb=kb, m=m, c0=c0, st=st: e.matmul(
                    psO[m][:, c0:512], lhsT=V[:, kb, :], rhs=E[st][m][:, c0:512],
                    start=(kb == 0), stop=(kb == nkb - 1)),
                    reads=["V%d" % (kb // 4), "E%d%d" % (st, m)], writes=["ps%d" % (2 + m)])
                P.op("pe", lambda e, kb=kb, m=m, c0=c0, st=st: e.matmul(
                    psD[m][:, c0:512], lhsT=ones_b[:], rhs=E[st][m][:, c0:512],
                    start=(kb == 0), stop=(kb == nkb - 1)),
                    reads=["ones_b", "E%d%d" % (st, m)], writes=["ps%d" % (4 + m)])

        scores(0)
        for kb in range(nkb):
            if kb + 1 < nkb:
                scores(kb + 1)
            accum(kb)
            yield
        tA, tB, tC, tD = tmp["A"][:], tmp["B"][:], tmp["C"][:], tmp["D"][:]
        P.op("act", lambda e: e.activation(out=tA, in_=psD[0][:, :], func=AF.Ln), reads=["ps4"], writes=["tA"])
        P.op("act", lambda e: e.activation(out=tB, in_=psD[1][:, :], func=AF.Ln), reads=["ps5"], writes=["tB"])
        P.op("act", lambda e: e.activation(out=tA, in_=tA, func=AF.Exp, scale=-1.0), reads=["tA"], writes=["tA"])
        P.op("act", lambda e: e.activation(out=tB, in_=tB, func=AF.Exp, scale=-1.0), reads=["tB"], writes=["tB"])
        P.op("dve", lambda e: e.tensor_tensor(out=tA, in0=tA, in1=psO[0][:, :], op=ALU.mult), reads=["tA", "ps2"], writes=["tA"])
        P.op("dve", lambda e: e.tensor_tensor(out=tB, in0=tB, in1=psO[1][:, :], op=ALU.mult), reads=["tB", "ps3"], writes=["tB"])
        P.op("dve", lambda e: e.scalar_tensor_tensor(out=tA, in0=tB, scalar=lams[:, 2:3], in1=tA, op0=ALU.mult, op1=ALU.add),
             reads=["tA", "tB", "neglam"], writes=["tA"])
        P.op("act", lambda e: e.activation(out=tB, in_=tA, func=AF.Square), reads=["tA"], writes=["tB"])
        P.op("pe", lambda e: e.matmul(psS[0][:, :], lhsT=ones_f[:], rhs=tB, start=True, stop=True),
             reads=["ones_f", "tB"], writes=["ps0"])
        P.op("act", lambda e: e.activation(out=tB, in_=psS[0][:, :], func=AF.Ln, bias=EPS, scale=1.0 / 128),
             reads=["ps0"], writes=["tB"])
        P.op("act", lambda e: e.activation(out=tB, in_=tB, func=AF.Exp, scale=-0.5), reads=["tB"], writes=["tB"])
        P.op("dve", lambda e: e.tensor_tensor(out=tA, in0=tA, in1=tB, op=ALU.mult), reads=["tA", "tB"], writes=["tA"])
        P.op("dve", lambda e: e.scalar_tensor_tensor(out=yatt_sb[:], in0=tA, scalar=lams[:, 3:4], in1=sg[:],
                                                     op0=ALU.mult, op1=ALU.mult),
             reads=["tA", "coef", "sg"], writes=["yatt_sb"])
        P.dma("sp", lambda e: e.dma_start(out=yatt[:, b * 512:(b + 1) * 512], in_=yatt_sb[:]), "yatt",
              reads=["yatt_sb"], is_output=True)
        yield

    def ssd_steps(b):
        for c in range(8):
            P.op("pe", lambda e, c=c: e.matmul(rcol[:, 2 * c:2 * c + 1], lhsT=dtBD[:, c * 128:(c + 1) * 128],
                                               rhs=ones_f[0:2, 0:1], start=True, stop=True),
                 reads=["dtBD", "ones_f"], writes=["ps6"])
            P.op("pe", lambda e, c=c: e.matmul(rcol[:, 2 * c + 1:2 * c + 2], lhsT=acsBD[:, c * 128:(c + 1) * 128],
                                               rhs=ones_f[0:2, 0:1], start=True, stop=True),
                 reads=["acsBD", "ones_f"], writes=["ps6"])
        P.op("act", lambda e: e.activation(out=cols[:], in_=rcol, func=AF.Copy), reads=["ps6"], writes=["cols"])
        P.op("dve", lambda e: e.tensor_scalar(
            out=negc[:], in0=cols[:].rearrange("p (c t) -> p c t", t=2)[:, :, 1], scalar1=-1.0, scalar2=None, op0=ALU.mult),
            reads=["cols"], writes=["negc"])
        for c in range(8):
            cs = slice(c * 64, (c + 1) * 64)
            cd = slice(c * 128, (c + 1) * 128)
            P.op("pe", lambda e, cs=cs, cd=cd: e.matmul(rG, lhsT=BTdup[:, cd], rhs=CT[:, cs], start=True, stop=True),
                 reads=["BTdup", "CT"], writes=["ps6"])
            P.op("pe", lambda e, cs=cs: e.matmul(rR, lhsT=hsel[:], rhs=acs[:, cs], start=True, stop=True),
                 reads=["hsel", "acs"], writes=["ps6"])
            P.op("pe", lambda e, cd=cd: e.matmul(rX, lhsT=xsdup[:, cd], rhs=identb[:], start=True, stop=True),
                 reads=["xsdup", "identb"], writes=["ps6"])
            P.op("pe", lambda e, c=c: e.matmul(rC, lhsT=lastbc[:, c, :], rhs=hsel[:], start=True, stop=True),
                 reads=["lastbc", "hsel"], writes=["ps7"])
            P.op("pe", lambda e, cd=cd: e.matmul(rB, lhsT=BTdup[:, cd], rhs=identb[:], start=True, stop=True),
                 reads=["BTdup", "identb"], writes=["ps7"])
            P.op("dve", lambda e: e.tensor_tensor(out=Gm[:], in0=rG, in1=mask2[:], op=ALU.mult),
                 reads=["ps6", "mask2"], writes=["Gm"])
            P.op("dve", lambda e, c=c: e.tensor_scalar(out=Dm[:], in0=rR, scalar1=negc[:, c:c + 1], scalar2=0.0,
                                                       op0=ALU.add, op1=ALU.min),
                 reads=["ps6", "negc"], writes=["Dm"])
            P.op("act", lambda e: e.activation(out=E2[:], in_=Dm[:], func=AF.Exp), reads=["Dm"], writes=["E2"])
            P.op("act", lambda e: e.activation(out=R2e[:], in_=rR, func=AF.Exp), reads=["ps6"], writes=["R2e"])
            P.op("dve", lambda e, c=c: e.scalar_tensor_tensor(
                out=XdtBD[:], in0=rX, scalar=cols[:, 2 * c:2 * c + 1], in1=bdm[:], op0=ALU.mult, op1=ALU.mult),
                reads=["ps6", "cols", "bdm"], writes=["XdtBD"])
            P.op("dve", lambda e: e.tensor_tensor(out=M2[:], in0=Gm[:], in1=E2[:], op=ALU.mult),
                 reads=["Gm", "E2"], writes=["M2"])
            P.op("act", lambda e: e.activation(out=XdecBD[:], in_=XdtBD[:], func=AF.Identity, scale=E2[:, 63:64]),
                 reads=["XdtBD", "E2"], writes=["XdecBD"])
            P.op("act", lambda e: e.activation(out=Btok[:], in_=rB, func=AF.Copy), reads=["ps7"], writes=["Btok"])
            P.op("act", lambda e: e.activation(out=CD[:], in_=rC, func=AF.Exp), reads=["ps7"], writes=["CD"])
            P.op("act", lambda e: e.activation(out=Sbf[:], in_=Sst[:], func=AF.Copy), reads=["Sst"], writes=["Sbf"])
            P.op("pe", lambda e, cs=cs: e.matmul(rYo, lhsT=Sbf[:], rhs=CT[:, cs], start=True, stop=True),
                 reads=["Sbf", "CT"], writes=["ps7"])
            P.op("pe", lambda e: e.matmul(rYd, lhsT=XdtBD[:], rhs=M2[:], start=True, stop=True),
                 reads=["XdtBD", "M2"], writes=["ps7"])
            P.op("pe", lambda e: e.matmul(rS, lhsT=Btok[:], rhs=XdecBD[:], start=True, stop=True),
                 reads=["Btok", "XdecBD"], writes=["ps7"])
            P.op("dve", lambda e: e.tensor_tensor(out=y1[:], in0=rYo, in1=R2e[:], op=ALU.mult),
                 reads=["ps7", "R2e"], writes=["y1"])
            P.op("dve", lambda e, cs=cs: e.tensor_tensor(out=ysT[:, cs], in0=y1[:], in1=rYd, op=ALU.add),
                 reads=["y1", "ps7"], writes=["ysT"])
            P.op("dve", lambda e: e.tensor_tensor(out=Sst[:], in0=Sst[:], in1=CD[:], op=ALU.mult),
                 reads=["Sst", "CD"], writes=["Sst"])
            P.op("dve", lambda e: e.tensor_tensor(out=Sst[:], in0=Sst[:], in1=rS, op=ALU.add),
                 reads=["Sst", "ps7"], writes=["Sst"])
            yield
        tE, tF = tmp["E"][:], tmp["F"][:]
        P.op("dve", lambda e: e.scalar_tensor_tensor(out=tE, in0=xsT[:], scalar=colp[:, 0:1], in1=ysT[:],
                                                     op0=ALU.mult, op1=ALU.add),
             reads=["xsT", "colp", "ysT"], writes=["tE"])
        P.op("dve", lambda e: e.tensor_tensor(out=tE, in0=tE, in1=sz[:], op=ALU.mult), reads=["tE", "sz"], writes=["tE"])
        P.op("act", lambda e: e.activation(out=tF, in_=tE, func=AF.Square), reads=["tE"], writes=["tF"])
        P.op("pe", lambda e: e.matmul(ps[6][0:1, :], lhsT=ones_f[:, 0:1], rhs=tF, start=True, stop=True),
             reads=["ones_f", "tF"], writes=["ps6"])
        P.op("act", lambda e: e.activation(out=ssq_sb[:], in_=ps[6][0:1, :], func=AF.Copy), reads=["ps6"], writes=["ssq_sb"])
        P.op("act", lambda e: e.activation(out=yssm_sb[:], in_=tE, func=AF.Identity, scale=colp[:, 1:2]),
             reads=["tE", "colp"], writes=["yssm_sb"])
        P.dma("sp", lambda e: e.dma_start(out=ssq[0:1, b * 512:(b + 1) * 512], in_=ssq_sb[:]), "ssqo",
              reads=["ssq_sb"], is_output=True)
        P.dma("sp", lambda e: e.dma_start(out=yssm[:, b * 512:(b + 1) * 512], in_=yssm_sb[:]), "yssm",
              reads=["yssm_sb"], is_output=True)
        yield

    def interleave(b):
        att = attention_steps(b)
        sd = ssd_steps(b)
        nk = 4 * b + 5
        ns = 9
        ia = isd = 0
        for _ in att:
            pass
        for _ in sd:
            pass
        return
        while ia < nk or isd < ns:
            if isd >= ns or (ia < nk and ia * ns <= isd * nk):
                next(att, None)
                ia += 1
            else:
                next(sd, None)
                isd += 1

    load_W(0)
    load_x(0)
    load_W(1)
    load_W(2)
    for b in range(NBLK):
        if b + 1 < NBLK:
            load_x(b + 1)
        inproj(b)
        interleave(b)
    P.emit()
    return nc


_NC = {}
BF = ml_dtypes.bfloat16


def _get(name, builder):
    if name not in _NC:
        _NC[name] = builder()
    return _NC[name]


def _launch(nc, in_maps):
    res = run_bass_kernel_spmd(nc, in_maps, core_ids=list(range(len(in_maps))))
    return res.results


_IDENTF = np.eye(128, dtype=np.float32)


def run_odd(xtok, xT_bf, w_in, w_grp, b_grp, scale, w_out, lng, lnb, cores=range(NCORES)):
    nc = _get("odd", lambda: build_tail(True))
    bg = np.ascontiguousarray(b_grp.reshape(16, 128).T)
    osc = np.ascontiguousarray(scale.reshape(16, 128).T)
    in_maps = []
    for c in cores:
        t0 = c * TPC
        xT_c = np.zeros((D, HALO + TPC), BF)
        xT_c[:, HALO:] = xT_bf[:, t0:t0 + TPC]
        if c > 0:
            xT_c[:, :HALO] = xT_bf[:, t0 - HALO:t0]
        inv = np.zeros((4, 16), np.float32)
        for gi in range(4):
            for i in range(16):
                inv[gi, i] = 1.0 / min(t0 + i + 1, 2 ** (gi + 1))
        in_maps.append(dict(xres=np.ascontiguousarray(xtok[t0:t0 + TPC]), w_out=w_out, lng=lng, lnb=lnb,
                            identf=_IDENTF, xT=xT_c, w_in=w_in, w_grp=w_grp, bgrp=bg, oscale=osc,
                            invtab=inv.reshape(-1)))
    res = _launch(nc, in_maps)
    xo = np.concatenate([r["xo"] for r in res], 0)
    xoT = np.concatenate([r["xoT"] for r in res], 1)
    return xo, xoT


def _even_consts():
    mask2 = np.zeros((128, 64), np.float32)
    for u in range(2):
        for s_ in range(64):
            mask2[u * 64 + s_, s_:] = 1.0
    bdm = np.zeros((128, 128), np.float32)
    bdm[:64, :64] = 1.0
    bdm[64:, 64:] = 1.0
    hsel = np.zeros((2, 128), np.float32)
    hsel[0, :64] = 1.0
    hsel[1, 64:] = 1.0
    return dict(identf=_IDENTF, mask2=mask2, bdmask=bdm, hsel=hsel, eye2=np.eye(2, dtype=np.float32))


def run_even(xT, w_in, conv_w, conv_b, dt_bias, a_log, d_skip, ssm_norm_g, lq1, lk1, lq2, lk2, subln_g,
             lam_init, cores=range(NCORES), nblk=S // 512):
    x_bf16 = xT.dtype != np.float32
    nc = _get(("even", x_bf16, nblk), lambda: build_even(x_bf16, nblk))
    consts = _even_consts()
    ar = np.arange(128)
    in_maps = []
    for c in cores:
        g = c // 4
        idx = np.concatenate([c * 128 + ar, 1024 + c * 128 + ar, 2048 + c * 128 + ar, 3072 + c * 128 + ar,
                              4096 + c * 128 + ar, 5120 + c * 128 + ar, 6144 + g * 128 + ar, 6400 + g * 128 + ar,
                              6656 + 2 * c + np.arange(2)])
        wA = np.ascontiguousarray(w_in[:, idx])
        chans = [c * 128 + ar, 1024 + g * 128 + ar, 1280 + g * 128 + ar]
        cw = np.zeros((128, 12), np.float32)
        cb = np.zeros((128, 3), np.float32)
        for t in range(3):
            cw[:, 4 * t:4 * t + 4] = conv_w[:, chans[t]].T
            cb[:, t] = conv_b[chans[t]]
        colp = np.zeros((128, 8), np.float32)
        colp[:, 0] = np.repeat(d_skip[2 * c:2 * c + 2], 64)
        colp[:, 1] = ssm_norm_g[c * 128:(c + 1) * 128]
        colp[:, 2] = subln_g
        colp[:, 3] = lam_init
        colp[:, 4] = 1.0 - lam_init
        dtp = np.stack([dt_bias[2 * c:2 * c + 2], a_log[2 * c:2 * c + 2]], 1).astype(np.float32)
        lamv = np.concatenate([lq1, lk1, lq2, lk2]).astype(np.float32)
        m = dict(xT=xT, wA=wA, cw=cw, cb=cb, colp=colp, dtp=dtp, lamv=lamv)
        m.update(consts)
        in_maps.append(m)
    res = _launch(nc, in_maps)
    return [r["yatt"] for r in res], [r["yssm"] for r in res], [r["ssq"] for r in res]


def _sel8():
    sel = np.zeros((8, 256), np.float32)
    sel[0:4, 0:128] = 1.0
    sel[4:8, 128:256] = 1.0
    return sel


def run_evtail(xtok, yT, ssq8, w_out, lng, lnb, cores=range(NCORES)):
    nc = _get("evtail", lambda: build_tail(False))
    sel = _sel8()
    in_maps = []
    for c in cores:
        t0 = c * TPC
        in_maps.append(dict(xres=np.ascontiguousarray(xtok[t0:t0 + TPC]), w_out=w_out, lng=lng, lnb=lnb,
                            identf=_IDENTF, yT=np.ascontiguousarray(yT[:, t0:t0 + TPC]),
                            ssq8=np.ascontiguousarray(ssq8[:, t0:t0 + TPC]), sel=sel))
    res = _launch(nc, in_maps)
    xo = np.concatenate([r["xo"] for r in res], 0)
    xoT = np.concatenate([r["xoT"] for r in res], 1)
    return xo, xoT


def kernel(x, ev_w_in, ev_conv_w, ev_conv_b, ev_dt_bias, ev_a_log, ev_d_skip, ev_ssm_norm_g,
           ev_lambda_q1, ev_lambda_k1, ev_lambda_q2, ev_lambda_k2, ev_subln_g, ev_w_out,
           od_w_in, od_w_grp, od_b_grp, od_scale, od_w_out, ln_g, ln_b):
    f = lambda a: np.asarray(a, dtype=np.float32)
    xtok = np.ascontiguousarray(f(x)[0])
    xT = np.ascontiguousarray(xtok.T)
    for l in range(DEPTH):
        i = l // 2
        if l % 2 == 0:
            lam_init = 0.8 - 0.6 * math.exp(-0.3 * l)
            ya, ys, sq = run_even(xT, f(ev_w_in[i]), f(ev_conv_w[i]), f(ev_conv_b[i]), f(ev_dt_bias[i]), f(ev_a_log[i]),
                                  f(ev_d_skip[i]), f(ev_ssm_norm_g[i]), f(ev_lambda_q1[i]), f(ev_lambda_k1[i]),
                                  f(ev_lambda_q2[i]), f(ev_lambda_k2[i]), f(ev_subln_g[i]), lam_init)
            yT = np.concatenate(list(ya) + list(ys), 0)
            ssq8 = np.concatenate(list(sq), 0)
            xtok, xT = run_evtail(xtok, yT, ssq8, f(ev_w_out[i]), f(ln_g[l]), f(ln_b[l]))
        else:
            xtok, xT = run_odd(xtok, xT, f(od_w_in[i]), f(od_w_grp[i]), f(od_b_grp[i]), f(od_scale[i]), f(od_w_out[i]),
                               f(ln_g[l]), f(ln_b[l]))
    return np.ascontiguousarray(xtok[None]).astype(np.float32)
```
